# Optimizing a Trainium2 kernel written in Bass

```python
import jax
import jax.numpy as jnp
from jax import lax
import numpy as np

D_MODEL = 2048
BATCH = 4
SEQ = 2048
DEPTH = 2
DEC_BATCH = 16
DEC_SEQ = 16
PAST_LEN = 4096

CHUNK = 64
EPS = 1e-6
SSD_INNER = D_MODEL
SSD_HEADDIM = 64
SSD_HEADS = SSD_INNER // SSD_HEADDIM
SSD_GROUPS = 4
SSD_STATE = 128
SSD_CONV = 4
SSD_CONV_DIM = SSD_INNER + 2 * SSD_GROUPS * SSD_STATE
SSD_CHUNK = CHUNK
GLA_HEADS = 4
GLA_KEY = D_MODEL // 2
GLA_VAL = D_MODEL
GLA_HEAD_K = GLA_KEY // GLA_HEADS
GLA_HEAD_V = GLA_VAL // GLA_HEADS
GLA_RANK = 16
GLA_GATE_NORM = 16.0
GLA_CHUNK = 16
N_BRANCH = 2
IN_SPLITS = (SSD_INNER, SSD_CONV_DIM, SSD_HEADS, GLA_KEY, GLA_KEY, GLA_VAL, GLA_VAL, GLA_RANK, N_BRANCH * D_MODEL)
IN_DIM = SSD_INNER + SSD_CONV_DIM + SSD_HEADS + 2 * GLA_KEY + 2 * GLA_VAL + GLA_RANK + N_BRANCH * D_MODEL
MEM_TOKENS = 256
MEM_HEADS = 4
MEM_HEAD_DIM = D_MODEL // MEM_HEADS
D_FF = 5632
FFN_CONV = 3

kernel_name = "hybrid_ssd_gla_streaming_encoder_step"


def _split(x, sizes):
    offsets = [int(o) for o in np.cumsum(sizes)[:-1]]
    return jnp.split(x, offsets, axis=-1)


def group_rmsnorm(x, g, groups):
    shp = x.shape
    xf = x.astype(jnp.float32).reshape(shp[:-1] + (groups, shp[-1] // groups))
    xf = xf * lax.rsqrt(jnp.mean(xf * xf, axis=-1, keepdims=True) + EPS)
    return xf.reshape(shp).astype(x.dtype) * g


def causal_dwconv(x, buf, w, b):
    width = w.shape[0]
    L = x.shape[1]
    xp = jnp.concatenate([buf.astype(x.dtype), x], axis=1)
    y = xp[:, 0:L] * w[0]
    for i in range(1, width):
        y = y + xp[:, i:i + L] * w[i]
    return y + b, xp[:, L:]


def _pad_time(t, pad):
    return jnp.pad(t, [(0, 0), (0, pad)] + [(0, 0)] * (t.ndim - 2))


def ssd_scan(x, dt, a, b_in, c_in, h0):
    out_dtype = x.dtype
    bt, L, H, P = x.shape
    G, N = b_in.shape[2], b_in.shape[3]
    R = H // G
    Q = min(SSD_CHUNK, L)
    pad = (-L) % Q
    f32 = jnp.float32
    x, dt, b_in, c_in = x.astype(f32), dt.astype(f32), b_in.astype(f32), c_in.astype(f32)
    if pad:
        x, dt, b_in, c_in = _pad_time(x, pad), _pad_time(dt, pad), _pad_time(b_in, pad), _pad_time(c_in, pad)
    nc = (L + pad) // Q
    xr = x.reshape(bt, nc, Q, G, R, P)
    dtr = dt.reshape(bt, nc, Q, G, R)
    br = b_in.reshape(bt, nc, Q, G, N)
    cr = c_in.reshape(bt, nc, Q, G, N)
    acum = jnp.cumsum(dtr * a.astype(f32).reshape(G, R), axis=2)
    causal = jnp.tril(jnp.ones((Q, Q), dtype=bool))
    seg = acum[:, :, :, None] - acum[:, :, None, :]
    decay = jnp.exp(jnp.where(causal[:, :, None, None], seg, -jnp.inf))
    cb = jnp.einsum('bcign,bcjgn->bcijg', cr, br)
    w_ij = cb[..., None] * decay * dtr[:, :, None]
    y_diag = jnp.einsum('bcijgr,bcjgrp->bcigrp', w_ij, xr)
    to_end = jnp.exp(acum[:, :, -1:] - acum) * dtr
    chunk_dec = jnp.exp(acum[:, :, -1])
    exp_acum = jnp.exp(acum)

    def step(h, inp):
        b_c, te_c, x_c, c_c, ea_c, dec_c = inp
        y_off = jnp.einsum('bign,bgrpn->bigrp', c_c, h) * ea_c[..., None]
        h_new = h * dec_c[..., None, None] + jnp.einsum('bjgn,bjgr,bjgrp->bgrpn', b_c, te_c, x_c)
        return h_new, y_off

    xs = tuple(jnp.moveaxis(t, 1, 0) for t in (br, to_end, xr, cr, exp_acum, chunk_dec))
    h_last, y_off = lax.scan(step, h0.astype(f32).reshape(bt, G, R, P, N), xs)
    y = y_diag + jnp.moveaxis(y_off, 0, 1)
    y = y.reshape(bt, nc * Q, H, P)[:, :L]
    return y.astype(out_dtype), h_last.reshape(bt, H, P, N).astype(h0.dtype)


def gla_scan(q, k, v, log_a, s0):
    out_dtype = v.dtype
    bt, L, H, K = q.shape
    V = v.shape[-1]
    Q = min(GLA_CHUNK, L)
    pad = (-L) % Q
    f32 = jnp.float32
    q, k, v, log_a = q.astype(f32), k.astype(f32), v.astype(f32), log_a.astype(f32)
    if pad:
        q, k, v, log_a = _pad_time(q, pad), _pad_time(k, pad), _pad_time(v, pad), _pad_time(log_a, pad)
    nc = (L + pad) // Q
    qr = q.reshape(bt, nc, Q, H, K)
    kr = k.reshape(bt, nc, Q, H, K)
    vr = v.reshape(bt, nc, Q, H, V)
    bcum = jnp.cumsum(log_a.reshape(bt, nc, Q, H, K), axis=2)
    q_dec = qr * jnp.exp(bcum)
    k_inv = kr * jnp.exp(-bcum)
    k_end = kr * jnp.exp(bcum[:, :, -1:] - bcum)
    chunk_dec = jnp.exp(bcum[:, :, -1])
    causal = jnp.tril(jnp.ones((Q, Q), dtype=bool))
    att = jnp.where(causal, jnp.einsum('bcihk,bcjhk->bchij', q_dec, k_inv), 0.0)
    o_intra = jnp.einsum('bchij,bcjhv->bcihv', att, vr)

    def step(s, inp):
        q_c, k_c, v_c, dec_c = inp
        o = jnp.einsum('bihk,bhkv->bihv', q_c, s)
        s_new = s * dec_c[..., None] + jnp.einsum('bjhk,bjhv->bhkv', k_c, v_c)
        return s_new, o

    xs = tuple(jnp.moveaxis(t, 1, 0) for t in (q_dec, k_end, vr, chunk_dec))
    s_last, o_inter = lax.scan(step, s0.astype(f32), xs)
    o = o_intra + jnp.moveaxis(o_inter, 0, 1)
    o = o.reshape(bt, nc * Q, H, V)[:, :L]
    return o.astype(out_dtype), s_last.astype(s0.dtype)


def token_mixer(h, w_in, ssd_conv_w, ssd_conv_b, ssd_dt_bias, ssd_a_log, ssd_d, ssd_norm,
                gla_wa2, gla_ba, gla_norm, w_branch, w_out, ssd_h0, ssd_conv0, gla_s0):
    bt, L, _ = h.shape
    z, xbc, dt, q, k, v, g, a_lr, gates = _split(h @ w_in, IN_SPLITS)
    xbc, ssd_conv_new = causal_dwconv(xbc, ssd_conv0, ssd_conv_w, ssd_conv_b)
    xbc = jax.nn.silu(xbc)
    xs, bs, cs = _split(xbc, (SSD_INNER, SSD_GROUPS * SSD_STATE, SSD_GROUPS * SSD_STATE))
    xs = xs.reshape(bt, L, SSD_HEADS, SSD_HEADDIM)
    bs = bs.reshape(bt, L, SSD_GROUPS, SSD_STATE)
    cs = cs.reshape(bt, L, SSD_GROUPS, SSD_STATE)
    dt = jax.nn.softplus(dt + ssd_dt_bias)
    y_ssd, ssd_h = ssd_scan(xs, dt, -jnp.exp(ssd_a_log), bs, cs, ssd_h0)
    y_ssd = (y_ssd + ssd_d[:, None] * xs).reshape(bt, L, SSD_INNER)
    y_ssd = group_rmsnorm(y_ssd * jax.nn.silu(z), ssd_norm, SSD_GROUPS)
    q = q.reshape(bt, L, GLA_HEADS, GLA_HEAD_K) * (GLA_HEAD_K ** -0.5)
    k = k.reshape(bt, L, GLA_HEADS, GLA_HEAD_K)
    v = v.reshape(bt, L, GLA_HEADS, GLA_HEAD_V)
    log_a = jax.nn.log_sigmoid((a_lr @ gla_wa2 + gla_ba).astype(jnp.float32)) / GLA_GATE_NORM
    log_a = log_a.reshape(bt, L, GLA_HEADS, GLA_HEAD_K)
    y_gla, gla_s = gla_scan(q, k, v, log_a, gla_s0)
    y_gla = group_rmsnorm(y_gla, gla_norm, 1).reshape(bt, L, GLA_VAL) * jax.nn.silu(g)
    branches = jnp.stack([y_ssd, y_gla], axis=2)
    branch_out = jnp.einsum('blnc,ncd->blnd', branches, w_branch)
    gate = jax.nn.sigmoid(gates.reshape(bt, L, N_BRANCH, D_MODEL))
    merged = jnp.sum(gate * branch_out, axis=2)
    return merged @ w_out, ssd_h, ssd_conv_new, gla_s


def memory_cross_attn(h, mem_k, mem_v, w_mq, w_mo):
    bt, L, _ = h.shape
    q = (h @ w_mq).reshape(bt, L, MEM_HEADS, MEM_HEAD_DIM)
    s = jnp.einsum('blhd,bmhd->bhlm', q, mem_k).astype(jnp.float32) * (MEM_HEAD_DIM ** -0.5)
    p = jax.nn.softmax(s, axis=-1).astype(mem_v.dtype)
    o = jnp.einsum('bhlm,bmhd->blhd', p, mem_v).reshape(bt, L, D_MODEL)
    return o @ w_mo


def conv_ffn(h, w_ffn_in, ffn_conv_w, ffn_conv_b, w_ffn_out, conv0):
    up, conv_new = causal_dwconv(h @ w_ffn_in, conv0, ffn_conv_w, ffn_conv_b)
    u, gt = _split(up, (D_FF, D_FF))
    return (jax.nn.silu(gt) * u) @ w_ffn_out, conv_new


def run_trunk(x, mem_k, mem_v, st_ssd, st_ssd_conv, st_gla, st_ffn_conv, layer_w, norm_final):
    (norm_mix, w_in, ssd_conv_w, ssd_conv_b, ssd_dt_bias, ssd_a_log, ssd_d, ssd_norm,
     gla_wa2, gla_ba, gla_norm, w_branch, w_out, norm_mem, w_mq, w_mo,
     norm_ffn, w_ffn_in, ffn_conv_w, ffn_conv_b, w_ffn_out) = layer_w
    new_ssd, new_ssd_conv, new_gla, new_ffn = [], [], [], []
    for i in range(DEPTH):
        mix, s_h, s_c, g_s = token_mixer(
            group_rmsnorm(x, norm_mix[i], 1), w_in[i], ssd_conv_w[i], ssd_conv_b[i], ssd_dt_bias[i],
            ssd_a_log[i], ssd_d[i], ssd_norm[i], gla_wa2[i], gla_ba[i], gla_norm[i], w_branch[i], w_out[i],
            st_ssd[i], st_ssd_conv[i], st_gla[i])
        x = x + mix
        x = x + memory_cross_attn(group_rmsnorm(x, norm_mem[i], 1), mem_k[i], mem_v[i], w_mq[i], w_mo[i])
        f, f_c = conv_ffn(group_rmsnorm(x, norm_ffn[i], 1), w_ffn_in[i], ffn_conv_w[i], ffn_conv_b[i],
                          w_ffn_out[i], st_ffn_conv[i])
        x = x + f
        new_ssd.append(s_h)
        new_ssd_conv.append(s_c)
        new_gla.append(g_s)
        new_ffn.append(f_c)
    y = group_rmsnorm(x, norm_final, 1)
    return y, jnp.stack(new_ssd), jnp.stack(new_ssd_conv), jnp.stack(new_gla), jnp.stack(new_ffn)


def setup_inputs(seed: int = 0) -> dict:
    key = jax.random.key(seed)
    ks = jax.random.split(key, 40)
    f32 = jnp.float32

    def nrm(k, shape, scale=1.0):
        return jax.random.normal(k, shape, f32) * scale

    def gain(k, shape):
        return 1.0 + 0.02 * jax.random.normal(k, shape, f32)

    dt0 = jnp.exp(jax.random.uniform(ks[12], (DEPTH, SSD_HEADS), f32, np.log(1e-3), np.log(1e-1)))
    return {
        "x_prompt": nrm(ks[0], (BATCH, SEQ, D_MODEL)),
        "x_sample": nrm(ks[1], (DEC_BATCH, DEC_SEQ, D_MODEL)),
        "mem_prompt": nrm(ks[2], (BATCH, MEM_TOKENS, D_MODEL)),
        "state_ssd": nrm(ks[3], (DEPTH, DEC_BATCH, SSD_HEADS, SSD_HEADDIM, SSD_STATE), 0.5),
        "state_ssd_conv": nrm(ks[4], (DEPTH, DEC_BATCH, SSD_CONV - 1, SSD_CONV_DIM)),
        "state_gla": nrm(ks[5], (DEPTH, DEC_BATCH, GLA_HEADS, GLA_HEAD_K, GLA_HEAD_V), 0.5),
        "state_ffn_conv": nrm(ks[6], (DEPTH, DEC_BATCH, FFN_CONV - 1, 2 * D_FF)),
        "cache_mem_k": nrm(ks[7], (DEPTH, DEC_BATCH, MEM_TOKENS, MEM_HEADS, MEM_HEAD_DIM)),
        "cache_mem_v": nrm(ks[8], (DEPTH, DEC_BATCH, MEM_TOKENS, MEM_HEADS, MEM_HEAD_DIM)),
        "norm_mix": gain(ks[9], (DEPTH, D_MODEL)),
        "w_in": nrm(ks[10], (DEPTH, D_MODEL, IN_DIM), D_MODEL ** -0.5),
        "ssd_conv_w": nrm(ks[11], (DEPTH, SSD_CONV, SSD_CONV_DIM), SSD_CONV ** -0.5),
        "ssd_conv_b": nrm(ks[13], (DEPTH, SSD_CONV_DIM), 0.02),
        "ssd_dt_bias": dt0 + jnp.log(-jnp.expm1(-dt0)),
        "ssd_a_log": jnp.log(jax.random.uniform(ks[14], (DEPTH, SSD_HEADS), f32, 1.0, 16.0)),
        "ssd_d": 1.0 + nrm(ks[15], (DEPTH, SSD_HEADS), 0.1),
        "ssd_norm": gain(ks[16], (DEPTH, SSD_INNER)),
        "gla_wa2": nrm(ks[17], (DEPTH, GLA_RANK, GLA_KEY), GLA_RANK ** -0.5),
        "gla_ba": nrm(ks[18], (DEPTH, GLA_KEY), 0.01),
        "gla_norm": gain(ks[19], (DEPTH, GLA_HEAD_V)),
        "w_branch": nrm(ks[20], (DEPTH, N_BRANCH, D_MODEL, D_MODEL), D_MODEL ** -0.5),
        "w_out": nrm(ks[21], (DEPTH, D_MODEL, D_MODEL), D_MODEL ** -0.5),
        "norm_mem": gain(ks[22], (DEPTH, D_MODEL)),
        "w_mq": nrm(ks[23], (DEPTH, D_MODEL, D_MODEL), D_MODEL ** -0.5),
        "w_mk": nrm(ks[24], (DEPTH, D_MODEL, D_MODEL), D_MODEL ** -0.5),
        "w_mv": nrm(ks[25], (DEPTH, D_MODEL, D_MODEL), D_MODEL ** -0.5),
        "w_mo": nrm(ks[26], (DEPTH, D_MODEL, D_MODEL), D_MODEL ** -0.5),
        "norm_ffn": gain(ks[27], (DEPTH, D_MODEL)),
        "w_ffn_in": nrm(ks[28], (DEPTH, D_MODEL, 2 * D_FF), D_MODEL ** -0.5),
        "ffn_conv_w": nrm(ks[29], (DEPTH, FFN_CONV, 2 * D_FF), FFN_CONV ** -0.5),
        "ffn_conv_b": nrm(ks[30], (DEPTH, 2 * D_FF), 0.02),
        "w_ffn_out": nrm(ks[31], (DEPTH, D_FF, D_MODEL), D_FF ** -0.5),
        "norm_final": gain(ks[32], (D_MODEL,)),
    }


def reference(x_prompt, x_sample, mem_prompt, state_ssd, state_ssd_conv, state_gla, state_ffn_conv,
              cache_mem_k, cache_mem_v, norm_mix, w_in, ssd_conv_w, ssd_conv_b, ssd_dt_bias, ssd_a_log,
              ssd_d, ssd_norm, gla_wa2, gla_ba, gla_norm, w_branch, w_out, norm_mem, w_mq, w_mk, w_mv,
              w_mo, norm_ffn, w_ffn_in, ffn_conv_w, ffn_conv_b, w_ffn_out, norm_final):
    layer_w = (norm_mix, w_in, ssd_conv_w, ssd_conv_b, ssd_dt_bias, ssd_a_log, ssd_d, ssd_norm,
               gla_wa2, gla_ba, gla_norm, w_branch, w_out, norm_mem, w_mq, w_mo,
               norm_ffn, w_ffn_in, ffn_conv_w, ffn_conv_b, w_ffn_out)
    dt = x_prompt.dtype
    p_mem_k = jnp.einsum('bmd,ldc->lbmc', mem_prompt, w_mk).reshape(DEPTH, BATCH, MEM_TOKENS, MEM_HEADS, MEM_HEAD_DIM)
    p_mem_v = jnp.einsum('bmd,ldc->lbmc', mem_prompt, w_mv).reshape(DEPTH, BATCH, MEM_TOKENS, MEM_HEADS, MEM_HEAD_DIM)
    y_prompt, p_ssd, p_ssd_conv, p_gla, p_ffn_conv = run_trunk(
        x_prompt, p_mem_k, p_mem_v,
        jnp.zeros((DEPTH, BATCH, SSD_HEADS, SSD_HEADDIM, SSD_STATE), dt),
        jnp.zeros((DEPTH, BATCH, SSD_CONV - 1, SSD_CONV_DIM), dt),
        jnp.zeros((DEPTH, BATCH, GLA_HEADS, GLA_HEAD_K, GLA_HEAD_V), dt),
        jnp.zeros((DEPTH, BATCH, FFN_CONV - 1, 2 * D_FF), dt),
        layer_w, norm_final)
    y_sample, s_ssd, s_ssd_conv, s_gla, s_ffn_conv = run_trunk(
        x_sample, cache_mem_k, cache_mem_v, state_ssd, state_ssd_conv, state_gla, state_ffn_conv,
        layer_w, norm_final)
    return (y_prompt, y_sample, p_ssd, p_ssd_conv, p_gla, p_ffn_conv, p_mem_k, p_mem_v,
            s_ssd, s_ssd_conv, s_gla, s_ffn_conv)
```

```python
import numpy as np
from contextlib import ExitStack
import concourse.bass as bass
import concourse.mybir as mybir
from concourse.bass_utils import run_bass_kernel_spmd

F32 = mybir.dt.float32
BF16 = mybir.dt.bfloat16
AF = mybir.ActivationFunctionType
ALU = mybir.AluOpType
AX = mybir.AxisListType

D = 2048
KC = 16
SEQ = 2048
NSAMP = 2
SLEN = 16
TOK = SEQ + NSAMP * SLEN
DEPTH = 2
EPS = 1e-6
DFF = 5632
IN_DIM = 15408
O_Z, O_XBC, O_DT, O_Q, O_K, O_V, O_G, O_ALR, O_GATES = 0, 2048, 5120, 5152, 6176, 7200, 9248, 11296, 11312
P_NMIX, P_NMEM, P_NFFN, P_NFIN, P_SCW, P_SCB, P_FCW, P_FCB, P_BA, NVP = 0, 16, 32, 48, 64, 160, 184, 448, 536, 544
R_DTB, R_ALOG, R_DSK, R_SNORM, R_GNORM, NVR = 0, 32, 64, 96, 2144, 2656
NSLAB = 72
NW = 512


def xbc_slab_chunks():
    xs = [[4 * g + m for m in range(4)] for g in range(4)]
    bc = [[16 + 2 * j, 20 + 2 * j, 16 + 2 * j + 1, 20 + 2 * j + 1] for j in range(2)]
    return xs, bc


class E:
    def __init__(self, h, sem, name):
        self.h, self.sem, self.cnt, self.waited, self.name = h, sem, 0, {}, name


class Builder:
    def __init__(self, n_tiles=5, n_layers=2):
        self.n_tiles, self.n_layers = n_tiles, n_layers
        self.nc = nc = bass.Bass("TRN2", target_bir_lowering=False)
        self.es = ExitStack()
        self.st = {}
        self.dsem = {}
        self.engs = {}
        for nm, h in (("pe", nc.tensor), ("act", nc.scalar), ("dve", nc.vector), ("pool", nc.gpsimd), ("sp", nc.sync)):
            sem = self.es.enter_context(nc.semaphore("e_" + nm))
            self.engs[nm] = E(h, sem, nm)
        self.psi = 0

    def _pre(self, en, R, W):
        eng = self.engs[en]
        deps = {}

        def add(ev):
            if ev is None:
                return
            sem, val = ev
            k = id(sem)
            if k not in deps or deps[k][1] < val:
                deps[k] = (sem, val)
        for k in R:
            s = self.st.get(k)
            if s:
                add(s[0])
        for k in W:
            s = self.st.get(k)
            if s:
                add(s[0])
                for ev in s[1].values():
                    add(ev)
        for sem, val in deps.values():
            if en == "pe" and sem is eng.sem:
                continue
            if eng.waited.get(id(sem), 0) < val:
                eng.h.wait_ge(sem, val)
                eng.waited[id(sem)] = val

    def _post(self, ev, R, W):
        for k in R:
            s = self.st.setdefault(k, [None, {}])
            s[1][id(ev[0])] = ev
        for k in W:
            self.st[k] = [ev, {}]

    def op(self, en, inst, R, W):
        eng = self.engs[en]
        eng.cnt += 1
        inst.then_inc(eng.sem, 1)
        self._post((eng.sem, eng.cnt), R, W)

    def dma(self, en, pairs, R, W, sk):
        eng = self.engs[en]
        self._pre(en, R, W)
        if sk not in self.dsem:
            self.dsem[sk] = [self.es.enter_context(self.nc.semaphore("d_" + sk)), 0]
        d = self.dsem[sk]
        if d[1] > 0 and eng.waited.get(id(d[0]), 0) < d[1]:
            eng.h.wait_ge(d[0], d[1])
            eng.waited[id(d[0])] = d[1]
        for o, i in pairs:
            eng.h.dma_start(out=o, in_=i).then_inc(d[0], 16)
            d[1] += 16
        self._post((d[0], d[1]), R, W)

    def barrier(self, keep=("wsl0", "wsl1")):
        names = ("pe", "act", "dve", "sp")
        for en in names + ("pool",):
            eng = self.engs[en]
            for on in names:
                o = self.engs[on]
                if o is eng or o.cnt == 0:
                    continue
                if eng.waited.get(id(o.sem), 0) < o.cnt:
                    eng.h.wait_ge(o.sem, o.cnt)
                    eng.waited[id(o.sem)] = o.cnt
            for sk, d in self.dsem.items():
                if sk in keep or d[1] == 0:
                    continue
                if eng.waited.get(id(d[0]), 0) < d[1]:
                    eng.h.wait_ge(d[0], d[1])
                    eng.waited[id(d[0])] = d[1]
        self.st = {k: v for k, v in self.st.items() if k in keep}

    def ACT(self, out, in_, func, R, W, bias=None, scale=None, accum=None):
        self._pre("act", R, W)
        kw = {}
        if bias is not None:
            kw["bias"] = bias
        if scale is not None:
            kw["scale"] = scale
        if accum is not None:
            kw["accum_out"] = accum
        i = self.nc.scalar.activation(out=out, in_=in_, func=func, **kw)
        self.op("act", i, R, W)

    def TT(self, out, in0, in1, op, R, W, en="dve"):
        self._pre(en, R, W)
        i = self.engs[en].h.tensor_tensor(out=out, in0=in0, in1=in1, op=op)
        self.op(en, i, R, W)

    def TS(self, out, in0, s1, s2, op0, op1, R, W, en="dve"):
        self._pre(en, R, W)
        if s2 is None:
            i = self.engs[en].h.tensor_scalar(out=out, in0=in0, scalar1=s1, scalar2=None, op0=op0)
        else:
            i = self.engs[en].h.tensor_scalar(out=out, in0=in0, scalar1=s1, scalar2=s2, op0=op0, op1=op1)
        self.op(en, i, R, W)

    def STT(self, out, in0, scalar, in1, op0, op1, R, W, en="dve"):
        self._pre(en, R, W)
        i = self.engs[en].h.scalar_tensor_tensor(out=out, in0=in0, scalar=scalar, in1=in1, op0=op0, op1=op1)
        self.op(en, i, R, W)

    def CP(self, out, in_, R, W, en="dve"):
        self._pre(en, R, W)
        if en == "act":
            i = self.nc.scalar.activation(out=out, in_=in_, func=AF.Copy)
        else:
            i = self.engs[en].h.tensor_copy(out=out, in_=in_)
        self.op(en, i, R, W)

    def RSQ(self, ap, key):
        self.ACT(ap, ap, AF.Sqrt, [key], [key])
        self._pre("dve", [key], [key])
        i = self.nc.vector.reciprocal(out=ap, in_=ap)
        self.op("dve", i, [key], [key])

    def MS(self, ap, val, W, en="dve"):
        self._pre(en, [], W)
        i = self.engs[en].h.memset(ap, val)
        self.op(en, i, [], W)

    def MM(self, out, lhsT, rhs, start, stop, R, W):
        self._pre("pe", R, W)
        i = self.nc.tensor.matmul(out, lhsT=lhsT, rhs=rhs, start=start, stop=stop)
        self.op("pe", i, R, W)

    def ps(self):
        i = self.psi
        self.psi = (self.psi + 1) % 8
        return self.psum[i], "ps%d" % i

    def slab(self, expect):
        q = self.wq
        i = self.wi
        assert q[i][0] == expect, (q[i][0], expect)
        for j in (i, i + 1):
            if j < len(q) and j >= self.wissued:
                name, src, n = q[j]
                b = j % 2
                self.dma("pool", [(self.wsl[b][:, 0:n], src)], [], ["wsl%d" % b], "wsl%d" % b)
                self.wissued = j + 1
        self.wi += 1
        return self.wsl[i % 2], "wsl%d" % (i % 2)

    def build(self):
        nc = self.nc
        es = self.es
        dt_in = lambda name, shape: nc.dram_tensor(name, shape, F32, kind="ExternalInput").ap()
        dt_out = lambda name, shape: nc.dram_tensor(name, shape, F32, kind="ExternalOutput").ap()
        self.xT = dt_in("xT", [D, TOK])
        self.memT = dt_in("memT", [D, 256])
        self.cst_d = dt_in("cst", [128, 512])
        self.W = [dt_in("W%d" % l, [NSLAB, 128, 8192]) for l in range(DEPTH)]
        self.WF = [dt_in("WF%d" % l, [2, 8, 128, 24 * 256]) for l in range(DEPTH)]
        self.WKV = [dt_in("WKV%d" % l, [8, 128, 8192]) for l in range(DEPTH)]
        self.wsm_d = [dt_in("wsm%d" % l, [128, 16 * 48]) for l in range(DEPTH)]
        self.wa2_d = [dt_in("wa2_%d" % l, [16, 1024]) for l in range(DEPTH)]
        self.vecP_d = [dt_in("vecP%d" % l, [128, NVP]) for l in range(DEPTH)]
        self.vecR_d = [dt_in("vecR%d" % l, [128, NVR]) for l in range(DEPTH)]
        self.s_ssd_d = dt_in("s_ssd", [DEPTH, NSAMP, 128, 2048])
        self.s_sconv_d = dt_in("s_sconv", [DEPTH, NSAMP, 128, 72])
        self.s_gla_d = dt_in("s_gla", [DEPTH, NSAMP, 1024, 512])
        self.s_fconv_d = dt_in("s_fconv", [DEPTH, NSAMP, 128, 176])
        self.s_kT_d = dt_in("s_kT", [DEPTH, NSAMP, D, 256])
        self.s_v_d = dt_in("s_v", [DEPTH, NSAMP, 256, D])
        self.yT = dt_out("yT", [D, TOK])
        self.o_ssd = dt_out("o_ssd", [DEPTH, 1 + NSAMP, 128, 2048])
        self.o_sconv = dt_out("o_sconv", [DEPTH, 1 + NSAMP, 128, 72])
        self.o_gla = dt_out("o_gla", [DEPTH, 1 + NSAMP, 1024, 512])
        self.o_fconv = dt_out("o_fconv", [DEPTH, 1 + NSAMP, 128, 176])
        self.o_kT = dt_out("o_kT", [DEPTH, D, 256])
        self.o_v = dt_out("o_v", [DEPTH, 256, D])

        sb = lambda name, shape, dt: es.enter_context(nc.sbuf_tensor(name, shape, dt))
        self.x = sb("x", [128, 16, 512], F32)
        self.h = sb("h", [128, 16, 512], BF16)
        self.wsl = [sb("wsl0", [128, 8192], BF16), sb("wsl1", [128, 8192], BF16)]
        self.cst = sb("cstf", [128, 512], F32)
        self.identb = sb("identb", [128, 128], BF16)
        self.vecP = sb("vecP", [128, NVP], F32)
        self.vecR = sb("vecR", [128, NVR], F32)
        self.nba = sb("nba", [128, 8], F32)
        self.arow = sb("arow", [128, 32], F32)
        self.wsm = sb("wsm", [128, 16 * 48], BF16)
        self.wa2 = sb("wa2", [16, 1024], BF16)
        self.alrT = sb("alrT", [16, 512], BF16)
        self.ssdv = sb("ssdv", [128, 4, 6, 32], F32)
        self.cv_s = sb("cv_s", [128, DEPTH, 72], F32)
        self.cv_f = sb("cv_f", [128, DEPTH, 176], F32)
        self.hs = sb("hs", [128, NSAMP, 176], F32)
        self.ho = sb("ho", [128, NSAMP, 176], F32)
        self.rs = sb("rs", [128, 512], F32)
        self.small = sb("small", [128, 64], F32)
        self.A = sb("arenaA", [128, 24576], BF16)
        self.Bn = sb("arenaB", [128, 12288], BF16)
        self.psum = [es.enter_context(nc.psum_tensor("ps%d" % i, [128, 512], F32)) for i in range(8)]

        self.wq = []
        for l in range(self.n_layers):
            for j in range(8):
                self.wq.append(("kv%d_%d" % (l, j), self.WKV[l][j], 8192))
        for t in range(self.n_tiles):
            for l in range(self.n_layers):
                for j in range(50 + 12):
                    self.wq.append(("L%d_%d" % (l, j), self.W[l][j], 8192))
                for j in range(8):
                    self.wq.append(("F%d_0_%d" % (l, j), self.WF[l][0, j], 24 * 256))
                for j in range(62, NSLAB):
                    self.wq.append(("L%d_%d" % (l, j), self.W[l][j], 8192))
                for j in range(8):
                    self.wq.append(("F%d_1_%d" % (l, j), self.WF[l][1, j][:, 0:20 * 256], 20 * 256))
        self.wi = 0
        self.wissued = 0

        self.dma("sp", [(self.cst[:], self.cst_d)], [], ["cst"], "cst")
        self.dma("pool", [(self.identb[:], self.cst_d[:, 0:128])], [], ["identb"], "identb")
        self.ident_f = self.cst[:, 0:128]
        self.U_f = self.cst[:, 128:256]
        self.SL_f = self.cst[:, 256:384]
        self.ones_f = self.cst[:, 384:512]
        self.MS(self.cv_s[:], 0.0, ["cv_s"])
        self.MS(self.cv_f[:], 0.0, ["cv_f"])

        self.prologue_kv()
        tiles = []
        for t in range(4):
            tiles.append(dict(kind="p", idx=t, TT=512, col0=512 * t, segs=[(128 * i, 128) for i in range(4)],
                              nrun=1, RL=512, first=(t == 0), last=(t == 3)))
        tiles.append(dict(kind="s", idx=0, TT=32, col0=SEQ, segs=[(0, 16), (16, 16)], nrun=2, RL=16, first=True, last=True))
        order = [tiles[4]] + tiles[:4]
        order = order[: self.n_tiles]
        for T in order:
            self.run_tile(T)
        sp = self.engs["sp"]
        for en, e in self.engs.items():
            if e is not sp and e.cnt > 0:
                sp.h.wait_ge(e.sem, e.cnt)
        for sk, d in self.dsem.items():
            if d[1] > 0:
                sp.h.wait_ge(d[0], d[1])
        return nc

    def prologue_kv(self):
        nc = self.nc
        mT = self.A[:, 0:16 * 256].rearrange("p (c m) -> p c m", c=16)
        stage = self.Bn[:, 0:4096].bitcast(F32).rearrange("p (a n) -> p a n", a=4)
        self.dma("pool", [(mT, self.memT.rearrange("(c p) m -> p c m", p=128))], [], ["mT"], "mT")
        for l in range(self.n_layers):
            for j in range(4):
                w, wk = self.slab("kv%d_%d" % (l, j))
                for m in range(4):
                    ps, pk = self.ps()
                    for kc in range(16):
                        self.MM(ps[:, 0:256], w[:, kc * NW + m * 128: kc * NW + m * 128 + 128], mT[:, kc, :], kc == 0, kc == 15,
                                [wk, "mT"], [pk])
                    self.CP(stage[:, m, 0:256], ps[:, 0:256], [pk], ["kvst"], en="act")
                self.dma("sp", [(self.o_kT[l, j * 512:(j + 1) * 512, :].rearrange("(m p) n -> p m n", p=128), stage[:, :, 0:256])],
                         ["kvst"], ["o_kT%d" % l], "o_kT%d" % l)
            for j in range(4):
                w, wk = self.slab("kv%d_%d" % (l, 4 + j))
                for mc in range(2):
                    ps, pk = self.ps()
                    for kc in range(16):
                        self.MM(ps[:, :], mT[:, kc, mc * 128:(mc + 1) * 128], w[:, kc * NW:(kc + 1) * NW], kc == 0, kc == 15,
                                [wk, "mT"], [pk])
                    self.CP(stage[:, mc, :], ps[:, :], [pk], ["kvst"], en="act")
                self.dma("sp", [(self.o_v[l, :, j * 512:(j + 1) * 512].rearrange("(m p) n -> p m n", p=128), stage[:, 0:2, :])],
                         ["kvst"], ["o_v%d" % l], "o_v%d" % l)
        self.barrier()

    def run_tile(self, T):
        TT = T["TT"]
        xsrc = self.xT.rearrange("(c p) t -> p c t", p=128)[:, :, T["col0"]:T["col0"] + TT]
        self.dma("sp", [(self.x[:, 0:8, 0:TT], xsrc[:, 0:8, :]), (self.x[:, 8:16, 0:TT], xsrc[:, 8:16, :])],
                 [], ["x%d" % c for c in range(16)], "x")
        for l in range(self.n_layers):
            self.layer(T, l)
        self.rms(T, P_NFIN, out_f32=True)
        ydst = self.yT.rearrange("(c p) t -> p c t", p=128)[:, :, T["col0"]:T["col0"] + TT]
        yv = self.A[:, 0:16384].bitcast(F32).rearrange("p (c t) -> p c t", c=16)
        self.dma("sp", [(ydst, yv[:, :, 0:TT])], ["yout"], [], "yout")
        self.barrier()

    def rms(self, T, g0, out_f32=False):
        TT = T["TT"]
        ps, pk = self.ps()
        for c in range(16):
            sq = self.Bn[:, 0:2048].bitcast(F32).rearrange("p (a t) -> p a t", a=2)[:, c % 2, 0:TT]
            self.ACT(sq, self.x[:, c, 0:TT], AF.Square, ["x%d" % c], ["sq%d" % (c % 2)])
            self.MM(ps[:, 0:TT], self.ones_f, sq, c == 0, c == 15, ["sq%d" % (c % 2), "cst"], [pk])
        rs = self.rs[:, 0:TT]
        self.TS(rs, ps[:, 0:TT], 1.0 / D, EPS, ALU.mult, ALU.add, [pk], ["rs"])
        self.RSQ(rs, "rs")
        if out_f32:
            yv = self.A[:, 0:16384].bitcast(F32).rearrange("p (c t) -> p c t", c=16)
        for c in range(16):
            o = yv[:, c, 0:TT] if out_f32 else self.h[:, c, 0:TT]
            self.STT(o, self.x[:, c, 0:TT], self.vecP[:, g0 + c:g0 + c + 1], rs, ALU.mult, ALU.mult,
                     ["x%d" % c, "rs", "vecP"], ["yout" if out_f32 else "h"])

    def layer(self, T, l):
        self.dma("sp", [(self.vecP[:], self.vecP_d[l]), (self.vecR[:], self.vecR_d[l])], [], ["vecP", "vecR"], "vec")
        self.dma("pool", [(self.wsm[:], self.wsm_d[l]), (self.wa2[:], self.wa2_d[l])], [], ["wsm", "wa2"], "wsmall")
        self.ACT(self.arow[:], self.vecR[:, R_ALOG:R_ALOG + 32], AF.Exp, ["vecR"], ["arow"])
        self.TS(self.arow[:], self.arow[:], -1.0, None, ALU.mult, None, ["arow"], ["arow"])
        self.TS(self.nba[:], self.vecP[:, P_BA:P_BA + 8], -1.0, None, ALU.mult, None, ["vecP"], ["nba"])
        self.rms(T, P_NMIX)
        self.mixer(T, l)
        self.barrier()
        self.rms(T, P_NMEM)
        self.attn(T, l)
        self.barrier()
        self.rms(T, P_NFFN)
        self.ffn(T, l)
        self.barrier()

    def fm(self, T, w, wk, rhs, rk, nk=16, nch=4, nw=NW):
        TT = T["TT"]
        for m in range(nch):
            ps, pk = self.ps()
            for kc in range(nk):
                self.MM(ps[:, 0:TT], w[:, kc * nw + m * 128: kc * nw + m * 128 + 128], rhs(kc), kc == 0, kc == nk - 1,
                        [wk] + rk, [pk])
            yield m, ps, pk

    def tm(self, T, w, wk):
        for si, (c0, L) in enumerate(T["segs"]):
            ps, pk = self.ps()
            for kc in range(16):
                self.MM(ps[0:L, :], self.h[:, kc, c0:c0 + L], w[:, kc * NW:(kc + 1) * NW], kc == 0, kc == 15, [wk, "h"], [pk])
            yield si, c0, L, ps, pk

    def conv(self, T, ps, pk, width, wcol0, wstride, bcol, oc, hist, hkey, newhist, nkey, raw, rawk, acc, acck):
        nrun, RL, TT = T["nrun"], T["RL"], T["TT"]
        H = width - 1
        rv = raw[:, 0:nrun * (H + RL)].rearrange("p (r t) -> p r t", r=nrun)
        av = acc[:, 0:TT].rearrange("p (r t) -> p r t", r=nrun)
        self.CP(rv[:, :, H:H + RL], ps[:, 0:TT].rearrange("p (r t) -> p r t", r=nrun), [pk], [rawk], en="act")
        self.CP(rv[:, :, 0:H], hist, [hkey], [rawk], en="dve")
        wc = lambda tap: self.vecP[:, wcol0 + tap * wstride + oc: wcol0 + tap * wstride + oc + 1]
        self.TS(av, rv[:, :, H:H + RL], wc(width - 1), self.vecP[:, bcol + oc:bcol + oc + 1], ALU.mult, ALU.add,
                [rawk, "vecP"], [acck])
        for tap in range(width - 1):
            self.STT(av, rv[:, :, tap:tap + RL], wc(tap), av, ALU.mult, ALU.add, [rawk, "vecP", acck], [acck])
        self.CP(newhist, rv[:, :, RL:RL + H], [rawk], [nkey], en="dve")
        return av

    def mixer(self, T, l):
        nc = self.nc
        TT, segs, nrun, RL = T["TT"], T["segs"], T["nrun"], T["RL"]
        nseg = len(segs)
        prompt = T["kind"] == "p"
        A, Bn = self.A, self.Bn
        ysnT = A[:, 0:8192].rearrange("p (c t) -> p c t", c=16)
        ygnT = A[:, 8192:16384].rearrange("p (c t) -> p c t", c=16)
        mg = A[:, 16384:24576].rearrange("p (c t) -> p c t", c=16)
        pools = [[Bn, 0, 12288], [A, 16384, 24576]]

        def carve(n):
            for pl in pools:
                if pl[1] + n <= pl[2]:
                    a = pl[0][:, pl[1]:pl[1] + n]
                    pl[1] += n
                    return a
            raise AssertionError("arena overflow")
        xsT = carve(4 * TT).rearrange("p (c t) -> p c t", c=4)
        bcT = carve(4 * TT).rearrange("p (c t) -> p c t", c=4)
        sz = carve(nseg * 512).rearrange("p (s n) -> p s n", s=nseg)
        raw = [carve(1040).bitcast(F32) for _ in range(2)]
        cacc = [carve(2 * TT).bitcast(F32) for _ in range(2)]
        xs_tok = carve(512)
        B_tok = carve(128)
        rhsA = carve(2048).bitcast(F32).rearrange("p (r i) -> p r i", r=8)
        Eg = rhsA
        Wt = carve(1024).rearrange("p (r i) -> p r i", r=8)
        cbm = carve(256).bitcast(F32)
        t1 = carve(1024).bitcast(F32)
        t2 = carve(1024).bitcast(F32)
        yb = carve(1024).bitcast(F32)
        xw = carve(512)
        Sg = [carve(1024).bitcast(F32) for _ in range(max(1, 1 if prompt else 2))]
        Sgb = [carve(512) for _ in range(len(Sg))]
        ytok = carve(512)
        vR, vP = self.vecR, self.vecP
        sv = self.ssdv

        if not prompt:
            self.dma("sp", [(self.hs[:, :, 0:72], self.s_sconv_d[l].rearrange("s p n -> p s n"))], [], ["hs"], "hs")

        wsm = self.wsm
        ps, pk = self.ps()
        for kc in range(16):
            self.MM(ps[0:16, 0:TT], wsm[:, kc * 48 + 32: kc * 48 + 48], self.h[:, kc, 0:TT], kc == 0, kc == 15, ["wsm", "h"], [pk])
        self.CP(self.alrT[:, 0:TT], ps[0:16, 0:TT], [pk], ["alrT"], en="act")
        for si, (c0, L) in enumerate(segs):
            ps, pk = self.ps()
            for kc in range(16):
                self.MM(ps[0:L, 0:32], self.h[:, kc, c0:c0 + L], wsm[:, kc * 48: kc * 48 + 32], kc == 0, kc == 15, ["wsm", "h"], [pk])
            dt, dA, acs, ea, te, cd = [sv[:, si, q, :] for q in range(6)]
            k = "sv%d" % si
            self.TT(dt[0:L], ps[0:L, 0:32], vR[0:L, R_DTB:R_DTB + 32], ALU.add, [pk, "vecR"], [k])
            self.ACT(dt[0:L], dt[0:L], AF.Exp, [k], [k])
            self.ACT(dt[0:L], dt[0:L], AF.Ln, [k], [k], bias=1.0)
            self.TT(dA[0:L], dt[0:L], self.arow[0:L], ALU.mult, [k, "arow"], [k])
            ps2, pk2 = self.ps()
            self.MM(ps2[0:L, 0:32], self.U_f[0:L, 0:L], dA[0:L], True, True, [k, "cst"], [pk2])
            self.MM(ps2[:, 32:64], self.ones_f[0:L, :], dA[0:L], True, True, [k, "cst"], [pk2])
            self.CP(acs[0:L], ps2[0:L, 0:32], [pk2], [k], en="act")
            self.ACT(ea[0:L], ps2[0:L, 0:32], AF.Exp, [pk2], [k])
            self.ACT(cd, ps2[:, 32:64], AF.Exp, [pk2], [k])
            self.TT(te[0:L], ps2[0:L, 32:64], acs[0:L], ALU.subtract, [pk2, k], [k])
            self.ACT(te[0:L], te[0:L], AF.Exp, [k], [k])
            self.TT(te[0:L], te[0:L], dt[0:L], ALU.mult, [k], [k])

        xs_ch, bc_ch = xbc_slab_chunks()
        sidx = [0]

        def nslab():
            w, wk = self.slab("L%d_%d" % (l, sidx[0]))
            sidx[0] += 1
            return w, wk
        hfn = lambda kc: self.h[:, kc, 0:TT]
        hist_s = (lambda oc: self.cv_s[:, l, oc * 3:oc * 3 + 3].unsqueeze(1)) if prompt else \
                 (lambda oc: self.hs[:, :, oc * 3:oc * 3 + 3])
        nh_s = (lambda oc: self.cv_s[:, l, oc * 3:oc * 3 + 3].unsqueeze(1)) if prompt else \
               (lambda oc: self.ho[:, :, oc * 3:oc * 3 + 3])
        hk, nk_ = ("cv_s", "cv_s") if prompt else ("hs", "ho")
        cnt = [0]

        def xbc_slab(chs, dst, dkey):
            w, wk = nslab()
            for m, ps, pk in self.fm(T, w, wk, hfn, ["h"]):
                oc = chs[m]
                b = cnt[0] % 2
                cnt[0] += 1
                av = self.conv(T, ps, pk, 4, P_SCW, 24, P_SCB, oc, hist_s(oc), hk, nh_s(oc), nk_,
                               raw[b], "raw%d" % b, cacc[b], "cacc%d" % b)
                self.ACT(dst[:, m, 0:TT].rearrange("p (r t) -> p r t", r=nrun), av, AF.Silu, ["cacc%d" % b], [dkey])

        for g in range(4):
            xbc_slab(xs_ch[g], xsT, "xsT")
            if g % 2 == 0:
                xbc_slab(bc_ch[g // 2], bcT, "bcT")
            BT = bcT[:, 2 * (g % 2), :]
            CT = bcT[:, 2 * (g % 2) + 1, :]
            w, wk = nslab()
            for si, c0, L, ps, pk in self.tm(T, w, wk):
                self.ACT(sz[0:L, si, :], ps[0:L, :], AF.Silu, [pk], ["sz"])
            if prompt:
                if T["first"]:
                    self.MS(Sg[0], 0.0, ["Sg0"])
                else:
                    self.dma("sp", [(Sg[0], self.o_ssd[l, 0, :, g * 512:(g + 1) * 512])], ["o_ssd%d" % l], ["Sg0"], "Sg0")
            else:
                for s in range(NSAMP):
                    self.dma("sp", [(Sg[s], self.s_ssd_d[l, s, :, g * 512:(g + 1) * 512])], [], ["Sg%d" % s], "Sg%d" % s)
            for si, (c0, L) in enumerate(segs):
                sq_ = 0 if prompt else si
                S, Sb, Sk = Sg[sq_], Sgb[sq_], "Sg%d" % sq_
                dt, dA, acs, ea, te, cd = [sv[:, si, q, :] for q in range(6)]
                k = "sv%d" % si
                h0 = 8 * g
                self.CP(Sb, S, [Sk], [Sk + "b"], en="act")
                ps, pk = self.ps()
                for m in range(4):
                    self.MM(ps[0:L, m * 128:(m + 1) * 128], xsT[:, m, c0:c0 + L], self.identb[:], True, True, ["xsT", "identb"], [pk])
                self.CP(xs_tok[0:L, :], ps[0:L, :], [pk], ["xs_tok"], en="act")
                ps, pk = self.ps()
                self.MM(ps[0:L, 0:128], BT[:, c0:c0 + L], self.identb[:], True, True, ["bcT", "identb"], [pk])
                self.CP(B_tok[0:L, :], ps[0:L, 0:128], [pk], ["B_tok"], en="act")
                ps, pk = self.ps()
                self.MM(ps[0:L, 0:L], BT[:, c0:c0 + L], CT[:, c0:c0 + L], True, True, ["bcT"], [pk])
                self.TT(cbm[0:L, 0:L], ps[0:L, 0:L], self.U_f[0:L, 0:L], ALU.mult, [pk, "cst"], ["cbm"])
                self.TT(rhsA[0:L, :, 0:L], dA[0:L, h0:h0 + 8].unsqueeze(2).to_broadcast([L, 8, L]),
                        self.U_f[0:L, 0:L].unsqueeze(1).to_broadcast([L, 8, L]), ALU.mult, [k, "cst"], ["rhsA0", "rhsA1"])
                for half in range(2):
                    ps, pk = self.ps()
                    for r in range(4):
                        self.MM(ps[0:L, r * L:(r + 1) * L], self.SL_f[0:L, 0:L], rhsA[0:L, half * 4 + r, 0:L], True, True,
                                ["rhsA%d" % half, "cst"], [pk])
                    self.ACT(Eg[0:L, half * 4:half * 4 + 4, 0:L], ps[0:L, 0:4 * L].rearrange("p (r i) -> p r i", r=4), AF.Exp,
                             [pk], ["rhsA%d" % half])
                self.TT(Eg[0:L, :, 0:L], Eg[0:L, :, 0:L], cbm[0:L, 0:L].unsqueeze(1).to_broadcast([L, 8, L]), ALU.mult,
                        ["rhsA0", "rhsA1", "cbm"], ["rhsA0", "rhsA1"])
                self.TT(Wt[0:L, :, 0:L], Eg[0:L, :, 0:L], dt[0:L, h0:h0 + 8].unsqueeze(2).to_broadcast([L, 8, L]), ALU.mult,
                        ["rhsA0", "rhsA1", k], ["Wt"])
                psA, pkA = self.ps()
                for r in range(8):
                    self.MM(psA[0:L, r * 64:(r + 1) * 64], Wt[0:L, r, 0:L], xs_tok[0:L, r * 64:(r + 1) * 64], True, True,
                            ["Wt", "xs_tok"], [pkA])
                psB, pkB = self.ps()
                self.MM(psB[0:L, :], CT[:, c0:c0 + L], Sb, True, True, ["bcT", Sk + "b"], [pkB])
                self.TT(t1[0:L, :].rearrange("p (r q) -> p r q", r=8), psB[0:L, :].rearrange("p (r q) -> p r q", r=8),
                        ea[0:L, h0:h0 + 8].unsqueeze(2).to_broadcast([L, 8, 64]), ALU.mult, [pkB, k], ["t1"])
                self.TT(t2[0:L, :].rearrange("p (r q) -> p r q", r=8), xs_tok[0:L, :].rearrange("p (r q) -> p r q", r=8),
                        vR[0:L, R_DSK + h0:R_DSK + h0 + 8].unsqueeze(2).to_broadcast([L, 8, 64]), ALU.mult,
                        ["xs_tok", "vecR"], ["t2"])
                self.TT(yb[0:L, :], psA[0:L, :], t1[0:L, :], ALU.add, [pkA, "t1"], ["yb"])
                self.TT(yb[0:L, :], yb[0:L, :], t2[0:L, :], ALU.add, ["yb", "t2"], ["yb"])
                self.TT(xw[0:L, :].rearrange("p (r q) -> p r q", r=8), xs_tok[0:L, :].rearrange("p (r q) -> p r q", r=8),
                        te[0:L, h0:h0 + 8].unsqueeze(2).to_broadcast([L, 8, 64]), ALU.mult, ["xs_tok", k], ["xw"])
                psC, pkC = self.ps()
                self.MM(psC[:, :], B_tok[0:L, :], xw[0:L, :], True, True, ["B_tok", "xw"], [pkC])
                self.TT(S.rearrange("p (r q) -> p r q", r=8), S.rearrange("p (r q) -> p r q", r=8),
                        cd[:, h0:h0 + 8].unsqueeze(2).to_broadcast([128, 8, 64]), ALU.mult, [Sk, k, Sk + "b"], [Sk])
                self.TT(S, S, psC[:, :], ALU.add, [Sk, pkC], [Sk])
                self.TT(yb[0:L, :], yb[0:L, :], sz[0:L, si, :], ALU.mult, ["yb", "sz"], ["yb"])
                ss = self.small[:, 0:1]
                self.ACT(t1[0:L, :], yb[0:L, :], AF.Square, ["yb", "t1"], ["t1", "ss"], accum=ss[0:L])
                self.TS(ss[0:L], ss[0:L], 1.0 / 512, EPS, ALU.mult, ALU.add, ["ss"], ["ss"])
                self.RSQ(ss[0:L], "ss")
                self.STT(ytok[0:L, :], yb[0:L, :], ss[0:L], vR[0:L, R_SNORM + g * 512:R_SNORM + (g + 1) * 512], ALU.mult, ALU.mult,
                         ["yb", "ss", "vecR"], ["ytok"])
                ps, pk = self.ps()
                for m in range(4):
                    self.MM(ps[:, m * L:(m + 1) * L], ytok[0:L, m * 128:(m + 1) * 128], self.identb[0:L, 0:L], True, True,
                            ["ytok", "identb"], [pk])
                self.CP(ysnT[:, 4 * g:4 * g + 4, c0:c0 + L], ps[:, 0:4 * L].rearrange("p (m i) -> p m i", m=4), [pk], ["ysnT"], en="act")
            if prompt:
                self.dma("sp", [(self.o_ssd[l, 0, :, g * 512:(g + 1) * 512], Sg[0])], ["Sg0"], ["o_ssd%d" % l], "o_ssd%d" % l)
            else:
                for s in range(NSAMP):
                    self.dma("sp", [(self.o_ssd[l, 1 + s, :, g * 512:(g + 1) * 512], Sg[s])], ["Sg%d" % s], [], "o_ssds")
        if prompt:
            if T["last"]:
                self.dma("sp", [(self.o_sconv[l, 0], self.cv_s[:, l, :])], ["cv_s"], [], "o_cv")
        else:
            self.dma("sp", [(self.o_sconv[l, 1:3].rearrange("s p n -> p s n"), self.ho[:, :, 0:72])], ["ho"], [], "o_cv")
        self.barrier()
        self.gla(T, l, nslab, ygnT)
        self.barrier()
        self.merge(T, l, nslab, ysnT, ygnT, mg)

    def gla(self, T, l, nslab, ygnT):
        TT, segs = T["TT"], T["segs"]
        prompt = T["kind"] == "p"
        nseg = len(segs)
        Bn = self.Bn
        A = self.A
        pools = [[Bn, 0, 12288], [A, 16384, 24576]]

        def carve(n):
            for pl in pools:
                if pl[1] + n <= pl[2]:
                    a = pl[0][:, pl[1]:pl[1] + n]
                    pl[1] += n
                    return a
            raise AssertionError("arena overflow")
        cum = carve(4 * TT).bitcast(F32).rearrange("p (c t) -> p c t", c=2)
        ex = [carve(2 * TT).bitcast(F32) for _ in range(2)]
        qd = carve(2 * TT).rearrange("p (c t) -> p c t", c=2)
        ki = carve(2 * TT).rearrange("p (c t) -> p c t", c=2)
        ke = carve(2 * TT).rearrange("p (c t) -> p c t", c=2)
        vt = carve(nseg * 512).rearrange("p (s n) -> p s n", s=nseg)
        sg = carve(nseg * 512).rearrange("p (s n) -> p s n", s=nseg)
        ketok = carve(256)
        att = carve(128)
        ob = carve(1024).bitcast(F32)
        osq = carve(1024).bitcast(F32)
        otok = carve(512)
        ns = 1 if prompt else NSAMP
        S = [carve(2048).bitcast(F32).rearrange("p (c v) -> p c v", c=2) for _ in range(ns)]
        Sb = [carve(1024).rearrange("p (c v) -> p c v", c=2) for _ in range(ns)]
        dec = carve(64).bitcast(F32)
        zeros = carve(256).bitcast(F32)
        vR = self.vecR
        self.MS(zeros, 0.0, ["zeros"])
        for hd in range(4):
            for cc in range(2):
                ch = 2 * hd + cc
                ps, pk = self.ps()
                self.MM(ps[:, 0:TT], self.wa2[:, ch * 128:(ch + 1) * 128], self.alrT[:, 0:TT], True, True, ["wa2", "alrT"], [pk])
                e = ex[cc][:, 0:TT]
                self.ACT(e, ps[:, 0:TT], AF.Exp, [pk, "nba"], ["ex%d" % cc], bias=self.nba[:, ch:ch + 1], scale=-1.0)
                self.ACT(e, e, AF.Ln, ["ex%d" % cc], ["ex%d" % cc], bias=1.0)
                for si, (c0, L) in enumerate(segs):
                    self._pre("dve", ["ex%d" % cc, "zeros"], ["cum"])
                    i = self.nc.vector.tensor_tensor_scan(out=cum[:, cc, c0:c0 + L], data0=e[:, c0:c0 + L], data1=zeros[:, 0:L],
                                                          initial=0.0, op0=ALU.add, op1=ALU.add)
                    self.op("dve", i, ["ex%d" % cc, "zeros"], ["cum"])
                    self.ACT(dec[:, cc * nseg + si: cc * nseg + si + 1], cum[:, cc, c0 + L - 1:c0 + L], AF.Exp, ["cum"], ["dec"],
                             scale=-1.0 / 16)
            w, wk = nslab()
            for m, ps, pk in self.fm(T, w, wk, lambda kc: self.h[:, kc, 0:TT], ["h"]):
                cc = m % 2
                e = ex[m % 2][:, 0:TT]
                if m < 2:
                    self.ACT(e, cum[:, cc, 0:TT], AF.Exp, ["cum"], ["ex%d" % (m % 2)], scale=-1.0 / 16)
                    self.STT(qd[:, cc, 0:TT], ps[:, 0:TT], 1.0 / 16, e, ALU.mult, ALU.mult, [pk, "ex%d" % (m % 2)], ["qd"])
                else:
                    self.ACT(e, cum[:, cc, 0:TT], AF.Exp, ["cum"], ["ex%d" % (m % 2)], scale=1.0 / 16)
                    self.TT(ki[:, cc, 0:TT], ps[:, 0:TT], e, ALU.mult, [pk, "ex%d" % (m % 2)], ["ki"])
                    for si, (c0, L) in enumerate(segs):
                        self.TS(ke[:, cc, c0:c0 + L], ki[:, cc, c0:c0 + L], dec[:, cc * nseg + si: cc * nseg + si + 1], None,
                                ALU.mult, None, ["ki", "dec"], ["ke"])
            w, wk = nslab()
            for si, c0, L, ps, pk in self.tm(T, w, wk):
                self.CP(vt[0:L, si, :], ps[0:L, :], [pk], ["vt"], en="act")
            w, wk = nslab()
            for si, c0, L, ps, pk in self.tm(T, w, wk):
                self.ACT(sg[0:L, si, :], ps[0:L, :], AF.Silu, [pk], ["sg"])
            gsrc = lambda ap: ap.rearrange("(c p) v -> p c v", p=128)[:, 2 * hd:2 * hd + 2, :]
            if prompt:
                if T["first"]:
                    self.MS(S[0], 0.0, ["S0"])
                else:
                    self.dma("sp", [(S[0], gsrc(self.o_gla[l, 0]))], ["o_gla%d" % l], ["S0"], "S0")
            else:
                for s in range(NSAMP):
                    self.dma("sp", [(S[s], gsrc(self.s_gla_d[l, s]))], [], ["S%d" % s], "S%d" % s)
            for si, (c0, L) in enumerate(segs):
                sq_ = 0 if prompt else si
                St, Sbt, Sk = S[sq_], Sb[sq_], "S%d" % sq_
                self.CP(Sbt, St, [Sk], [Sk + "b"], en="act")
                ps, pk = self.ps()
                for cc in range(2):
                    self.MM(ps[0:L, cc * 128:(cc + 1) * 128], ke[:, cc, c0:c0 + L], self.identb[:], True, True, ["ke", "identb"], [pk])
                self.CP(ketok[0:L, :], ps[0:L, 0:256], [pk], ["ketok"], en="act")
                ps, pk = self.ps()
                for cc in range(2):
                    self.MM(ps[0:L, 0:L], ki[:, cc, c0:c0 + L], qd[:, cc, c0:c0 + L], cc == 0, cc == 1, ["ki", "qd"], [pk])
                self.TT(att[0:L, 0:L], ps[0:L, 0:L], self.U_f[0:L, 0:L], ALU.mult, [pk, "cst"], ["att"])
                psO, pkO = self.ps()
                self.MM(psO[0:L, :], att[0:L, 0:L], vt[0:L, si, :], True, False, ["att", "vt"], [pkO])
                for cc in range(2):
                    self.MM(psO[0:L, :], qd[:, cc, c0:c0 + L], Sbt[:, cc, :], False, cc == 1, ["qd", Sk + "b"], [pkO])
                for cc in range(2):
                    psS, pkS = self.ps()
                    self.MM(psS[:, :], ketok[0:L, cc * 128:(cc + 1) * 128], vt[0:L, si, :], True, True, ["ketok", "vt"], [pkS])
                    self.STT(St[:, cc, :], St[:, cc, :], dec[:, cc * nseg + si: cc * nseg + si + 1], psS[:, :], ALU.mult, ALU.add,
                             [Sk, Sk + "b", "dec", pkS], [Sk])
                ss = self.small[:, 1:2]
                self.ACT(osq[0:L, :], psO[0:L, :], AF.Square, [pkO], ["osq", "ss2"], accum=ss[0:L])
                self.TS(ss[0:L], ss[0:L], 1.0 / 512, EPS, ALU.mult, ALU.add, ["ss2"], ["ss2"])
                self.RSQ(ss[0:L], "ss2")
                self.STT(ob[0:L, :], psO[0:L, :], ss[0:L], vR[0:L, R_GNORM:R_GNORM + 512], ALU.mult, ALU.mult,
                         [pkO, "ss2", "vecR"], ["ob"])
                self.TT(otok[0:L, :], ob[0:L, :], sg[0:L, si, :], ALU.mult, ["ob", "sg"], ["otok"])
                ps, pk = self.ps()
                for m in range(4):
                    self.MM(ps[:, m * L:(m + 1) * L], otok[0:L, m * 128:(m + 1) * 128], self.identb[0:L, 0:L], True, True,
                            ["otok", "identb"], [pk])
                self.CP(ygnT[:, 4 * hd:4 * hd + 4, c0:c0 + L], ps[:, 0:4 * L].rearrange("p (m i) -> p m i", m=4), [pk], ["ygnT"], en="act")
            gdst = lambda ap: ap.rearrange("(c p) v -> p c v", p=128)[:, 2 * hd:2 * hd + 2, :]
            if prompt:
                self.dma("sp", [(gdst(self.o_gla[l, 0]), S[0])], ["S0"], ["o_gla%d" % l], "o_gla%d" % l)
            else:
                for s in range(NSAMP):
                    self.dma("sp", [(gdst(self.o_gla[l, 1 + s]), S[s])], ["S%d" % s], [], "o_glas")

    def merge(self, T, l, nslab, ysnT, ygnT, mg):
        TT = T["TT"]
        Bn = self.Bn
        sgt = Bn[:, 0:2048].rearrange("p (c t) -> p c t", c=4)
        acc = Bn[:, 2048:2048 + 4096].bitcast(F32).rearrange("p (c t) -> p c t", c=4)
        hfn = lambda kc: self.h[:, kc, 0:TT]
        for j in range(4):
            for n, src, sk in ((0, ysnT, "ysnT"), (1, ygnT, "ygnT")):
                w, wk = nslab()
                for m, ps, pk in self.fm(T, w, wk, hfn, ["h"]):
                    self.ACT(sgt[:, m, 0:TT], ps[:, 0:TT], AF.Sigmoid, [pk], ["sgt%d" % m])
                w, wk = nslab()
                for m, ps, pk in self.fm(T, w, wk, lambda kc: src[:, kc, 0:TT], [sk]):
                    if n == 0:
                        self.TT(acc[:, m, 0:TT], ps[:, 0:TT], sgt[:, m, 0:TT], ALU.mult, [pk, "sgt%d" % m], ["acc%d" % m])
                    else:
                        self.TT(self.rs[:, 0:TT], ps[:, 0:TT], sgt[:, m, 0:TT], ALU.mult, [pk, "sgt%d" % m], ["rs"])
                        self.TT(mg[:, 4 * j + m, 0:TT], self.rs[:, 0:TT], acc[:, m, 0:TT], ALU.add, ["rs", "acc%d" % m], ["mg"])
        for j in range(4):
            w, wk = nslab()
            for m, ps, pk in self.fm(T, w, wk, lambda kc: mg[:, kc, 0:TT], ["mg"]):
                c = 4 * j + m
                self.TT(self.x[:, c, 0:TT], self.x[:, c, 0:TT], ps[:, 0:TT], ALU.add, [pk, "x%d" % c], ["x%d" % c])
        self.sidx_after_mixer = None

    def attn(self, T, l):
        TT, segs = T["TT"], T["segs"]
        prompt = T["kind"] == "p"
        A, Bn = self.A, self.Bn
        qT = A[:, 0:8192].rearrange("p (c t) -> p c t", c=16)
        oT = A[:, 8192:16384].rearrange("p (c t) -> p c t", c=16)
        ngrp = 1 if prompt else NSAMP
        kvreg = [Bn[:, 0:8192], A[:, 16384:24576]]
        kTb = [kvreg[g][:, 0:4096].rearrange("p (c m) -> p c m", c=16) for g in range(ngrp)]
        vb = [kvreg[g][:, 4096:8192].rearrange("p (c n) -> p c n", c=2) for g in range(ngrp)]
        base = 8192
        sc = Bn[:, base:base + 512].bitcast(F32)
        pb = Bn[:, base + 512:base + 768]
        pT = Bn[:, base + 768:base + 768 + 1024].rearrange("p (c t) -> p c t", c=2)
        sm = self.small
        for g in range(ngrp):
            if prompt:
                ksrc, vsrc, rk = self.o_kT[l], self.o_v[l], ["o_kT%d" % l, "o_v%d" % l]
            else:
                ksrc, vsrc, rk = self.s_kT_d[l, g], self.s_v_d[l, g], []
            self.dma("pool", [(kTb[g], ksrc.rearrange("(c p) m -> p c m", p=128)),
                              (vb[g], vsrc.rearrange("(c p) n -> p c n", p=128))], rk, ["kv%d" % g, "sq0", "sq1"], "kvl%d" % g)
        sidx = [42]

        def nslab():
            w, wk = self.slab("L%d_%d" % (l, sidx[0]))
            sidx[0] += 1
            return w, wk
        for j in range(4):
            w, wk = nslab()
            for m, ps, pk in self.fm(T, w, wk, lambda kc: self.h[:, kc, 0:TT], ["h"]):
                self.ACT(qT[:, 4 * j + m, 0:TT], ps[:, 0:TT], AF.Copy, [pk], ["qT"], scale=float(512 ** -0.5))
        groups = [(0, segs)] if prompt else [(s, [segs[s]]) for s in range(NSAMP)]
        for g, gsegs in groups:
            g0 = gsegs[0][0]
            gl = sum(L for _, L in gsegs)
            for hd in range(4):
                for (c0, L) in gsegs:
                    ps, pk = self.ps()
                    for cc in range(4):
                        self.MM(ps[0:L, 0:256], qT[:, 4 * hd + cc, c0:c0 + L], kTb[g][:, 4 * hd + cc, :], cc == 0, cc == 3,
                                ["qT", "kv%d" % g], [pk])
                    mx = sm[:, 2:3]
                    self._pre("dve", [pk], ["mx"])
                    i = self.nc.vector.reduce_max(out=mx[0:L], in_=ps[0:L, 0:256], axis=AX.X)
                    self.op("dve", i, [pk], ["mx"])
                    self.TS(mx[0:L], mx[0:L], -1.0, None, ALU.mult, None, ["mx"], ["mx"])
                    sme = sm[:, 3:4]
                    self.ACT(sc[0:L, :], ps[0:L, 0:256], AF.Exp, [pk, "mx"], ["sc", "sme"], bias=mx[0:L], accum=sme[0:L])
                    self._pre("dve", ["sme"], ["sme"])
                    i = self.nc.vector.reciprocal(out=sme[0:L], in_=sme[0:L])
                    self.op("dve", i, ["sme"], ["sme"])
                    self.TS(pb[0:L, :], sc[0:L, :], sme[0:L], None, ALU.mult, None, ["sc", "sme"], ["pb"])
                    ps2, pk2 = self.ps()
                    for mc in range(2):
                        self.MM(ps2[:, mc * L:(mc + 1) * L], pb[0:L, mc * 128:(mc + 1) * 128], self.identb[0:L, 0:L], True, True,
                                ["pb", "identb"], [pk2])
                    self.CP(pT[:, :, c0 - g0:c0 - g0 + L], ps2[:, 0:2 * L].rearrange("p (c i) -> p c i", c=2), [pk2], ["pT"], en="act")
                for cc in range(4):
                    ps, pk = self.ps()
                    for mc in range(2):
                        self.MM(ps[:, 0:gl], vb[g][:, mc, (4 * hd + cc) * 128:(4 * hd + cc + 1) * 128], pT[:, mc, 0:gl], mc == 0, mc == 1,
                                ["kv%d" % g, "pT"], [pk])
                    self.CP(oT[:, 4 * hd + cc, g0:g0 + gl], ps[:, 0:gl], [pk], ["oT"], en="act")
        for j in range(4):
            w, wk = nslab()
            for m, ps, pk in self.fm(T, w, wk, lambda kc: oT[:, kc, 0:TT], ["oT"]):
                c = 4 * j + m
                self.TT(self.x[:, c, 0:TT], self.x[:, c, 0:TT], ps[:, 0:TT], ALU.add, [pk, "x%d" % c], ["x%d" % c])

    def ffn(self, T, l):
        TT, nrun, RL = T["TT"], T["nrun"], T["RL"]
        prompt = T["kind"] == "p"
        A, Bn = self.A, self.Bn
        act = A[:, 0:24 * 512].rearrange("p (c t) -> p c t", c=24)
        raw = [Bn[:, i * 1040:(i + 1) * 1040].bitcast(F32) for i in range(2)]
        cacc = [Bn[:, 2080 + i * 1024:2080 + (i + 1) * 1024].bitcast(F32) for i in range(2)]
        cu = Bn[:, 4128:4128 + 4096].bitcast(F32).rearrange("p (c t) -> p c t", c=4)
        if not prompt:
            self.dma("sp", [(self.hs[:, :, :], self.s_fconv_d[l].rearrange("s p n -> p s n"))], [], ["hs"], "hs")
        hist = (lambda oc: self.cv_f[:, l, oc * 2:oc * 2 + 2].unsqueeze(1)) if prompt else (lambda oc: self.hs[:, :, oc * 2:oc * 2 + 2])
        nh = (lambda oc: self.cv_f[:, l, oc * 2:oc * 2 + 2].unsqueeze(1)) if prompt else (lambda oc: self.ho[:, :, oc * 2:oc * 2 + 2])
        hk, nk_ = ("cv_f", "cv_f") if prompt else ("hs", "ho")
        sidx = [50]
        cnt = 0
        hfn = lambda kc: self.h[:, kc, 0:TT]
        for hf, (j0, j1) in enumerate(((0, 6), (6, 11))):
            nkc = 4 * (j1 - j0)
            for j in range(j0, j1):
                for part in range(2):
                    w, wk = self.slab("L%d_%d" % (l, sidx[0]))
                    sidx[0] += 1
                    for m, ps, pk in self.fm(T, w, wk, hfn, ["h"]):
                        oc = part * 44 + 4 * j + m
                        b = cnt % 2
                        cnt += 1
                        av = self.conv(T, ps, pk, 3, P_FCW, 88, P_FCB, oc, hist(oc), hk, nh(oc), nk_,
                                       raw[b], "raw%d" % b, cacc[b], "cacc%d" % b)
                        cuv = cu[:, m, 0:TT].rearrange("p (r t) -> p r t", r=nrun)
                        if part == 0:
                            self.CP(cuv, av, ["cacc%d" % b], ["cu%d" % m], en="act")
                        else:
                            self.ACT(av, av, AF.Silu, ["cacc%d" % b], ["cacc%d" % b])
                            self.TT(act[:, 4 * (j - j0) + m, 0:TT].rearrange("p (r t) -> p r t", r=nrun), av, cuv, ALU.mult,
                                    ["cacc%d" % b, "cu%d" % m], ["act"])
            for j in range(8):
                w, wk = self.slab("F%d_%d_%d" % (l, hf, j))
                for m, ps, pk in self.fm(T, w, wk, lambda kc: act[:, kc, 0:TT], ["act"], nk=nkc, nch=2, nw=256):
                    c = 2 * j + m
                    self.TT(self.x[:, c, 0:TT], self.x[:, c, 0:TT], ps[:, 0:TT], ALU.add, [pk, "x%d" % c], ["x%d" % c])
        if prompt:
            if T["last"]:
                self.dma("sp", [(self.o_fconv[l, 0], self.cv_f[:, l, :])], ["cv_f"], [], "o_cv")
        else:
            self.dma("sp", [(self.o_fconv[l, 1:3].rearrange("s p n -> p s n"), self.ho[:, :, :])], ["ho"], [], "o_cv")


def _slab(wcols):
    n = wcols.shape[1]
    return np.ascontiguousarray(wcols.reshape(16, 128, n).transpose(1, 0, 2).reshape(128, 16 * n))


def _layer_slabs(w_in, w_branch, w_out, w_mq, w_mo, w_ffn_in):
    xs_ch, bc_ch = xbc_slab_chunks()
    xbc = w_in[:, O_XBC:O_XBC + 3072]
    sl = []
    pick = lambda chs: np.concatenate([xbc[:, c * 128:(c + 1) * 128] for c in chs], axis=1)
    for g in range(4):
        sl.append(pick(xs_ch[g]))
        if g % 2 == 0:
            sl.append(pick(bc_ch[g // 2]))
        sl.append(w_in[:, O_Z + g * 512:O_Z + (g + 1) * 512])
    for hd in range(4):
        sl.append(np.concatenate([w_in[:, O_Q + hd * 256:O_Q + (hd + 1) * 256], w_in[:, O_K + hd * 256:O_K + (hd + 1) * 256]], axis=1))
        sl.append(w_in[:, O_V + hd * 512:O_V + (hd + 1) * 512])
        sl.append(w_in[:, O_G + hd * 512:O_G + (hd + 1) * 512])
    for j in range(4):
        for n in range(2):
            sl.append(w_in[:, O_GATES + n * 2048 + j * 512:O_GATES + n * 2048 + (j + 1) * 512])
            sl.append(w_branch[n][:, j * 512:(j + 1) * 512])
    for j in range(4):
        sl.append(w_out[:, j * 512:(j + 1) * 512])
    for j in range(4):
        sl.append(w_mq[:, j * 512:(j + 1) * 512])
    for j in range(4):
        sl.append(w_mo[:, j * 512:(j + 1) * 512])
    for j in range(11):
        sl.append(w_ffn_in[:, j * 512:(j + 1) * 512])
        sl.append(w_ffn_in[:, DFF + j * 512:DFF + (j + 1) * 512])
    assert len(sl) == NSLAB
    return np.stack([_slab(s) for s in sl])


def _pvec(v):
    return v.reshape(-1, 128).T


_NC_CACHE = {}


def kernel(x_prompt, x_sample, mem_prompt, state_ssd, state_ssd_conv, state_gla, state_ffn_conv,
           cache_mem_k, cache_mem_v, norm_mix, w_in, ssd_conv_w, ssd_conv_b, ssd_dt_bias, ssd_a_log,
           ssd_d, ssd_norm, gla_wa2, gla_ba, gla_norm, w_branch, w_out, norm_mem, w_mq, w_mk, w_mv,
           w_mo, norm_ffn, w_ffn_in, ffn_conv_w, ffn_conv_b, w_ffn_out, norm_final, _n_tiles=5, _n_layers=2, _cores=8):
    f = lambda a: np.asarray(a, dtype=np.float32)
    (x_prompt, x_sample, mem_prompt, state_ssd, state_ssd_conv, state_gla, state_ffn_conv, cache_mem_k, cache_mem_v,
     norm_mix, w_in, ssd_conv_w, ssd_conv_b, ssd_dt_bias, ssd_a_log, ssd_d, ssd_norm, gla_wa2, gla_ba, gla_norm,
     w_branch, w_out, norm_mem, w_mq, w_mk, w_mv, w_mo, norm_ffn, w_ffn_in, ffn_conv_w, ffn_conv_b, w_ffn_out,
     norm_final) = map(f, (x_prompt, x_sample, mem_prompt, state_ssd, state_ssd_conv, state_gla, state_ffn_conv,
                           cache_mem_k, cache_mem_v, norm_mix, w_in, ssd_conv_w, ssd_conv_b, ssd_dt_bias, ssd_a_log,
                           ssd_d, ssd_norm, gla_wa2, gla_ba, gla_norm, w_branch, w_out, norm_mem, w_mq, w_mk, w_mv,
                           w_mo, norm_ffn, w_ffn_in, ffn_conv_w, ffn_conv_b, w_ffn_out, norm_final))
    key = (_n_tiles, _n_layers)
    if key not in _NC_CACHE:
        _NC_CACHE[key] = Builder(_n_tiles, _n_layers).build()
    nc = _NC_CACHE[key]

    cst = np.zeros((128, 512), np.float32)
    idx = np.arange(128)
    cst[:, 0:128] = np.eye(128, dtype=np.float32)
    cst[:, 128:256] = (idx[:, None] <= idx[None, :])
    cst[:, 256:384] = (idx[:, None] > idx[None, :])
    cst[:, 384:512] = 1.0
    shared = {"cst": cst}
    for l in range(DEPTH):
        shared["W%d" % l] = _layer_slabs(w_in[l], w_branch[l], w_out[l], w_mq[l], w_mo[l], w_ffn_in[l])
        wf = np.zeros((2, 8, 128, 24 * 256), np.float32)
        for hf, (k0, k1) in enumerate(((0, 24), (24, 44))):
            for j in range(8):
                blk = w_ffn_out[l][k0 * 128:k1 * 128, j * 256:(j + 1) * 256].reshape(k1 - k0, 128, 256).transpose(1, 0, 2)
                wf[hf, j, :, 0:(k1 - k0) * 256] = blk.reshape(128, (k1 - k0) * 256)
        shared["WF%d" % l] = wf
        shared["WKV%d" % l] = np.stack([_slab(w_mk[l][:, j * 512:(j + 1) * 512]) for j in range(4)] +
                                       [_slab(w_mv[l][:, j * 512:(j + 1) * 512]) for j in range(4)])
        shared["wsm%d" % l] = _slab(np.concatenate([w_in[l][:, O_DT:O_DT + 32], w_in[l][:, O_ALR:O_ALR + 16]], axis=1))
        shared["wa2_%d" % l] = np.ascontiguousarray(gla_wa2[l])
        vp = np.zeros((128, NVP), np.float32)
        vp[:, P_NMIX:P_NMIX + 16] = _pvec(norm_mix[l])
        vp[:, P_NMEM:P_NMEM + 16] = _pvec(norm_mem[l])
        vp[:, P_NFFN:P_NFFN + 16] = _pvec(norm_ffn[l])
        vp[:, P_NFIN:P_NFIN + 16] = _pvec(norm_final)
        for tap in range(4):
            vp[:, P_SCW + tap * 24:P_SCW + (tap + 1) * 24] = _pvec(ssd_conv_w[l, tap])
        vp[:, P_SCB:P_SCB + 24] = _pvec(ssd_conv_b[l])
        for tap in range(3):
            vp[:, P_FCW + tap * 88:P_FCW + (tap + 1) * 88] = _pvec(ffn_conv_w[l, tap])
        vp[:, P_FCB:P_FCB + 88] = _pvec(ffn_conv_b[l])
        vp[:, P_BA:P_BA + 8] = _pvec(gla_ba[l])
        shared["vecP%d" % l] = vp
        vr = np.concatenate([ssd_dt_bias[l], ssd_a_log[l], ssd_d[l], ssd_norm[l], gla_norm[l]])
        shared["vecR%d" % l] = np.ascontiguousarray(np.broadcast_to(vr[None, :], (128, NVR)))

    in_maps = []
    for c in range(8):
        b = c % 4
        ss = [2 * c, 2 * c + 1]
        m = dict(shared)
        m["xT"] = np.ascontiguousarray(np.concatenate([x_prompt[b].T] + [x_sample[s].T for s in ss], axis=1))
        m["memT"] = np.ascontiguousarray(mem_prompt[b].T)
        m["s_ssd"] = np.ascontiguousarray(np.stack([[state_ssd[l, s].reshape(2048, 128).T for s in ss] for l in range(DEPTH)]))
        m["s_sconv"] = np.ascontiguousarray(np.stack([[state_ssd_conv[l, s].reshape(3, 24, 128).transpose(2, 1, 0).reshape(128, 72)
                                                       for s in ss] for l in range(DEPTH)]))
        m["s_fconv"] = np.ascontiguousarray(np.stack([[state_ffn_conv[l, s].reshape(2, 88, 128).transpose(2, 1, 0).reshape(128, 176)
                                                       for s in ss] for l in range(DEPTH)]))
        m["s_gla"] = np.ascontiguousarray(np.stack([[state_gla[l, s].reshape(1024, 512) for s in ss] for l in range(DEPTH)]))
        m["s_kT"] = np.ascontiguousarray(np.stack([[cache_mem_k[l, s].reshape(256, 2048).T for s in ss] for l in range(DEPTH)]))
        m["s_v"] = np.ascontiguousarray(np.stack([[cache_mem_v[l, s].reshape(256, 2048) for s in ss] for l in range(DEPTH)]))
        in_maps.append(m)

    res = run_bass_kernel_spmd(nc, in_maps[:_cores], core_ids=list(range(_cores))).results
    res = list(res) + [res[0]] * (8 - _cores)

    B, NSEQ = 4, 16
    y_prompt = np.stack([res[b]["yT"][:, 0:SEQ].T for b in range(B)])
    y_sample = np.stack([res[s // 2]["yT"][:, SEQ + (s % 2) * SLEN:SEQ + (s % 2 + 1) * SLEN].T for s in range(NSEQ)])

    def un_ssd(a):
        return a.T.reshape(32, 64, 128)

    def un_conv(a, w, nch):
        return a.reshape(128, nch, w).transpose(2, 1, 0).reshape(w, nch * 128)
    p_ssd = np.stack([[un_ssd(res[b]["o_ssd"][l, 0]) for b in range(B)] for l in range(DEPTH)])
    s_ssd = np.stack([[un_ssd(res[s // 2]["o_ssd"][l, 1 + s % 2]) for s in range(NSEQ)] for l in range(DEPTH)])
    p_sconv = np.stack([[un_conv(res[b]["o_sconv"][l, 0], 3, 24) for b in range(B)] for l in range(DEPTH)])
    s_sconv = np.stack([[un_conv(res[s // 2]["o_sconv"][l, 1 + s % 2], 3, 24) for s in range(NSEQ)] for l in range(DEPTH)])
    p_gla = np.stack([[res[b]["o_gla"][l, 0].reshape(4, 256, 512) for b in range(B)] for l in range(DEPTH)])
    s_gla = np.stack([[res[s // 2]["o_gla"][l, 1 + s % 2].reshape(4, 256, 512) for s in range(NSEQ)] for l in range(DEPTH)])
    p_fconv = np.stack([[un_conv(res[b]["o_fconv"][l, 0], 2, 88) for b in range(B)] for l in range(DEPTH)])
    s_fconv = np.stack([[un_conv(res[s // 2]["o_fconv"][l, 1 + s % 2], 2, 88) for s in range(NSEQ)] for l in range(DEPTH)])
    p_mem_k = np.stack([[res[b]["o_kT"][l].T.reshape(256, 4, 512) for b in range(B)] for l in range(DEPTH)])
    p_mem_v = np.stack([[res[b]["o_v"][l].reshape(256, 4, 512) for b in range(B)] for l in range(DEPTH)])
    outs = (y_prompt, y_sample, p_ssd, p_sconv, p_gla, p_fconv, p_mem_k, p_mem_v, s_ssd, s_sconv, s_gla, s_fconv)
    return tuple(np.ascontiguousarray(o, dtype=np.float32) for o in outs)
```

```python
import numpy as np
from contextlib import ExitStack
import concourse.bass as bass
import concourse.mybir as mybir
from concourse.bass_utils import run_bass_kernel_spmd

F32 = mybir.dt.float32
BF16 = mybir.dt.bfloat16
AF = mybir.ActivationFunctionType
ALU = mybir.AluOpType
AX = mybir.AxisListType

D = 2048
KC = 16
SEQ = 2048
NSAMP = 4
SLEN = 16
TOK = 64 + 5 * 512
NY = 2 * 64 + 5 * 512
NSTEP = 7
DEPTH = 2
EPS = 1e-6
DFF = 5632
IN_DIM = 15408
O_Z, O_XBC, O_DT, O_Q, O_K, O_V, O_G, O_ALR, O_GATES = 0, 2048, 5120, 5152, 6176, 7200, 9248, 11296, 11312
P_NMIX, P_NMEM, P_NFFN, P_NFIN, P_SCW, P_SCB, P_FCW, P_FCB, P_BA, NVP = 0, 16, 32, 48, 64, 160, 184, 448, 536, 544
R_DTB, R_ALOG, R_DSK, R_SNORM, R_GNORM, NVR = 0, 32, 64, 96, 2144, 2656
NSLAB = 72
NW = 512


def xbc_slab_chunks():
    xs = [[4 * g + m for m in range(4)] for g in range(4)]
    bc = [[16 + 2 * j, 20 + 2 * j, 16 + 2 * j + 1, 20 + 2 * j + 1] for j in range(2)]
    return xs, bc


class E:
    def __init__(self, h, sem, name):
        self.h, self.sem, self.cnt, self.waited, self.name = h, sem, 0, {}, name


class Builder:
    def __init__(self, n_tiles=NSTEP, rg=None):
        self.n_tiles, self.n_layers = n_tiles, 1
        self.rg = rg or [[0, 4], [1, 5], [2, 6], [3, 7]]
        self.nc = nc = bass.Bass("TRN2", target_bir_lowering=False)
        self.es = ExitStack()
        self.st = {}
        self.dsem = {}
        self.engs = {}
        for nm, h in (("pe", nc.tensor), ("act", nc.scalar), ("dve", nc.vector), ("pool", nc.gpsimd), ("sp", nc.sync)):
            sem = self.es.enter_context(nc.semaphore("e_" + nm))
            self.engs[nm] = E(h, sem, nm)
        self.psi = 0

    def _pre(self, en, R, W):
        eng = self.engs[en]
        deps = {}

        def add(ev):
            if ev is None:
                return
            sem, val = ev
            k = id(sem)
            if k not in deps or deps[k][1] < val:
                deps[k] = (sem, val)
        for k in R:
            s = self.st.get(k)
            if s:
                add(s[0])
        for k in W:
            s = self.st.get(k)
            if s:
                add(s[0])
                for ev in s[1].values():
                    add(ev)
        for sem, val in deps.values():
            if en == "pe" and sem is eng.sem:
                continue
            if eng.waited.get(id(sem), 0) < val:
                eng.h.wait_ge(sem, val)
                eng.waited[id(sem)] = val

    def _post(self, ev, R, W):
        for k in R:
            s = self.st.setdefault(k, [None, {}])
            s[1][id(ev[0])] = ev
        for k in W:
            self.st[k] = [ev, {}]

    def op(self, en, inst, R, W):
        eng = self.engs[en]
        eng.cnt += 1
        inst.then_inc(eng.sem, 1)
        self._post((eng.sem, eng.cnt), R, W)

    def dma(self, en, pairs, R, W, sk):
        eng = self.engs[en]
        self._pre(en, R, W)
        if sk not in self.dsem:
            self.dsem[sk] = [self.es.enter_context(self.nc.semaphore("d_" + sk)), 0]
        d = self.dsem[sk]
        if d[1] > 0 and eng.waited.get(id(d[0]), 0) < d[1]:
            eng.h.wait_ge(d[0], d[1])
            eng.waited[id(d[0])] = d[1]
        for o, i in pairs:
            eng.h.dma_start(out=o, in_=i).then_inc(d[0], 16)
            d[1] += 16
        self._post((d[0], d[1]), R, W)

    def barrier(self, keep=("wsl0", "wsl1")):
        names = ("pe", "act", "dve", "sp")
        for en in names + ("pool",):
            eng = self.engs[en]
            for on in names:
                o = self.engs[on]
                if o is eng or o.cnt == 0:
                    continue
                if eng.waited.get(id(o.sem), 0) < o.cnt:
                    eng.h.wait_ge(o.sem, o.cnt)
                    eng.waited[id(o.sem)] = o.cnt
            for sk, d in self.dsem.items():
                if sk in keep or d[1] == 0 or (sk == "cc" and en != "pool"):
                    continue
                if eng.waited.get(id(d[0]), 0) < d[1]:
                    eng.h.wait_ge(d[0], d[1])
                    eng.waited[id(d[0])] = d[1]
        self.st = {k: v for k, v in self.st.items() if k in keep}

    def ACT(self, out, in_, func, R, W, bias=None, scale=None, accum=None):
        self._pre("act", R, W)
        kw = {}
        if bias is not None:
            kw["bias"] = bias
        if scale is not None:
            kw["scale"] = scale
        if accum is not None:
            kw["accum_out"] = accum
        i = self.nc.scalar.activation(out=out, in_=in_, func=func, **kw)
        self.op("act", i, R, W)

    def TT(self, out, in0, in1, op, R, W, en="dve"):
        self._pre(en, R, W)
        i = self.engs[en].h.tensor_tensor(out=out, in0=in0, in1=in1, op=op)
        self.op(en, i, R, W)

    def TS(self, out, in0, s1, s2, op0, op1, R, W, en="dve"):
        self._pre(en, R, W)
        if s2 is None:
            i = self.engs[en].h.tensor_scalar(out=out, in0=in0, scalar1=s1, scalar2=None, op0=op0)
        else:
            i = self.engs[en].h.tensor_scalar(out=out, in0=in0, scalar1=s1, scalar2=s2, op0=op0, op1=op1)
        self.op(en, i, R, W)

    def STT(self, out, in0, scalar, in1, op0, op1, R, W, en="dve"):
        self._pre(en, R, W)
        i = self.engs[en].h.scalar_tensor_tensor(out=out, in0=in0, scalar=scalar, in1=in1, op0=op0, op1=op1)
        self.op(en, i, R, W)

    def CP(self, out, in_, R, W, en="dve"):
        self._pre(en, R, W)
        if en == "act":
            i = self.nc.scalar.activation(out=out, in_=in_, func=AF.Copy)
        else:
            i = self.engs[en].h.tensor_copy(out=out, in_=in_)
        self.op(en, i, R, W)

    def RSQ(self, ap, key):
        self.ACT(ap, ap, AF.Sqrt, [key], [key])
        self._pre("dve", [key], [key])
        i = self.nc.vector.reciprocal(out=ap, in_=ap)
        self.op("dve", i, [key], [key])

    def MS(self, ap, val, W, en="dve"):
        self._pre(en, [], W)
        i = self.engs[en].h.memset(ap, val)
        self.op(en, i, [], W)

    def MM(self, out, lhsT, rhs, start, stop, R, W):
        self._pre("pe", R, W)
        i = self.nc.tensor.matmul(out, lhsT=lhsT, rhs=rhs, start=start, stop=stop)
        self.op("pe", i, R, W)

    def ps(self):
        i = self.psi
        self.psi = (self.psi + 1) % 8
        return self.psum[i], "ps%d" % i

    def slab(self, expect):
        q = self.wq
        i = self.wi
        assert q[i][0] == expect, (q[i][0], expect)
        for j in (i, i + 1):
            if j < len(q) and j >= self.wissued:
                name, src, n = q[j]
                b = j % 2
                self.dma("pool", [(self.wsl[b][:, 0:n], src)], [], ["wsl%d" % b], "wsl%d" % b)
                self.wissued = j + 1
        self.wi += 1
        return self.wsl[i % 2], "wsl%d" % (i % 2)

    def build(self):
        nc = self.nc
        es = self.es
        dt_in = lambda name, shape: nc.dram_tensor(name, shape, F32, kind="ExternalInput").ap()
        dt_out = lambda name, shape: nc.dram_tensor(name, shape, F32, kind="ExternalOutput").ap()
        self.xT = dt_in("xT", [D, TOK])
        self.memT = dt_in("memT", [D, 256])
        self.cst_d = dt_in("cst", [128, 512])
        self.flg_d = dt_in("flg", [128, 8])
        self.W = [dt_in("W0", [NSLAB, 128, 8192])]
        self.WF = [dt_in("WF0", [2, 8, 128, 24 * 256])]
        self.WKV = [dt_in("WKV0", [8, 128, 8192])]
        self.wsm_d = [dt_in("wsm0", [128, 16 * 48])]
        self.wa2_d = [dt_in("wa2_0", [16, 1024])]
        self.vecP_d = [dt_in("vecP0", [128, NVP])]
        self.vecR_d = [dt_in("vecR0", [128, NVR])]
        self.s_ssd_d = dt_in("s_ssd", [1, NSAMP, 128, 2048])
        self.s_sconv_d = dt_in("s_sconv", [1, NSAMP, 128, 72])
        self.s_gla_d = dt_in("s_gla", [1, NSAMP, 1024, 512])
        self.s_fconv_d = dt_in("s_fconv", [1, NSAMP, 128, 176])
        self.s_kT_d = dt_in("s_kT", [1, NSAMP, D, 256])
        self.s_v_d = dt_in("s_v", [1, NSAMP, 256, D])
        self.yT = dt_out("yT", [D, NY])
        NO = 2 + 2 * NSAMP
        self.o_ssd = dt_out("o_ssd", [NO, 128, 2048])
        self.o_sconv = dt_out("o_sconv", [NO, 128, 72])
        self.o_gla = dt_out("o_gla", [NO, 1024, 512])
        self.o_fconv = dt_out("o_fconv", [NO, 128, 176])
        self.o_kT = dt_out("o_kT", [1, D, 256])
        self.o_v = dt_out("o_v", [1, 256, D])
        self.run_ssd = nc.dram_tensor("run_ssd", [128, 2048], F32).ap()
        self.run_gla = nc.dram_tensor("run_gla", [1024, 512], F32).ap()
        self.sendP = [nc.dram_tensor("sendP%d" % i, [128, 4096], F32).ap() for i in range(2)]
        self.gathP = [nc.dram_tensor("gathP%d" % i, [256, 4096], F32).ap() for i in range(2)]
        self.sendS = [nc.dram_tensor("sendS", [128, 1024], F32).ap()]
        self.gathS = [nc.dram_tensor("gathS", [256, 1024], F32).ap()]

        sb = lambda name, shape, dt: es.enter_context(nc.sbuf_tensor(name, shape, dt))
        self.x = sb("x", [128, 16, 512], F32)
        self.h = sb("h", [128, 16, 512], BF16)
        self.wsl = [sb("wsl0", [128, 8192], BF16), sb("wsl1", [128, 8192], BF16)]
        self.cst = sb("cstf", [128, 512], F32)
        self.identb = sb("identb", [128, 128], BF16)
        self.vecP = sb("vecP", [128, NVP], F32)
        self.vecR = sb("vecR", [128, NVR], F32)
        self.nba = sb("nba", [128, 8], F32)
        self.arow = sb("arow", [128, 32], F32)
        self.wsm = sb("wsm", [128, 16 * 48], BF16)
        self.wa2 = sb("wa2", [16, 1024], BF16)
        self.alrT = sb("alrT", [16, 512], BF16)
        self.ssdv = sb("ssdv", [128, 4, 6, 32], F32)
        self.cv_s = sb("cv_s", [128, DEPTH, 72], F32)
        self.cv_f = sb("cv_f", [128, DEPTH, 176], F32)
        self.hs = sb("hs", [128, NSAMP, 176], F32)
        self.ho = sb("ho", [128, NSAMP, 176], F32)
        self.rs = sb("rs", [128, 512], F32)
        self.small = sb("small", [128, 64], F32)
        self.flg = sb("flgs", [128, 8], F32)
        self.A = sb("arenaA", [128, 24576], BF16)
        self.Bn = sb("arenaB", [128, 12288], BF16)
        self.psum = [es.enter_context(nc.psum_tensor("ps%d" % i, [128, 512], F32)) for i in range(8)]

        self.wq = []
        for l in range(self.n_layers):
            for j in range(8):
                self.wq.append(("kv%d_%d" % (l, j), self.WKV[l][j], 8192))
        for t in range(self.n_tiles):
            for l in range(self.n_layers):
                for j in range(50 + 12):
                    self.wq.append(("L%d_%d" % (l, j), self.W[l][j], 8192))
                for j in range(8):
                    self.wq.append(("F%d_0_%d" % (l, j), self.WF[l][0, j], 24 * 256))
                for j in range(62, NSLAB):
                    self.wq.append(("L%d_%d" % (l, j), self.W[l][j], 8192))
                for j in range(8):
                    self.wq.append(("F%d_1_%d" % (l, j), self.WF[l][1, j][:, 0:20 * 256], 20 * 256))
        self.wi = 0
        self.wissued = 0

        self.dma("sp", [(self.cst[:], self.cst_d)], [], ["cst"], "cst")
        self.dma("pool", [(self.identb[:], self.cst_d[:, 0:128])], [], ["identb"], "identb")
        self.ident_f = self.cst[:, 0:128]
        self.U_f = self.cst[:, 128:256]
        self.SL_f = self.cst[:, 256:384]
        self.ones_f = self.cst[:, 384:512]
        self.MS(self.cv_s[:], 0.0, ["cv_s"])
        self.MS(self.cv_f[:], 0.0, ["cv_f"])

        self.dma("sp", [(self.flg[:], self.flg_d)], [], ["flg"], "flg")
        z = self.A[:, 0:16384].bitcast(F32)
        self.MS(z, 0.0, ["zt"])
        v3 = lambda ap: ap.rearrange("(c p) t -> p c t", p=128)
        self.dma("sp", [(self.run_ssd, z[:, 0:2048]),
                        (v3(self.run_gla), z[:, 0:4096].rearrange("p (c v) -> p c v", c=8)),
                        (self.gathP[0][0:128, :], z[:, 0:4096]), (self.gathP[1][0:128, :], z[:, 4096:8192]),
                        (self.gathS[0][0:128, :], z[:, 0:1024])],
                 ["zt"], ["run_ssd", "run_gla", "gathP", "gathS"], "zinit")
        self.barrier()
        self.prologue_kv()
        steps = []
        for st_ in range(NSTEP):
            if st_ < 2:
                steps.append(dict(kind="s", step=st_, TT=64, xcol=0, ycol=64 * st_, segs=[(16 * i, 16) for i in range(4)],
                                  nrun=4, RL=16))
            else:
                steps.append(dict(kind="p", step=st_, TT=512, xcol=64 + 512 * (st_ - 2), ycol=128 + 512 * (st_ - 2),
                                  segs=[(128 * i, 128) for i in range(4)], nrun=1, RL=512))
        for T in steps[: self.n_tiles]:
            self.run_tile(T)
        sp = self.engs["sp"]
        for en, e in self.engs.items():
            if e is not sp and e.cnt > 0:
                sp.h.wait_ge(e.sem, e.cnt)
        for sk, d in self.dsem.items():
            if d[1] > 0:
                (self.engs["pool"] if sk == "cc" else sp).h.wait_ge(d[0], d[1])
        return nc

    def prologue_kv(self):
        nc = self.nc
        mT = self.A[:, 0:16 * 256].rearrange("p (c m) -> p c m", c=16)
        stage = self.Bn[:, 0:4096].bitcast(F32).rearrange("p (a n) -> p a n", a=4)
        self.dma("pool", [(mT, self.memT.rearrange("(c p) m -> p c m", p=128))], [], ["mT"], "mT")
        for l in range(self.n_layers):
            for j in range(4):
                w, wk = self.slab("kv%d_%d" % (l, j))
                for m in range(4):
                    ps, pk = self.ps()
                    for kc in range(16):
                        self.MM(ps[:, 0:256], w[:, kc * NW + m * 128: kc * NW + m * 128 + 128], mT[:, kc, :], kc == 0, kc == 15,
                                [wk, "mT"], [pk])
                    self.CP(stage[:, m, 0:256], ps[:, 0:256], [pk], ["kvst"], en="act")
                self.dma("sp", [(self.o_kT[l, j * 512:(j + 1) * 512, :].rearrange("(m p) n -> p m n", p=128), stage[:, :, 0:256])],
                         ["kvst"], ["o_kT%d" % l], "o_kT%d" % l)
            for j in range(4):
                w, wk = self.slab("kv%d_%d" % (l, 4 + j))
                for mc in range(2):
                    ps, pk = self.ps()
                    for kc in range(16):
                        self.MM(ps[:, :], mT[:, kc, mc * 128:(mc + 1) * 128], w[:, kc * NW:(kc + 1) * NW], kc == 0, kc == 15,
                                [wk, "mT"], [pk])
                    self.CP(stage[:, mc, :], ps[:, :], [pk], ["kvst"], en="act")
                self.dma("sp", [(self.o_v[l, :, j * 512:(j + 1) * 512].rearrange("(m p) n -> p m n", p=128), stage[:, 0:2, :])],
                         ["kvst"], ["o_v%d" % l], "o_v%d" % l)
        self.barrier()

    def cc(self, send, gath, R, W):
        self._pre("pool", R, W)
        if "cc" not in self.dsem:
            self.dsem["cc"] = [self.es.enter_context(self.nc.semaphore("ccsem")), 0]
        d = self.dsem["cc"]
        self.nc.gpsimd.collective_compute("AllGather", ALU.bypass, replica_groups=self.rg, ins=[send], outs=[gath]).then_inc(d[0], 1)
        d[1] += 1
        self._post((d[0], d[1]), R, W)

    def run_tile(self, T):
        TT, stp = T["TT"], T["step"]
        samp = T["kind"] == "s"
        xk = ["x%d" % c for c in range(16)]
        v3 = lambda ap: ap.rearrange("(c p) t -> p c t", p=128)
        xsrc = v3(self.xT)[:, :, T["xcol"]:T["xcol"] + TT]
        self.dma("sp", [(self.x[:, 0:8, 0:TT], xsrc[:, 0:8, :]), (self.x[:, 8:16, 0:TT], xsrc[:, 8:16, :])], [], xk, "x")
        gath, gk = (self.gathS, "gathS") if samp else (self.gathP, "gathP")
        send = self.sendS if samp else self.sendP
        xb = self.A[:, 0:16384].bitcast(F32).rearrange("p (c t) -> p c t", c=16)
        ng = len(gath)
        cpg = 16 // ng
        self.dma("pool", [(xb[:, cpg * i:cpg * (i + 1), 0:TT], gath[i][0:128, :].rearrange("p (c t) -> p c t", c=cpg)) for i in range(ng)],
                 [gk], ["xb"], "xb")
        for c in range(16):
            self.STT(self.x[:, c, 0:TT], xb[:, c, 0:TT], self.flg[:, 0:1], self.x[:, c, 0:TT], ALU.mult, ALU.add,
                     ["xb", "flg", "x%d" % c], ["x%d" % c])
        if not samp:
            rf = self.flg[:, 1 + stp:2 + stp]
            self.TS(self.cv_s[:, 0, :], self.cv_s[:, 0, :], rf, None, ALU.mult, None, ["cv_s", "flg"], ["cv_s"])
            self.TS(self.cv_f[:, 0, :], self.cv_f[:, 0, :], rf, None, ALU.mult, None, ["cv_f", "flg"], ["cv_f"])
        self.barrier()
        self.layer(T, 0)
        if stp < NSTEP - 1:
            self.dma("sp", [(send[i].rearrange("p (c t) -> p c t", c=cpg), self.x[:, cpg * i:cpg * (i + 1), 0:TT]) for i in range(ng)],
                     xk, ["send"], "send")
            for i in range(ng):
                self.cc(send[i], gath[i], ["send"], [gk])
        self.rms(T, P_NFIN, out_f32=True)
        ydst = v3(self.yT)[:, :, T["ycol"]:T["ycol"] + TT]
        yv = self.A[:, 0:16384].bitcast(F32).rearrange("p (c t) -> p c t", c=16)
        self.dma("sp", [(ydst, yv[:, :, 0:TT])], ["yout"], [], "yout")
        self.barrier()

    def rms(self, T, g0, out_f32=False):
        TT = T["TT"]
        ps, pk = self.ps()
        for c in range(16):
            sq = self.Bn[:, 0:2048].bitcast(F32).rearrange("p (a t) -> p a t", a=2)[:, c % 2, 0:TT]
            self.ACT(sq, self.x[:, c, 0:TT], AF.Square, ["x%d" % c], ["sq%d" % (c % 2)])
            self.MM(ps[:, 0:TT], self.ones_f, sq, c == 0, c == 15, ["sq%d" % (c % 2), "cst"], [pk])
        rs = self.rs[:, 0:TT]
        self.TS(rs, ps[:, 0:TT], 1.0 / D, EPS, ALU.mult, ALU.add, [pk], ["rs"])
        self.RSQ(rs, "rs")
        if out_f32:
            yv = self.A[:, 0:16384].bitcast(F32).rearrange("p (c t) -> p c t", c=16)
        for c in range(16):
            o = yv[:, c, 0:TT] if out_f32 else self.h[:, c, 0:TT]
            self.STT(o, self.x[:, c, 0:TT], self.vecP[:, g0 + c:g0 + c + 1], rs, ALU.mult, ALU.mult,
                     ["x%d" % c, "rs", "vecP"], ["yout" if out_f32 else "h"])

    def layer(self, T, l):
        self.dma("sp", [(self.vecP[:], self.vecP_d[l]), (self.vecR[:], self.vecR_d[l])], [], ["vecP", "vecR"], "vec")
        self.dma("pool", [(self.wsm[:], self.wsm_d[l]), (self.wa2[:], self.wa2_d[l])], [], ["wsm", "wa2"], "wsmall")
        self.ACT(self.arow[:], self.vecR[:, R_ALOG:R_ALOG + 32], AF.Exp, ["vecR"], ["arow"])
        self.TS(self.arow[:], self.arow[:], -1.0, None, ALU.mult, None, ["arow"], ["arow"])
        self.TS(self.nba[:], self.vecP[:, P_BA:P_BA + 8], -1.0, None, ALU.mult, None, ["vecP"], ["nba"])
        self.rms(T, P_NMIX)
        self.mixer(T, l)
        self.barrier()
        self.rms(T, P_NMEM)
        self.attn(T, l)
        self.barrier()
        self.rms(T, P_NFFN)
        self.ffn(T, l)
        self.barrier()

    def fm(self, T, w, wk, rhs, rk, nk=16, nch=4, nw=NW):
        TT = T["TT"]
        for m in range(nch):
            ps, pk = self.ps()
            for kc in range(nk):
                self.MM(ps[:, 0:TT], w[:, kc * nw + m * 128: kc * nw + m * 128 + 128], rhs(kc), kc == 0, kc == nk - 1,
                        [wk] + rk, [pk])
            yield m, ps, pk

    def tm(self, T, w, wk):
        for si, (c0, L) in enumerate(T["segs"]):
            ps, pk = self.ps()
            for kc in range(16):
                self.MM(ps[0:L, :], self.h[:, kc, c0:c0 + L], w[:, kc * NW:(kc + 1) * NW], kc == 0, kc == 15, [wk, "h"], [pk])
            yield si, c0, L, ps, pk

    def conv(self, T, ps, pk, width, wcol0, wstride, bcol, oc, hist, hkey, newhist, nkey, raw, rawk, acc, acck):
        nrun, RL, TT = T["nrun"], T["RL"], T["TT"]
        H = width - 1
        rv = raw[:, 0:nrun * (H + RL)].rearrange("p (r t) -> p r t", r=nrun)
        av = acc[:, 0:TT].rearrange("p (r t) -> p r t", r=nrun)
        self.CP(rv[:, :, H:H + RL], ps[:, 0:TT].rearrange("p (r t) -> p r t", r=nrun), [pk], [rawk], en="act")
        self.CP(rv[:, :, 0:H], hist, [hkey], [rawk], en="dve")
        wc = lambda tap: self.vecP[:, wcol0 + tap * wstride + oc: wcol0 + tap * wstride + oc + 1]
        self.TS(av, rv[:, :, H:H + RL], wc(width - 1), self.vecP[:, bcol + oc:bcol + oc + 1], ALU.mult, ALU.add,
                [rawk, "vecP"], [acck])
        for tap in range(width - 1):
            self.STT(av, rv[:, :, tap:tap + RL], wc(tap), av, ALU.mult, ALU.add, [rawk, "vecP", acck], [acck])
        self.CP(newhist, rv[:, :, RL:RL + H], [rawk], [nkey], en="dve")
        return av

    def mixer(self, T, l):
        nc = self.nc
        TT, segs, nrun, RL = T["TT"], T["segs"], T["nrun"], T["RL"]
        nseg = len(segs)
        prompt = T["kind"] == "p"
        A, Bn = self.A, self.Bn
        ysnT = A[:, 0:8192].rearrange("p (c t) -> p c t", c=16)
        ygnT = A[:, 8192:16384].rearrange("p (c t) -> p c t", c=16)
        mg = A[:, 16384:24576].rearrange("p (c t) -> p c t", c=16)
        pools = [[Bn, 0, 12288], [A, 16384, 24576]]

        def carve(n):
            for pl in pools:
                if pl[1] + n <= pl[2]:
                    a = pl[0][:, pl[1]:pl[1] + n]
                    pl[1] += n
                    return a
            raise AssertionError("arena overflow")
        xsT = carve(4 * TT).rearrange("p (c t) -> p c t", c=4)
        bcT = carve(4 * TT).rearrange("p (c t) -> p c t", c=4)
        sz = carve(nseg * 512).rearrange("p (s n) -> p s n", s=nseg)
        raw = [carve(1040).bitcast(F32) for _ in range(2)]
        cacc = [carve(2 * TT).bitcast(F32) for _ in range(2)]
        xs_tok = carve(512)
        B_tok = carve(128)
        rhsA = carve(2048).bitcast(F32).rearrange("p (r i) -> p r i", r=8)
        Eg = rhsA
        Wt = carve(1024).rearrange("p (r i) -> p r i", r=8)
        cbm = carve(256).bitcast(F32)
        t1 = carve(1024).bitcast(F32)
        t2 = carve(1024).bitcast(F32)
        yb = carve(1024).bitcast(F32)
        xw = carve(512)
        Sg = [carve(1024).bitcast(F32) for _ in range(1 if prompt else NSAMP)]
        Sgb1 = carve(512)
        ytok = carve(512)
        vR, vP = self.vecR, self.vecP
        sv = self.ssdv

        if not prompt:
            self.dma("sp", [(self.hs[:, :, 0:72], self.s_sconv_d[l].rearrange("s p n -> p s n"))], [], ["hs"], "hs")

        wsm = self.wsm
        ps, pk = self.ps()
        for kc in range(16):
            self.MM(ps[0:16, 0:TT], wsm[:, kc * 48 + 32: kc * 48 + 48], self.h[:, kc, 0:TT], kc == 0, kc == 15, ["wsm", "h"], [pk])
        self.CP(self.alrT[:, 0:TT], ps[0:16, 0:TT], [pk], ["alrT"], en="act")
        for si, (c0, L) in enumerate(segs):
            ps, pk = self.ps()
            for kc in range(16):
                self.MM(ps[0:L, 0:32], self.h[:, kc, c0:c0 + L], wsm[:, kc * 48: kc * 48 + 32], kc == 0, kc == 15, ["wsm", "h"], [pk])
            dt, dA, acs, ea, te, cd = [sv[:, si, q, :] for q in range(6)]
            k = "sv%d" % si
            self.TT(dt[0:L], ps[0:L, 0:32], vR[0:L, R_DTB:R_DTB + 32], ALU.add, [pk, "vecR"], [k])
            self.ACT(dt[0:L], dt[0:L], AF.Exp, [k], [k])
            self.ACT(dt[0:L], dt[0:L], AF.Ln, [k], [k], bias=1.0)
            self.TT(dA[0:L], dt[0:L], self.arow[0:L], ALU.mult, [k, "arow"], [k])
            ps2, pk2 = self.ps()
            self.MM(ps2[0:L, 0:32], self.U_f[0:L, 0:L], dA[0:L], True, True, [k, "cst"], [pk2])
            self.MM(ps2[:, 32:64], self.ones_f[0:L, :], dA[0:L], True, True, [k, "cst"], [pk2])
            self.CP(acs[0:L], ps2[0:L, 0:32], [pk2], [k], en="act")
            self.ACT(ea[0:L], ps2[0:L, 0:32], AF.Exp, [pk2], [k])
            self.ACT(cd, ps2[:, 32:64], AF.Exp, [pk2], [k])
            self.TT(te[0:L], ps2[0:L, 32:64], acs[0:L], ALU.subtract, [pk2, k], [k])
            self.ACT(te[0:L], te[0:L], AF.Exp, [k], [k])
            self.TT(te[0:L], te[0:L], dt[0:L], ALU.mult, [k], [k])

        xs_ch, bc_ch = xbc_slab_chunks()
        sidx = [0]

        def nslab():
            w, wk = self.slab("L%d_%d" % (l, sidx[0]))
            sidx[0] += 1
            return w, wk
        hfn = lambda kc: self.h[:, kc, 0:TT]
        hist_s = (lambda oc: self.cv_s[:, l, oc * 3:oc * 3 + 3].unsqueeze(1)) if prompt else \
                 (lambda oc: self.hs[:, :, oc * 3:oc * 3 + 3])
        nh_s = (lambda oc: self.cv_s[:, l, oc * 3:oc * 3 + 3].unsqueeze(1)) if prompt else \
               (lambda oc: self.ho[:, :, oc * 3:oc * 3 + 3])
        hk, nk_ = ("cv_s", "cv_s") if prompt else ("hs", "ho")
        cnt = [0]

        def xbc_slab(chs, dst, dkey):
            w, wk = nslab()
            for m, ps, pk in self.fm(T, w, wk, hfn, ["h"]):
                oc = chs[m]
                b = cnt[0] % 2
                cnt[0] += 1
                av = self.conv(T, ps, pk, 4, P_SCW, 24, P_SCB, oc, hist_s(oc), hk, nh_s(oc), nk_,
                               raw[b], "raw%d" % b, cacc[b], "cacc%d" % b)
                self.ACT(dst[:, m, 0:TT].rearrange("p (r t) -> p r t", r=nrun), av, AF.Silu, ["cacc%d" % b], [dkey])

        for g in range(4):
            xbc_slab(xs_ch[g], xsT, "xsT")
            if g % 2 == 0:
                xbc_slab(bc_ch[g // 2], bcT, "bcT")
            BT = bcT[:, 2 * (g % 2), :]
            CT = bcT[:, 2 * (g % 2) + 1, :]
            w, wk = nslab()
            for si, c0, L, ps, pk in self.tm(T, w, wk):
                self.ACT(sz[0:L, si, :], ps[0:L, :], AF.Silu, [pk], ["sz"])
            if prompt:
                self.dma("sp", [(Sg[0], self.run_ssd[:, g * 512:(g + 1) * 512])], ["run_ssd"], ["Sg0"], "Sg0")
                self.TS(Sg[0], Sg[0], self.flg[:, 1 + T["step"]:2 + T["step"]], None, ALU.mult, None, ["Sg0", "flg"], ["Sg0"])
            else:
                for s in range(NSAMP):
                    self.dma("sp", [(Sg[s], self.s_ssd_d[l, s, :, g * 512:(g + 1) * 512])], [], ["Sg%d" % s], "Sg%d" % s)
            for si, (c0, L) in enumerate(segs):
                sq_ = 0 if prompt else si
                S, Sb, Sk = Sg[sq_], Sgb1, "Sg%d" % sq_
                dt, dA, acs, ea, te, cd = [sv[:, si, q, :] for q in range(6)]
                k = "sv%d" % si
                h0 = 8 * g
                self.CP(Sb, S, [Sk], ["Sgbb"], en="act")
                ps, pk = self.ps()
                for m in range(4):
                    self.MM(ps[0:L, m * 128:(m + 1) * 128], xsT[:, m, c0:c0 + L], self.identb[:], True, True, ["xsT", "identb"], [pk])
                self.CP(xs_tok[0:L, :], ps[0:L, :], [pk], ["xs_tok"], en="act")
                ps, pk = self.ps()
                self.MM(ps[0:L, 0:128], BT[:, c0:c0 + L], self.identb[:], True, True, ["bcT", "identb"], [pk])
                self.CP(B_tok[0:L, :], ps[0:L, 0:128], [pk], ["B_tok"], en="act")
                ps, pk = self.ps()
                self.MM(ps[0:L, 0:L], BT[:, c0:c0 + L], CT[:, c0:c0 + L], True, True, ["bcT"], [pk])
                self.TT(cbm[0:L, 0:L], ps[0:L, 0:L], self.U_f[0:L, 0:L], ALU.mult, [pk, "cst"], ["cbm"])
                self.TT(rhsA[0:L, :, 0:L], dA[0:L, h0:h0 + 8].unsqueeze(2).to_broadcast([L, 8, L]),
                        self.U_f[0:L, 0:L].unsqueeze(1).to_broadcast([L, 8, L]), ALU.mult, [k, "cst"], ["rhsA0", "rhsA1"])
                for half in range(2):
                    ps, pk = self.ps()
                    for r in range(4):
                        self.MM(ps[0:L, r * L:(r + 1) * L], self.SL_f[0:L, 0:L], rhsA[0:L, half * 4 + r, 0:L], True, True,
                                ["rhsA%d" % half, "cst"], [pk])
                    self.ACT(Eg[0:L, half * 4:half * 4 + 4, 0:L], ps[0:L, 0:4 * L].rearrange("p (r i) -> p r i", r=4), AF.Exp,
                             [pk], ["rhsA%d" % half])
                self.TT(Eg[0:L, :, 0:L], Eg[0:L, :, 0:L], cbm[0:L, 0:L].unsqueeze(1).to_broadcast([L, 8, L]), ALU.mult,
                        ["rhsA0", "rhsA1", "cbm"], ["rhsA0", "rhsA1"])
                self.TT(Wt[0:L, :, 0:L], Eg[0:L, :, 0:L], dt[0:L, h0:h0 + 8].unsqueeze(2).to_broadcast([L, 8, L]), ALU.mult,
                        ["rhsA0", "rhsA1", k], ["Wt"])
                psA, pkA = self.ps()
                for r in range(8):
                    self.MM(psA[0:L, r * 64:(r + 1) * 64], Wt[0:L, r, 0:L], xs_tok[0:L, r * 64:(r + 1) * 64], True, True,
                            ["Wt", "xs_tok"], [pkA])
                psB, pkB = self.ps()
                self.MM(psB[0:L, :], CT[:, c0:c0 + L], Sb, True, True, ["bcT", "Sgbb"], [pkB])
                self.TT(t1[0:L, :].rearrange("p (r q) -> p r q", r=8), psB[0:L, :].rearrange("p (r q) -> p r q", r=8),
                        ea[0:L, h0:h0 + 8].unsqueeze(2).to_broadcast([L, 8, 64]), ALU.mult, [pkB, k], ["t1"])
                self.TT(t2[0:L, :].rearrange("p (r q) -> p r q", r=8), xs_tok[0:L, :].rearrange("p (r q) -> p r q", r=8),
                        vR[0:L, R_DSK + h0:R_DSK + h0 + 8].unsqueeze(2).to_broadcast([L, 8, 64]), ALU.mult,
                        ["xs_tok", "vecR"], ["t2"])
                self.TT(yb[0:L, :], psA[0:L, :], t1[0:L, :], ALU.add, [pkA, "t1"], ["yb"])
                self.TT(yb[0:L, :], yb[0:L, :], t2[0:L, :], ALU.add, ["yb", "t2"], ["yb"])
                self.TT(xw[0:L, :].rearrange("p (r q) -> p r q", r=8), xs_tok[0:L, :].rearrange("p (r q) -> p r q", r=8),
                        te[0:L, h0:h0 + 8].unsqueeze(2).to_broadcast([L, 8, 64]), ALU.mult, ["xs_tok", k], ["xw"])
                psC, pkC = self.ps()
                self.MM(psC[:, :], B_tok[0:L, :], xw[0:L, :], True, True, ["B_tok", "xw"], [pkC])
                self.TT(S.rearrange("p (r q) -> p r q", r=8), S.rearrange("p (r q) -> p r q", r=8),
                        cd[:, h0:h0 + 8].unsqueeze(2).to_broadcast([128, 8, 64]), ALU.mult, [Sk, k], [Sk])
                self.TT(S, S, psC[:, :], ALU.add, [Sk, pkC], [Sk])
                self.TT(yb[0:L, :], yb[0:L, :], sz[0:L, si, :], ALU.mult, ["yb", "sz"], ["yb"])
                ss = self.small[:, 0:1]
                self.ACT(t1[0:L, :], yb[0:L, :], AF.Square, ["yb", "t1"], ["t1", "ss"], accum=ss[0:L])
                self.TS(ss[0:L], ss[0:L], 1.0 / 512, EPS, ALU.mult, ALU.add, ["ss"], ["ss"])
                self.RSQ(ss[0:L], "ss")
                self.STT(ytok[0:L, :], yb[0:L, :], ss[0:L], vR[0:L, R_SNORM + g * 512:R_SNORM + (g + 1) * 512], ALU.mult, ALU.mult,
                         ["yb", "ss", "vecR"], ["ytok"])
                ps, pk = self.ps()
                for m in range(4):
                    self.MM(ps[:, m * L:(m + 1) * L], ytok[0:L, m * 128:(m + 1) * 128], self.identb[0:L, 0:L], True, True,
                            ["ytok", "identb"], [pk])
                self.CP(ysnT[:, 4 * g:4 * g + 4, c0:c0 + L], ps[:, 0:4 * L].rearrange("p (m i) -> p m i", m=4), [pk], ["ysnT"], en="act")
            if prompt:
                prs = [(self.run_ssd[:, g * 512:(g + 1) * 512], Sg[0])]
                if T["step"] >= 5:
                    prs.append((self.o_ssd[T["step"] - 5, :, g * 512:(g + 1) * 512], Sg[0]))
                self.dma("sp", prs, ["Sg0"], ["run_ssd"], "run_ssd")
            else:
                for s in range(NSAMP):
                    self.dma("sp", [(self.o_ssd[2 + NSAMP * T["step"] + s, :, g * 512:(g + 1) * 512], Sg[s])], ["Sg%d" % s], [], "o_ssds")
        if prompt:
            if T["step"] >= 5:
                self.dma("sp", [(self.o_sconv[T["step"] - 5], self.cv_s[:, l, :])], ["cv_s"], [], "o_cv")
        else:
            self.dma("sp", [(self.o_sconv[2 + NSAMP * T["step"]:2 + NSAMP * T["step"] + NSAMP].rearrange("s p n -> p s n"), self.ho[:, :, 0:72])], ["ho"], [], "o_cv")
        self.barrier()
        self.gla(T, l, nslab, ygnT)
        self.barrier()
        self.merge(T, l, nslab, ysnT, ygnT, mg)

    def gla(self, T, l, nslab, ygnT):
        TT, segs = T["TT"], T["segs"]
        prompt = T["kind"] == "p"
        nseg = len(segs)
        Bn = self.Bn
        A = self.A
        pools = [[Bn, 0, 12288], [A, 16384, 24576]]

        def carve(n):
            for pl in pools:
                if pl[1] + n <= pl[2]:
                    a = pl[0][:, pl[1]:pl[1] + n]
                    pl[1] += n
                    return a
            raise AssertionError("arena overflow")
        cum = carve(4 * TT).bitcast(F32).rearrange("p (c t) -> p c t", c=2)
        ex = [carve(2 * TT).bitcast(F32) for _ in range(2)]
        qd = carve(2 * TT).rearrange("p (c t) -> p c t", c=2)
        ki = carve(2 * TT).rearrange("p (c t) -> p c t", c=2)
        ke = carve(2 * TT).rearrange("p (c t) -> p c t", c=2)
        vt = carve(nseg * 512).rearrange("p (s n) -> p s n", s=nseg)
        sg = carve(nseg * 512).rearrange("p (s n) -> p s n", s=nseg)
        ketok = carve(256)
        att = carve(128)
        ob = carve(1024).bitcast(F32)
        osq = carve(1024).bitcast(F32)
        otok = carve(512)
        ns = 1 if prompt else NSAMP
        S = [carve(2048).bitcast(F32).rearrange("p (c v) -> p c v", c=2) for _ in range(ns)]
        Sb1 = carve(1024).rearrange("p (c v) -> p c v", c=2)
        dec = carve(64).bitcast(F32)
        zeros = carve(256).bitcast(F32)
        vR = self.vecR
        self.MS(zeros, 0.0, ["zeros"])
        for hd in range(4):
            for cc in range(2):
                ch = 2 * hd + cc
                ps, pk = self.ps()
                self.MM(ps[:, 0:TT], self.wa2[:, ch * 128:(ch + 1) * 128], self.alrT[:, 0:TT], True, True, ["wa2", "alrT"], [pk])
                e = ex[cc][:, 0:TT]
                self.ACT(e, ps[:, 0:TT], AF.Exp, [pk, "nba"], ["ex%d" % cc], bias=self.nba[:, ch:ch + 1], scale=-1.0)
                self.ACT(e, e, AF.Ln, ["ex%d" % cc], ["ex%d" % cc], bias=1.0)
                for si, (c0, L) in enumerate(segs):
                    self._pre("dve", ["ex%d" % cc, "zeros"], ["cum"])
                    i = self.nc.vector.tensor_tensor_scan(out=cum[:, cc, c0:c0 + L], data0=e[:, c0:c0 + L], data1=zeros[:, 0:L],
                                                          initial=0.0, op0=ALU.add, op1=ALU.add)
                    self.op("dve", i, ["ex%d" % cc, "zeros"], ["cum"])
                    self.ACT(dec[:, cc * nseg + si: cc * nseg + si + 1], cum[:, cc, c0 + L - 1:c0 + L], AF.Exp, ["cum"], ["dec"],
                             scale=-1.0 / 16)
            w, wk = nslab()
            for m, ps, pk in self.fm(T, w, wk, lambda kc: self.h[:, kc, 0:TT], ["h"]):
                cc = m % 2
                e = ex[m % 2][:, 0:TT]
                if m < 2:
                    self.ACT(e, cum[:, cc, 0:TT], AF.Exp, ["cum"], ["ex%d" % (m % 2)], scale=-1.0 / 16)
                    self.STT(qd[:, cc, 0:TT], ps[:, 0:TT], 1.0 / 16, e, ALU.mult, ALU.mult, [pk, "ex%d" % (m % 2)], ["qd"])
                else:
                    self.ACT(e, cum[:, cc, 0:TT], AF.Exp, ["cum"], ["ex%d" % (m % 2)], scale=1.0 / 16)
                    self.TT(ki[:, cc, 0:TT], ps[:, 0:TT], e, ALU.mult, [pk, "ex%d" % (m % 2)], ["ki"])
                    for si, (c0, L) in enumerate(segs):
                        self.TS(ke[:, cc, c0:c0 + L], ki[:, cc, c0:c0 + L], dec[:, cc * nseg + si: cc * nseg + si + 1], None,
                                ALU.mult, None, ["ki", "dec"], ["ke"])
            w, wk = nslab()
            for si, c0, L, ps, pk in self.tm(T, w, wk):
                self.CP(vt[0:L, si, :], ps[0:L, :], [pk], ["vt"], en="act")
            w, wk = nslab()
            for si, c0, L, ps, pk in self.tm(T, w, wk):
                self.ACT(sg[0:L, si, :], ps[0:L, :], AF.Silu, [pk], ["sg"])
            gsrc = lambda ap: ap.rearrange("(c p) v -> p c v", p=128)[:, 2 * hd:2 * hd + 2, :]
            if prompt:
                self.dma("sp", [(S[0], gsrc(self.run_gla))], ["run_gla"], ["S0"], "S0")
                self.TS(S[0], S[0], self.flg[:, 1 + T["step"]:2 + T["step"]], None, ALU.mult, None, ["S0", "flg"], ["S0"])
            else:
                for s in range(NSAMP):
                    self.dma("sp", [(S[s], gsrc(self.s_gla_d[l, s]))], [], ["S%d" % s], "S%d" % s)
            for si, (c0, L) in enumerate(segs):
                sq_ = 0 if prompt else si
                St, Sbt, Sk = S[sq_], Sb1, "S%d" % sq_
                self.CP(Sbt, St, [Sk], ["Sbb"], en="act")
                ps, pk = self.ps()
                for cc in range(2):
                    self.MM(ps[0:L, cc * 128:(cc + 1) * 128], ke[:, cc, c0:c0 + L], self.identb[:], True, True, ["ke", "identb"], [pk])
                self.CP(ketok[0:L, :], ps[0:L, 0:256], [pk], ["ketok"], en="act")
                ps, pk = self.ps()
                for cc in range(2):
                    self.MM(ps[0:L, 0:L], ki[:, cc, c0:c0 + L], qd[:, cc, c0:c0 + L], cc == 0, cc == 1, ["ki", "qd"], [pk])
                self.TT(att[0:L, 0:L], ps[0:L, 0:L], self.U_f[0:L, 0:L], ALU.mult, [pk, "cst"], ["att"])
                psO, pkO = self.ps()
                self.MM(psO[0:L, :], att[0:L, 0:L], vt[0:L, si, :], True, False, ["att", "vt"], [pkO])
                for cc in range(2):
                    self.MM(psO[0:L, :], qd[:, cc, c0:c0 + L], Sbt[:, cc, :], False, cc == 1, ["qd", "Sbb"], [pkO])
                for cc in range(2):
                    psS, pkS = self.ps()
                    self.MM(psS[:, :], ketok[0:L, cc * 128:(cc + 1) * 128], vt[0:L, si, :], True, True, ["ketok", "vt"], [pkS])
                    self.STT(St[:, cc, :], St[:, cc, :], dec[:, cc * nseg + si: cc * nseg + si + 1], psS[:, :], ALU.mult, ALU.add,
                             [Sk, "dec", pkS], [Sk])
                ss = self.small[:, 1:2]
                self.ACT(osq[0:L, :], psO[0:L, :], AF.Square, [pkO], ["osq", "ss2"], accum=ss[0:L])
                self.TS(ss[0:L], ss[0:L], 1.0 / 512, EPS, ALU.mult, ALU.add, ["ss2"], ["ss2"])
                self.RSQ(ss[0:L], "ss2")
                self.STT(ob[0:L, :], psO[0:L, :], ss[0:L], vR[0:L, R_GNORM:R_GNORM + 512], ALU.mult, ALU.mult,
                         [pkO, "ss2", "vecR"], ["ob"])
                self.TT(otok[0:L, :], ob[0:L, :], sg[0:L, si, :], ALU.mult, ["ob", "sg"], ["otok"])
                ps, pk = self.ps()
                for m in range(4):
                    self.MM(ps[:, m * L:(m + 1) * L], otok[0:L, m * 128:(m + 1) * 128], self.identb[0:L, 0:L], True, True,
                            ["otok", "identb"], [pk])
                self.CP(ygnT[:, 4 * hd:4 * hd + 4, c0:c0 + L], ps[:, 0:4 * L].rearrange("p (m i) -> p m i", m=4), [pk], ["ygnT"], en="act")
            gdst = lambda ap: ap.rearrange("(c p) v -> p c v", p=128)[:, 2 * hd:2 * hd + 2, :]
            if prompt:
                prs = [(gdst(self.run_gla), S[0])]
                if T["step"] >= 5:
                    prs.append((gdst(self.o_gla[T["step"] - 5]), S[0]))
                self.dma("sp", prs, ["S0"], ["run_gla"], "run_gla")
            else:
                for s in range(NSAMP):
                    self.dma("sp", [(gdst(self.o_gla[2 + NSAMP * T["step"] + s]), S[s])], ["S%d" % s], [], "o_glas")

    def merge(self, T, l, nslab, ysnT, ygnT, mg):
        TT = T["TT"]
        Bn = self.Bn
        sgt = Bn[:, 0:2048].rearrange("p (c t) -> p c t", c=4)
        acc = Bn[:, 2048:2048 + 4096].bitcast(F32).rearrange("p (c t) -> p c t", c=4)
        hfn = lambda kc: self.h[:, kc, 0:TT]
        for j in range(4):
            for n, src, sk in ((0, ysnT, "ysnT"), (1, ygnT, "ygnT")):
                w, wk = nslab()
                for m, ps, pk in self.fm(T, w, wk, hfn, ["h"]):
                    self.ACT(sgt[:, m, 0:TT], ps[:, 0:TT], AF.Sigmoid, [pk], ["sgt%d" % m])
                w, wk = nslab()
                for m, ps, pk in self.fm(T, w, wk, lambda kc: src[:, kc, 0:TT], [sk]):
                    if n == 0:
                        self.TT(acc[:, m, 0:TT], ps[:, 0:TT], sgt[:, m, 0:TT], ALU.mult, [pk, "sgt%d" % m], ["acc%d" % m])
                    else:
                        self.TT(self.rs[:, 0:TT], ps[:, 0:TT], sgt[:, m, 0:TT], ALU.mult, [pk, "sgt%d" % m], ["rs"])
                        self.TT(mg[:, 4 * j + m, 0:TT], self.rs[:, 0:TT], acc[:, m, 0:TT], ALU.add, ["rs", "acc%d" % m], ["mg"])
        for j in range(4):
            w, wk = nslab()
            for m, ps, pk in self.fm(T, w, wk, lambda kc: mg[:, kc, 0:TT], ["mg"]):
                c = 4 * j + m
                self.TT(self.x[:, c, 0:TT], self.x[:, c, 0:TT], ps[:, 0:TT], ALU.add, [pk, "x%d" % c], ["x%d" % c])
        self.sidx_after_mixer = None

    def attn(self, T, l):
        TT, segs = T["TT"], T["segs"]
        prompt = T["kind"] == "p"
        A, Bn = self.A, self.Bn
        qT = A[:, 0:8192].rearrange("p (c t) -> p c t", c=16)
        oT = A[:, 8192:16384].rearrange("p (c t) -> p c t", c=16)
        ngrp = 1 if prompt else NSAMP
        kvreg = [Bn[:, 0:8192], A[:, 16384:24576]]
        kTb = [kvreg[g % 2][:, 0:4096].rearrange("p (c m) -> p c m", c=16) for g in range(ngrp)]
        vb = [kvreg[g % 2][:, 4096:8192].rearrange("p (c n) -> p c n", c=2) for g in range(ngrp)]
        base = 8192
        sc = Bn[:, base:base + 512].bitcast(F32)
        pb = Bn[:, base + 512:base + 768]
        pT = Bn[:, base + 768:base + 768 + 1024].rearrange("p (c t) -> p c t", c=2)
        sm = self.small

        def load_kv(g):
            if prompt:
                ksrc, vsrc, rk = self.o_kT[l], self.o_v[l], ["o_kT%d" % l, "o_v%d" % l]
            else:
                ksrc, vsrc, rk = self.s_kT_d[l, g], self.s_v_d[l, g], []
            self.dma("pool", [(kTb[g], ksrc.rearrange("(c p) m -> p c m", p=128)),
                              (vb[g], vsrc.rearrange("(c p) n -> p c n", p=128))], rk, ["kv%d" % (g % 2), "sq0", "sq1"], "kvl%d" % (g % 2))
        load_kv(0)
        sidx = [42]

        def nslab():
            w, wk = self.slab("L%d_%d" % (l, sidx[0]))
            sidx[0] += 1
            return w, wk
        for j in range(4):
            w, wk = nslab()
            for m, ps, pk in self.fm(T, w, wk, lambda kc: self.h[:, kc, 0:TT], ["h"]):
                self.ACT(qT[:, 4 * j + m, 0:TT], ps[:, 0:TT], AF.Copy, [pk], ["qT"], scale=float(512 ** -0.5))
        groups = [(0, segs)] if prompt else [(s, [segs[s]]) for s in range(NSAMP)]
        for g, gsegs in groups:
            if g + 1 < ngrp:
                load_kv(g + 1)
            g0 = gsegs[0][0]
            gl = sum(L for _, L in gsegs)
            for hd in range(4):
                for (c0, L) in gsegs:
                    ps, pk = self.ps()
                    for cc in range(4):
                        self.MM(ps[0:L, 0:256], qT[:, 4 * hd + cc, c0:c0 + L], kTb[g][:, 4 * hd + cc, :], cc == 0, cc == 3,
                                ["qT", "kv%d" % (g % 2)], [pk])
                    mx = sm[:, 2:3]
                    self._pre("dve", [pk], ["mx"])
                    i = self.nc.vector.reduce_max(out=mx[0:L], in_=ps[0:L, 0:256], axis=AX.X)
                    self.op("dve", i, [pk], ["mx"])
                    self.TS(mx[0:L], mx[0:L], -1.0, None, ALU.mult, None, ["mx"], ["mx"])
                    sme = sm[:, 3:4]
                    self.ACT(sc[0:L, :], ps[0:L, 0:256], AF.Exp, [pk, "mx"], ["sc", "sme"], bias=mx[0:L], accum=sme[0:L])
                    self._pre("dve", ["sme"], ["sme"])
                    i = self.nc.vector.reciprocal(out=sme[0:L], in_=sme[0:L])
                    self.op("dve", i, ["sme"], ["sme"])
                    self.TS(pb[0:L, :], sc[0:L, :], sme[0:L], None, ALU.mult, None, ["sc", "sme"], ["pb"])
                    ps2, pk2 = self.ps()
                    for mc in range(2):
                        self.MM(ps2[:, mc * L:(mc + 1) * L], pb[0:L, mc * 128:(mc + 1) * 128], self.identb[0:L, 0:L], True, True,
                                ["pb", "identb"], [pk2])
                    self.CP(pT[:, :, c0 - g0:c0 - g0 + L], ps2[:, 0:2 * L].rearrange("p (c i) -> p c i", c=2), [pk2], ["pT"], en="act")
                for cc in range(4):
                    ps, pk = self.ps()
                    for mc in range(2):
                        self.MM(ps[:, 0:gl], vb[g][:, mc, (4 * hd + cc) * 128:(4 * hd + cc + 1) * 128], pT[:, mc, 0:gl], mc == 0, mc == 1,
                                ["kv%d" % (g % 2), "pT"], [pk])
                    self.CP(oT[:, 4 * hd + cc, g0:g0 + gl], ps[:, 0:gl], [pk], ["oT"], en="act")
        for j in range(4):
            w, wk = nslab()
            for m, ps, pk in self.fm(T, w, wk, lambda kc: oT[:, kc, 0:TT], ["oT"]):
                c = 4 * j + m
                self.TT(self.x[:, c, 0:TT], self.x[:, c, 0:TT], ps[:, 0:TT], ALU.add, [pk, "x%d" % c], ["x%d" % c])

    def ffn(self, T, l):
        TT, nrun, RL = T["TT"], T["nrun"], T["RL"]
        prompt = T["kind"] == "p"
        A, Bn = self.A, self.Bn
        act = A[:, 0:24 * 512].rearrange("p (c t) -> p c t", c=24)
        raw = [Bn[:, i * 1040:(i + 1) * 1040].bitcast(F32) for i in range(2)]
        cacc = [Bn[:, 2080 + i * 1024:2080 + (i + 1) * 1024].bitcast(F32) for i in range(2)]
        cu = Bn[:, 4128:4128 + 4096].bitcast(F32).rearrange("p (c t) -> p c t", c=4)
        if not prompt:
            self.dma("sp", [(self.hs[:, :, :], self.s_fconv_d[l].rearrange("s p n -> p s n"))], [], ["hs"], "hs")
        hist = (lambda oc: self.cv_f[:, l, oc * 2:oc * 2 + 2].unsqueeze(1)) if prompt else (lambda oc: self.hs[:, :, oc * 2:oc * 2 + 2])
        nh = (lambda oc: self.cv_f[:, l, oc * 2:oc * 2 + 2].unsqueeze(1)) if prompt else (lambda oc: self.ho[:, :, oc * 2:oc * 2 + 2])
        hk, nk_ = ("cv_f", "cv_f") if prompt else ("hs", "ho")
        sidx = [50]
        cnt = 0
        hfn = lambda kc: self.h[:, kc, 0:TT]
        for hf, (j0, j1) in enumerate(((0, 6), (6, 11))):
            nkc = 4 * (j1 - j0)
            for j in range(j0, j1):
                for part in range(2):
                    w, wk = self.slab("L%d_%d" % (l, sidx[0]))
                    sidx[0] += 1
                    for m, ps, pk in self.fm(T, w, wk, hfn, ["h"]):
                        oc = part * 44 + 4 * j + m
                        b = cnt % 2
                        cnt += 1
                        av = self.conv(T, ps, pk, 3, P_FCW, 88, P_FCB, oc, hist(oc), hk, nh(oc), nk_,
                                       raw[b], "raw%d" % b, cacc[b], "cacc%d" % b)
                        cuv = cu[:, m, 0:TT].rearrange("p (r t) -> p r t", r=nrun)
                        if part == 0:
                            self.CP(cuv, av, ["cacc%d" % b], ["cu%d" % m], en="act")
                        else:
                            self.ACT(av, av, AF.Silu, ["cacc%d" % b], ["cacc%d" % b])
                            self.TT(act[:, 4 * (j - j0) + m, 0:TT].rearrange("p (r t) -> p r t", r=nrun), av, cuv, ALU.mult,
                                    ["cacc%d" % b, "cu%d" % m], ["act"])
            for j in range(8):
                w, wk = self.slab("F%d_%d_%d" % (l, hf, j))
                for m, ps, pk in self.fm(T, w, wk, lambda kc: act[:, kc, 0:TT], ["act"], nk=nkc, nch=2, nw=256):
                    c = 2 * j + m
                    self.TT(self.x[:, c, 0:TT], self.x[:, c, 0:TT], ps[:, 0:TT], ALU.add, [pk, "x%d" % c], ["x%d" % c])
        if prompt:
            if T["step"] >= 5:
                self.dma("sp", [(self.o_fconv[T["step"] - 5], self.cv_f[:, l, :])], ["cv_f"], [], "o_cv")
        else:
            self.dma("sp", [(self.o_fconv[2 + NSAMP * T["step"]:2 + NSAMP * T["step"] + NSAMP].rearrange("s p n -> p s n"), self.ho[:, :, :])], ["ho"], [], "o_cv")


def _slab(wcols):
    n = wcols.shape[1]
    return np.ascontiguousarray(wcols.reshape(16, 128, n).transpose(1, 0, 2).reshape(128, 16 * n))


def _layer_slabs(w_in, w_branch, w_out, w_mq, w_mo, w_ffn_in):
    xs_ch, bc_ch = xbc_slab_chunks()
    xbc = w_in[:, O_XBC:O_XBC + 3072]
    sl = []
    pick = lambda chs: np.concatenate([xbc[:, c * 128:(c + 1) * 128] for c in chs], axis=1)
    for g in range(4):
        sl.append(pick(xs_ch[g]))
        if g % 2 == 0:
            sl.append(pick(bc_ch[g // 2]))
        sl.append(w_in[:, O_Z + g * 512:O_Z + (g + 1) * 512])
    for hd in range(4):
        sl.append(np.concatenate([w_in[:, O_Q + hd * 256:O_Q + (hd + 1) * 256], w_in[:, O_K + hd * 256:O_K + (hd + 1) * 256]], axis=1))
        sl.append(w_in[:, O_V + hd * 512:O_V + (hd + 1) * 512])
        sl.append(w_in[:, O_G + hd * 512:O_G + (hd + 1) * 512])
    for j in range(4):
        for n in range(2):
            sl.append(w_in[:, O_GATES + n * 2048 + j * 512:O_GATES + n * 2048 + (j + 1) * 512])
            sl.append(w_branch[n][:, j * 512:(j + 1) * 512])
    for j in range(4):
        sl.append(w_out[:, j * 512:(j + 1) * 512])
    for j in range(4):
        sl.append(w_mq[:, j * 512:(j + 1) * 512])
    for j in range(4):
        sl.append(w_mo[:, j * 512:(j + 1) * 512])
    for j in range(11):
        sl.append(w_ffn_in[:, j * 512:(j + 1) * 512])
        sl.append(w_ffn_in[:, DFF + j * 512:DFF + (j + 1) * 512])
    assert len(sl) == NSLAB
    return np.stack([_slab(s) for s in sl])


def _pvec(v):
    return v.reshape(-1, 128).T


_NC_CACHE = {}


def kernel(x_prompt, x_sample, mem_prompt, state_ssd, state_ssd_conv, state_gla, state_ffn_conv,
           cache_mem_k, cache_mem_v, norm_mix, w_in, ssd_conv_w, ssd_conv_b, ssd_dt_bias, ssd_a_log,
           ssd_d, ssd_norm, gla_wa2, gla_ba, gla_norm, w_branch, w_out, norm_mem, w_mq, w_mk, w_mv,
           w_mo, norm_ffn, w_ffn_in, ffn_conv_w, ffn_conv_b, w_ffn_out, norm_final, _steps=NSTEP, _pairs=4):
    f = lambda a: np.asarray(a, dtype=np.float32)
    (x_prompt, x_sample, mem_prompt, state_ssd, state_ssd_conv, state_gla, state_ffn_conv, cache_mem_k, cache_mem_v,
     norm_mix, w_in, ssd_conv_w, ssd_conv_b, ssd_dt_bias, ssd_a_log, ssd_d, ssd_norm, gla_wa2, gla_ba, gla_norm,
     w_branch, w_out, norm_mem, w_mq, w_mk, w_mv, w_mo, norm_ffn, w_ffn_in, ffn_conv_w, ffn_conv_b, w_ffn_out,
     norm_final) = map(f, (x_prompt, x_sample, mem_prompt, state_ssd, state_ssd_conv, state_gla, state_ffn_conv,
                           cache_mem_k, cache_mem_v, norm_mix, w_in, ssd_conv_w, ssd_conv_b, ssd_dt_bias, ssd_a_log,
                           ssd_d, ssd_norm, gla_wa2, gla_ba, gla_norm, w_branch, w_out, norm_mem, w_mq, w_mk, w_mv,
                           w_mo, norm_ffn, w_ffn_in, ffn_conv_w, ffn_conv_b, w_ffn_out, norm_final))
    rg = [[p, p + _pairs] for p in range(_pairs)]
    key = (_steps, _pairs)
    if key not in _NC_CACHE:
        _NC_CACHE[key] = Builder(_steps, rg).build()
    nc = _NC_CACHE[key]

    cst = np.zeros((128, 512), np.float32)
    idx = np.arange(128)
    cst[:, 0:128] = np.eye(128, dtype=np.float32)
    cst[:, 128:256] = (idx[:, None] <= idx[None, :])
    cst[:, 256:384] = (idx[:, None] > idx[None, :])
    cst[:, 384:512] = 1.0
    lay = []
    for l in range(DEPTH):
        sh = {"cst": cst}
        sh["W0"] = _layer_slabs(w_in[l], w_branch[l], w_out[l], w_mq[l], w_mo[l], w_ffn_in[l])
        wf = np.zeros((2, 8, 128, 24 * 256), np.float32)
        for hf, (k0, k1) in enumerate(((0, 24), (24, 44))):
            for j in range(8):
                blk = w_ffn_out[l][k0 * 128:k1 * 128, j * 256:(j + 1) * 256].reshape(k1 - k0, 128, 256).transpose(1, 0, 2)
                wf[hf, j, :, 0:(k1 - k0) * 256] = blk.reshape(128, (k1 - k0) * 256)
        sh["WF0"] = wf
        sh["WKV0"] = np.stack([_slab(w_mk[l][:, j * 512:(j + 1) * 512]) for j in range(4)] +
                              [_slab(w_mv[l][:, j * 512:(j + 1) * 512]) for j in range(4)])
        sh["wsm0"] = _slab(np.concatenate([w_in[l][:, O_DT:O_DT + 32], w_in[l][:, O_ALR:O_ALR + 16]], axis=1))
        sh["wa2_0"] = np.ascontiguousarray(gla_wa2[l])
        vp = np.zeros((128, NVP), np.float32)
        vp[:, P_NMIX:P_NMIX + 16] = _pvec(norm_mix[l])
        vp[:, P_NMEM:P_NMEM + 16] = _pvec(norm_mem[l])
        vp[:, P_NFFN:P_NFFN + 16] = _pvec(norm_ffn[l])
        vp[:, P_NFIN:P_NFIN + 16] = _pvec(norm_final)
        for tap in range(4):
            vp[:, P_SCW + tap * 24:P_SCW + (tap + 1) * 24] = _pvec(ssd_conv_w[l, tap])
        vp[:, P_SCB:P_SCB + 24] = _pvec(ssd_conv_b[l])
        for tap in range(3):
            vp[:, P_FCW + tap * 88:P_FCW + (tap + 1) * 88] = _pvec(ffn_conv_w[l, tap])
        vp[:, P_FCB:P_FCB + 88] = _pvec(ffn_conv_b[l])
        vp[:, P_BA:P_BA + 8] = _pvec(gla_ba[l])
        sh["vecP0"] = vp
        vr = np.concatenate([ssd_dt_bias[l], ssd_a_log[l], ssd_d[l], ssd_norm[l], gla_norm[l]])
        sh["vecR0"] = np.ascontiguousarray(np.broadcast_to(vr[None, :], (128, NVR)))
        lay.append(sh)

    maps = {}
    for l in range(DEPTH):
        for p in range(_pairs):
            ss = [4 * p + i for i in range(NSAMP)]
            m = dict(lay[l])
            xT = np.zeros((D, TOK), np.float32)
            if l == 0:
                for i, s_ in enumerate(ss):
                    xT[:, 16 * i:16 * (i + 1)] = x_sample[s_].T
                xT[:, 64:64 + SEQ] = x_prompt[p].T
            m["xT"] = xT
            m["memT"] = np.ascontiguousarray(mem_prompt[p].T)
            flg = np.ones((128, 8), np.float32)
            flg[:, 0] = float(l)
            if l == 1:
                flg[:, 1 + 3] = 0.0
            m["flg"] = flg
            m["s_ssd"] = np.ascontiguousarray(np.stack([state_ssd[l, s_].reshape(2048, 128).T for s_ in ss])[None])
            m["s_sconv"] = np.ascontiguousarray(np.stack([state_ssd_conv[l, s_].reshape(3, 24, 128).transpose(2, 1, 0).reshape(128, 72)
                                                          for s_ in ss])[None])
            m["s_fconv"] = np.ascontiguousarray(np.stack([state_ffn_conv[l, s_].reshape(2, 88, 128).transpose(2, 1, 0).reshape(128, 176)
                                                          for s_ in ss])[None])
            m["s_gla"] = np.ascontiguousarray(np.stack([state_gla[l, s_].reshape(1024, 512) for s_ in ss])[None])
            m["s_kT"] = np.ascontiguousarray(np.stack([cache_mem_k[l, s_].reshape(256, 2048).T for s_ in ss])[None])
            m["s_v"] = np.ascontiguousarray(np.stack([cache_mem_v[l, s_].reshape(256, 2048) for s_ in ss])[None])
            maps[(l, p)] = m
    in_maps = [maps[(0, p)] for p in range(_pairs)] + [maps[(1, p)] for p in range(_pairs)]
    res = run_bass_kernel_spmd(nc, in_maps, core_ids=list(range(2 * _pairs))).results
    RA = lambda p: res[p % _pairs]
    RB = lambda p: res[_pairs + p % _pairs]
    R = lambda l, p: RA(p) if l == 0 else RB(p)

    B, NSEQ = 4, 16
    y_prompt = np.stack([RB(p)["yT"][:, 640:640 + SEQ].T for p in range(B)])
    y_sample = np.stack([RB(s // 4)["yT"][:, 64 + 16 * (s % 4):64 + 16 * (s % 4 + 1)].T for s in range(NSEQ)])

    def un_ssd(a):
        return a.T.reshape(32, 64, 128)

    def un_conv(a, w, nch):
        return a.reshape(128, nch, w).transpose(2, 1, 0).reshape(w, nch * 128)
    fin = lambda l: l
    sl = lambda l, s: 2 + NSAMP * l + s % 4
    p_ssd = np.stack([[un_ssd(R(l, p)["o_ssd"][fin(l)]) for p in range(B)] for l in range(DEPTH)])
    s_ssd = np.stack([[un_ssd(R(l, s // 4)["o_ssd"][sl(l, s)]) for s in range(NSEQ)] for l in range(DEPTH)])
    p_sconv = np.stack([[un_conv(R(l, p)["o_sconv"][fin(l)], 3, 24) for p in range(B)] for l in range(DEPTH)])
    s_sconv = np.stack([[un_conv(R(l, s // 4)["o_sconv"][sl(l, s)], 3, 24) for s in range(NSEQ)] for l in range(DEPTH)])
    p_gla = np.stack([[R(l, p)["o_gla"][fin(l)].reshape(4, 256, 512) for p in range(B)] for l in range(DEPTH)])
    s_gla = np.stack([[R(l, s // 4)["o_gla"][sl(l, s)].reshape(4, 256, 512) for s in range(NSEQ)] for l in range(DEPTH)])
    p_fconv = np.stack([[un_conv(R(l, p)["o_fconv"][fin(l)], 2, 88) for p in range(B)] for l in range(DEPTH)])
    s_fconv = np.stack([[un_conv(R(l, s // 4)["o_fconv"][sl(l, s)], 2, 88) for s in range(NSEQ)] for l in range(DEPTH)])
    p_mem_k = np.stack([[R(l, p)["o_kT"][0].T.reshape(256, 4, 512) for p in range(B)] for l in range(DEPTH)])
    p_mem_v = np.stack([[R(l, p)["o_v"][0].reshape(256, 4, 512) for p in range(B)] for l in range(DEPTH)])
    outs = (y_prompt, y_sample, p_ssd, p_sconv, p_gla, p_fconv, p_mem_k, p_mem_v, s_ssd, s_sconv, s_gla, s_fconv)
    return tuple(np.ascontiguousarray(o, dtype=np.float32) for o in outs)
```

```python
import numpy as np
from contextlib import ExitStack
import concourse.bass as bass
import concourse.mybir as mybir
from concourse.bass_utils import run_bass_kernel_spmd

F32 = mybir.dt.float32
BF16 = mybir.dt.bfloat16
AF = mybir.ActivationFunctionType
ALU = mybir.AluOpType
AX = mybir.AxisListType

D = 2048
KC = 16
SEQ = 2048
NSAMP = 4
SLEN = 16
TOK = 64 + 5 * 512
NY = 2 * 64 + 5 * 512
NSTEP = 7
DEPTH = 2
EPS = 1e-6
DFF = 5632
IN_DIM = 15408
O_Z, O_XBC, O_DT, O_Q, O_K, O_V, O_G, O_ALR, O_GATES = 0, 2048, 5120, 5152, 6176, 7200, 9248, 11296, 11312
P_NMIX, P_NMEM, P_NFFN, P_NFIN, P_SCW, P_SCB, P_FCW, P_FCB, P_BA, NVP = 0, 16, 32, 48, 64, 160, 184, 448, 536, 544
R_DTB, R_ALOG, R_DSK, R_SNORM, R_GNORM, NVR = 0, 32, 64, 96, 2144, 2656
NSLAB = 72
NW = 512


def xbc_slab_chunks():
    xs = [[4 * g + m for m in range(4)] for g in range(4)]
    bc = [[16 + 2 * j, 20 + 2 * j, 16 + 2 * j + 1, 20 + 2 * j + 1] for j in range(2)]
    return xs, bc


class E:
    def __init__(self, h, sem, name):
        self.h, self.sem, self.cnt, self.waited, self.name = h, sem, 0, {}, name


class Builder:
    def __init__(self, n_tiles=NSTEP, rg=None):
        self.n_tiles, self.n_layers = n_tiles, 1
        self.rg = rg or [[0, 4], [1, 5], [2, 6], [3, 7]]
        self.nc = nc = bass.Bass("TRN2", target_bir_lowering=False)
        self.es = ExitStack()
        self.st = {}
        self.dsem = {}
        self.engs = {}
        for nm, h in (("pe", nc.tensor), ("act", nc.scalar), ("dve", nc.vector), ("pool", nc.gpsimd), ("sp", nc.sync)):
            sem = self.es.enter_context(nc.semaphore("e_" + nm))
            self.engs[nm] = E(h, sem, nm)
        self.psi = 0

    def _pre(self, en, R, W):
        eng = self.engs[en]
        deps = {}

        def add(ev):
            if ev is None:
                return
            sem, val = ev
            k = id(sem)
            if k not in deps or deps[k][1] < val:
                deps[k] = (sem, val)
        for k in R:
            s = self.st.get(k)
            if s:
                add(s[0])
        for k in W:
            s = self.st.get(k)
            if s:
                add(s[0])
                for ev in s[1].values():
                    add(ev)
        for sem, val in deps.values():
            if en == "pe" and sem is eng.sem:
                continue
            if eng.waited.get(id(sem), 0) < val:
                eng.h.wait_ge(sem, val)
                eng.waited[id(sem)] = val

    def _post(self, ev, R, W):
        for k in R:
            s = self.st.setdefault(k, [None, {}])
            s[1][id(ev[0])] = ev
        for k in W:
            self.st[k] = [ev, {}]

    def op(self, en, inst, R, W):
        eng = self.engs[en]
        eng.cnt += 1
        inst.then_inc(eng.sem, 1)
        self._post((eng.sem, eng.cnt), R, W)

    def dma(self, en, pairs, R, W, sk):
        eng = self.engs[en]
        self._pre(en, R, W)
        if sk not in self.dsem:
            self.dsem[sk] = [self.es.enter_context(self.nc.semaphore("d_" + sk)), 0]
        d = self.dsem[sk]
        if d[1] > 0 and eng.waited.get(id(d[0]), 0) < d[1]:
            eng.h.wait_ge(d[0], d[1])
            eng.waited[id(d[0])] = d[1]
        for o, i in pairs:
            eng.h.dma_start(out=o, in_=i).then_inc(d[0], 16)
            d[1] += 16
        self._post((d[0], d[1]), R, W)

    def barrier(self, keep=("wsl0", "wsl1")):
        names = ("pe", "act", "dve", "sp")
        for en in names + ("pool",):
            eng = self.engs[en]
            for on in names:
                o = self.engs[on]
                if o is eng or o.cnt == 0:
                    continue
                if eng.waited.get(id(o.sem), 0) < o.cnt:
                    eng.h.wait_ge(o.sem, o.cnt)
                    eng.waited[id(o.sem)] = o.cnt
            for sk, d in self.dsem.items():
                if sk in keep or d[1] == 0 or (sk == "cc" and en != "pool"):
                    continue
                if eng.waited.get(id(d[0]), 0) < d[1]:
                    eng.h.wait_ge(d[0], d[1])
                    eng.waited[id(d[0])] = d[1]
        self.st = {k: v for k, v in self.st.items() if k in keep}

    def ACT(self, out, in_, func, R, W, bias=None, scale=None, accum=None):
        self._pre("act", R, W)
        kw = {}
        if bias is not None:
            kw["bias"] = bias
        if scale is not None:
            kw["scale"] = scale
        if accum is not None:
            kw["accum_out"] = accum
        i = self.nc.scalar.activation(out=out, in_=in_, func=func, **kw)
        self.op("act", i, R, W)

    def TT(self, out, in0, in1, op, R, W, en="dve"):
        self._pre(en, R, W)
        i = self.engs[en].h.tensor_tensor(out=out, in0=in0, in1=in1, op=op)
        self.op(en, i, R, W)

    def TS(self, out, in0, s1, s2, op0, op1, R, W, en="dve"):
        self._pre(en, R, W)
        if s2 is None:
            i = self.engs[en].h.tensor_scalar(out=out, in0=in0, scalar1=s1, scalar2=None, op0=op0)
        else:
            i = self.engs[en].h.tensor_scalar(out=out, in0=in0, scalar1=s1, scalar2=s2, op0=op0, op1=op1)
        self.op(en, i, R, W)

    def STT(self, out, in0, scalar, in1, op0, op1, R, W, en="dve"):
        self._pre(en, R, W)
        i = self.engs[en].h.scalar_tensor_tensor(out=out, in0=in0, scalar=scalar, in1=in1, op0=op0, op1=op1)
        self.op(en, i, R, W)

    def CP(self, out, in_, R, W, en="dve"):
        self._pre(en, R, W)
        if en == "act":
            i = self.nc.scalar.activation(out=out, in_=in_, func=AF.Copy)
        else:
            i = self.engs[en].h.tensor_copy(out=out, in_=in_)
        self.op(en, i, R, W)

    def RSQ(self, ap, key):
        self.ACT(ap, ap, AF.Sqrt, [key], [key])
        self._pre("dve", [key], [key])
        i = self.nc.vector.reciprocal(out=ap, in_=ap)
        self.op("dve", i, [key], [key])

    def MS(self, ap, val, W, en="dve"):
        self._pre(en, [], W)
        i = self.engs[en].h.memset(ap, val)
        self.op(en, i, [], W)

    def MM(self, out, lhsT, rhs, start, stop, R, W):
        self._pre("pe", R, W)
        i = self.nc.tensor.matmul(out, lhsT=lhsT, rhs=rhs, start=start, stop=stop)
        self.op("pe", i, R, W)

    def ps(self):
        i = self.psi
        self.psi = (self.psi + 1) % 8
        return self.psum[i], "ps%d" % i

    def slab(self, expect):
        q = self.wq
        i = self.wi
        assert q[i][0] == expect, (q[i][0], expect)
        for j in (i, i + 1):
            if j < len(q) and j >= self.wissued:
                name, src, n = q[j]
                b = j % 2
                self.dma("pool", [(self.wsl[b][:, 0:n], src)], [], ["wsl%d" % b], "wsl%d" % b)
                self.wissued = j + 1
        self.wi += 1
        return self.wsl[i % 2], "wsl%d" % (i % 2)

    def build(self):
        nc = self.nc
        es = self.es
        dt_in = lambda name, shape: nc.dram_tensor(name, shape, F32, kind="ExternalInput").ap()
        dt_out = lambda name, shape: nc.dram_tensor(name, shape, F32, kind="ExternalOutput").ap()
        self.xT = dt_in("xT", [D, TOK])
        self.memT = dt_in("memT", [D, 256])
        self.cst_d = dt_in("cst", [128, 512])
        self.flg_d = dt_in("flg", [128, 8])
        self.W = [dt_in("W0", [NSLAB, 128, 8192])]
        self.WF = [dt_in("WF0", [2, 8, 128, 24 * 256])]
        self.WKV = [dt_in("WKV0", [8, 128, 8192])]
        self.wsm_d = [dt_in("wsm0", [128, 16 * 48])]
        self.wa2_d = [dt_in("wa2_0", [16, 1024])]
        self.vecP_d = [dt_in("vecP0", [128, NVP])]
        self.vecR_d = [dt_in("vecR0", [128, NVR])]
        self.s_ssd_d = dt_in("s_ssd", [1, NSAMP, 128, 2048])
        self.s_sconv_d = dt_in("s_sconv", [1, NSAMP, 128, 72])
        self.s_gla_d = dt_in("s_gla", [1, NSAMP, 1024, 512])
        self.s_fconv_d = dt_in("s_fconv", [1, NSAMP, 128, 176])
        self.s_kT_d = dt_in("s_kT", [1, NSAMP, D, 256])
        self.s_v_d = dt_in("s_v", [1, NSAMP, 256, D])
        self.yT = dt_out("yT", [D, NY])
        NO = 2 + 2 * NSAMP
        self.o_ssd = dt_out("o_ssd", [NO, 128, 2048])
        self.o_sconv = dt_out("o_sconv", [NO, 128, 72])
        self.o_gla = dt_out("o_gla", [NO, 1024, 512])
        self.o_fconv = dt_out("o_fconv", [NO, 128, 176])
        self.o_kT = dt_out("o_kT", [1, D, 256])
        self.o_v = dt_out("o_v", [1, 256, D])
        self.run_ssd = nc.dram_tensor("run_ssd", [128, 2048], F32).ap()
        self.run_gla = nc.dram_tensor("run_gla", [1024, 512], F32).ap()
        self.sendP = [nc.dram_tensor("sendP%d" % i, [128, 4096], F32).ap() for i in range(2)]
        self.gathP = [nc.dram_tensor("gathP%d" % i, [256, 4096], F32).ap() for i in range(2)]
        self.sendS = [nc.dram_tensor("sendS%d" % i, [128, 512], F32).ap() for i in range(2)]
        self.gathS = [nc.dram_tensor("gathS%d" % i, [256, 512], F32).ap() for i in range(2)]

        sb = lambda name, shape, dt: es.enter_context(nc.sbuf_tensor(name, shape, dt))
        self.x = sb("x", [128, 16, 512], F32)
        self.h = sb("h", [128, 16, 512], BF16)
        self.wsl = [sb("wsl0", [128, 8192], BF16), sb("wsl1", [128, 8192], BF16)]
        self.cst = sb("cstf", [128, 512], F32)
        self.identb = sb("identb", [128, 128], BF16)
        self.onesb = sb("onesb", [128, 128], BF16)
        self.vecP = sb("vecP", [128, NVP], F32)
        self.vecR = sb("vecR", [128, NVR], F32)
        self.nba = sb("nba", [128, 8], F32)
        self.arow = sb("arow", [128, 32], F32)
        self.wsm = sb("wsm", [128, 16 * 48], BF16)
        self.wa2 = sb("wa2", [16, 1024], BF16)
        self.alrT = sb("alrT", [16, 512], BF16)
        self.ssdv = sb("ssdv", [128, 4, 6, 32], F32)
        self.cv_s = sb("cv_s", [128, DEPTH, 72], F32)
        self.cv_f = sb("cv_f", [128, DEPTH, 176], F32)
        self.hs = sb("hs", [128, NSAMP, 176], F32)
        self.ho = sb("ho", [128, NSAMP, 176], F32)
        self.rs = sb("rs", [128, 512], F32)
        self.small = sb("small", [128, 64], F32)
        self.flg = sb("flgs", [128, 8], F32)
        self.A = sb("arenaA", [128, 24576], BF16)
        self.Bn = sb("arenaB", [128, 12288], BF16)
        self.psum = [es.enter_context(nc.psum_tensor("ps%d" % i, [128, 512], F32)) for i in range(8)]

        self.wq = []
        for l in range(self.n_layers):
            for j in range(8):
                self.wq.append(("kv%d_%d" % (l, j), self.WKV[l][j], 8192))
        for t in range(self.n_tiles):
            for l in range(self.n_layers):
                for j in range(50 + 12):
                    self.wq.append(("L%d_%d" % (l, j), self.W[l][j], 8192))
                for j in range(8):
                    self.wq.append(("F%d_0_%d" % (l, j), self.WF[l][0, j], 24 * 256))
                for j in range(62, NSLAB):
                    self.wq.append(("L%d_%d" % (l, j), self.W[l][j], 8192))
                for j in range(8):
                    self.wq.append(("F%d_1_%d" % (l, j), self.WF[l][1, j][:, 0:20 * 256], 20 * 256))
        self.wi = 0
        self.wissued = 0

        self.dma("sp", [(self.cst[:], self.cst_d)], [], ["cst"], "cst")
        self.dma("pool", [(self.identb[:], self.cst_d[:, 0:128]), (self.onesb[:], self.cst_d[:, 384:512])], [], ["identb"], "identb")
        self.ident_f = self.cst[:, 0:128]
        self.U_f = self.cst[:, 128:256]
        self.SL_f = self.cst[:, 256:384]
        self.ones_f = self.cst[:, 384:512]
        self.MS(self.cv_s[:], 0.0, ["cv_s"])
        self.MS(self.cv_f[:], 0.0, ["cv_f"])

        self.dma("sp", [(self.flg[:], self.flg_d)], [], ["flg"], "flg")
        z = self.A[:, 0:16384].bitcast(F32)
        self.MS(z, 0.0, ["zt"])
        v3 = lambda ap: ap.rearrange("(c p) t -> p c t", p=128)
        self.dma("sp", [(self.run_ssd, z[:, 0:2048]),
                        (v3(self.run_gla), z[:, 0:4096].rearrange("p (c v) -> p c v", c=8)),
                        ] + [(self.gathP[i][0:128, :], z[:, 4096 * i:4096 * (i + 1)]) for i in range(2)]
                        + [(self.gathS[i][0:128, :], z[:, 512 * i:512 * (i + 1)]) for i in range(2)],
                 ["zt"], ["run_ssd", "run_gla", "gathP", "gathS"], "zinit")
        self.barrier()
        self.prologue_kv()
        steps = []
        for st_ in range(NSTEP):
            if st_ < 2:
                steps.append(dict(kind="s", step=st_, TT=64, xcol=0, ycol=64 * st_, segs=[(16 * i, 16) for i in range(4)],
                                  nrun=4, RL=16))
            else:
                steps.append(dict(kind="p", step=st_, TT=512, xcol=64 + 512 * (st_ - 2), ycol=128 + 512 * (st_ - 2),
                                  segs=[(128 * i, 128) for i in range(4)], nrun=1, RL=512))
        for T in steps[: self.n_tiles]:
            self.run_tile(T)
        sp = self.engs["sp"]
        for en, e in self.engs.items():
            if e is not sp and e.cnt > 0:
                sp.h.wait_ge(e.sem, e.cnt)
        for sk, d in self.dsem.items():
            if d[1] > 0:
                (self.engs["pool"] if sk == "cc" else sp).h.wait_ge(d[0], d[1])
        return nc

    def prologue_kv(self):
        nc = self.nc
        mT = self.A[:, 0:16 * 256].rearrange("p (c m) -> p c m", c=16)
        stage = self.Bn[:, 0:4096].bitcast(F32).rearrange("p (a n) -> p a n", a=4)
        self.dma("pool", [(mT, self.memT.rearrange("(c p) m -> p c m", p=128))], [], ["mT"], "mT")
        for l in range(self.n_layers):
            for j in range(4):
                w, wk = self.slab("kv%d_%d" % (l, j))
                for m in range(4):
                    ps, pk = self.ps()
                    for kc in range(16):
                        self.MM(ps[:, 0:256], w[:, kc * NW + m * 128: kc * NW + m * 128 + 128], mT[:, kc, :], kc == 0, kc == 15,
                                [wk, "mT"], [pk])
                    self.CP(stage[:, m, 0:256], ps[:, 0:256], [pk], ["kvst"], en="act")
                self.dma("sp", [(self.o_kT[l, j * 512:(j + 1) * 512, :].rearrange("(m p) n -> p m n", p=128), stage[:, :, 0:256])],
                         ["kvst"], ["o_kT%d" % l], "o_kT%d" % l)
            for j in range(4):
                w, wk = self.slab("kv%d_%d" % (l, 4 + j))
                for mc in range(2):
                    ps, pk = self.ps()
                    for kc in range(16):
                        self.MM(ps[:, :], mT[:, kc, mc * 128:(mc + 1) * 128], w[:, kc * NW:(kc + 1) * NW], kc == 0, kc == 15,
                                [wk, "mT"], [pk])
                    self.CP(stage[:, mc, :], ps[:, :], [pk], ["kvst"], en="act")
                self.dma("sp", [(self.o_v[l, :, j * 512:(j + 1) * 512].rearrange("(m p) n -> p m n", p=128), stage[:, 0:2, :])],
                         ["kvst"], ["o_v%d" % l], "o_v%d" % l)
        self.barrier()

    def cc(self, send, gath, R, W):
        self._pre("pool", R, W)
        if "cc" not in self.dsem:
            self.dsem["cc"] = [self.es.enter_context(self.nc.semaphore("ccsem")), 0]
        d = self.dsem["cc"]
        self.nc.gpsimd.collective_compute("AllGather", ALU.bypass, replica_groups=self.rg, ins=[send], outs=[gath]).then_inc(d[0], 1)
        d[1] += 1
        self._post((d[0], d[1]), R, W)

    def run_tile(self, T):
        TT, stp = T["TT"], T["step"]
        samp = T["kind"] == "s"
        xk = ["x%d" % c for c in range(16)]
        v3 = lambda ap: ap.rearrange("(c p) t -> p c t", p=128)
        xsrc = v3(self.xT)[:, :, T["xcol"]:T["xcol"] + TT]
        self.dma("sp", [(self.x[:, 0:8, 0:TT], xsrc[:, 0:8, :]), (self.x[:, 8:16, 0:TT], xsrc[:, 8:16, :])], [], xk, "x")
        gath, gk = (self.gathS, "gathS") if samp else (self.gathP, "gathP")
        send = self.sendS if samp else self.sendP
        xb = self.A[:, 0:16384].bitcast(F32).rearrange("p (c t) -> p c t", c=16)
        ng = len(gath)
        cpg = 16 // ng
        self.dma("pool", [(xb[:, cpg * i:cpg * (i + 1), 0:TT], gath[i][0:128, :].rearrange("p (c t) -> p c t", c=cpg)) for i in range(ng)],
                 [gk], ["xb"], "xb")
        for c in range(16):
            self.STT(self.x[:, c, 0:TT], xb[:, c, 0:TT], self.flg[:, 0:1], self.x[:, c, 0:TT], ALU.mult, ALU.add,
                     ["xb", "flg", "x%d" % c], ["x%d" % c])
        if not samp:
            rf = self.flg[:, 1 + stp:2 + stp]
            self.TS(self.cv_s[:, 0, :], self.cv_s[:, 0, :], rf, None, ALU.mult, None, ["cv_s", "flg"], ["cv_s"])
            self.TS(self.cv_f[:, 0, :], self.cv_f[:, 0, :], rf, None, ALU.mult, None, ["cv_f", "flg"], ["cv_f"])
        self.barrier()
        self.layer(T, 0)
        self.rms(T, P_NFIN, out_f32=True)
        ydst = v3(self.yT)[:, :, T["ycol"]:T["ycol"] + TT]
        yv = self.A[:, 0:16384].bitcast(F32).rearrange("p (c t) -> p c t", c=16)
        self.dma("sp", [(ydst, yv[:, :, 0:TT])], ["yout"], [], "yout")
        self.barrier()

    def rms(self, T, g0, out_f32=False):
        TT = T["TT"]
        ps, pk = self.ps()
        for c in range(16):
            sq = self.Bn[:, 0:1024].rearrange("p (a t) -> p a t", a=2)[:, c % 2, 0:TT]
            self.ACT(sq, self.x[:, c, 0:TT], AF.Square, ["x%d" % c], ["sq%d" % (c % 2)])
            self.MM(ps[:, 0:TT], self.onesb[:], sq, c == 0, c == 15, ["sq%d" % (c % 2), "identb"], [pk])
        rs = self.rs[:, 0:TT]
        self.TS(rs, ps[:, 0:TT], 1.0 / D, EPS, ALU.mult, ALU.add, [pk], ["rs"])
        self.RSQ(rs, "rs")
        if out_f32:
            yv = self.A[:, 0:16384].bitcast(F32).rearrange("p (c t) -> p c t", c=16)
        for c in range(16):
            o = yv[:, c, 0:TT] if out_f32 else self.h[:, c, 0:TT]
            self.STT(o, self.x[:, c, 0:TT], self.vecP[:, g0 + c:g0 + c + 1], rs, ALU.mult, ALU.mult,
                     ["x%d" % c, "rs", "vecP"], ["yout" if out_f32 else "h"])

    def layer(self, T, l):
        self.dma("sp", [(self.vecP[:], self.vecP_d[l]), (self.vecR[:], self.vecR_d[l])], [], ["vecP", "vecR"], "vec")
        self.dma("pool", [(self.wsm[:], self.wsm_d[l]), (self.wa2[:], self.wa2_d[l])], [], ["wsm", "wa2"], "wsmall")
        self.ACT(self.arow[:], self.vecR[:, R_ALOG:R_ALOG + 32], AF.Exp, ["vecR"], ["arow"])
        self.TS(self.arow[:], self.arow[:], -1.0, None, ALU.mult, None, ["arow"], ["arow"])
        self.TS(self.nba[:], self.vecP[:, P_BA:P_BA + 8], -1.0, None, ALU.mult, None, ["vecP"], ["nba"])
        self.rms(T, P_NMIX)
        self.mixer(T, l)
        self.barrier()
        self.rms(T, P_NMEM)
        self.attn(T, l)
        self.barrier()
        self.rms(T, P_NFFN)
        self.ffn(T, l)
        self.barrier()

    def fm(self, T, w, wk, rhs, rk, nk=16, nch=4, nw=NW):
        TT = T["TT"]
        for m in range(nch):
            ps, pk = self.ps()
            for kc in range(nk):
                self.MM(ps[:, 0:TT], w[:, kc * nw + m * 128: kc * nw + m * 128 + 128], rhs(kc), kc == 0, kc == nk - 1,
                        [wk] + rk, [pk])
            yield m, ps, pk

    def tm(self, T, w, wk):
        for si, (c0, L) in enumerate(T["segs"]):
            ps, pk = self.ps()
            for kc in range(16):
                self.MM(ps[0:L, :], self.h[:, kc, c0:c0 + L], w[:, kc * NW:(kc + 1) * NW], kc == 0, kc == 15, [wk, "h"], [pk])
            yield si, c0, L, ps, pk

    def conv(self, T, ps, pk, width, wcol0, wstride, bcol, oc, hist, hkey, newhist, nkey, raw, rawk, acc, acck):
        nrun, RL, TT = T["nrun"], T["RL"], T["TT"]
        H = width - 1
        rv = raw[:, 0:nrun * (H + RL)].rearrange("p (r t) -> p r t", r=nrun)
        av = acc[:, 0:TT].rearrange("p (r t) -> p r t", r=nrun)
        self.CP(rv[:, :, H:H + RL], ps[:, 0:TT].rearrange("p (r t) -> p r t", r=nrun), [pk], [rawk], en="act")
        self.CP(rv[:, :, 0:H], hist, [hkey], [rawk], en="dve")
        wc = lambda tap: self.vecP[:, wcol0 + tap * wstride + oc: wcol0 + tap * wstride + oc + 1]
        self.TS(av, rv[:, :, H:H + RL], wc(width - 1), self.vecP[:, bcol + oc:bcol + oc + 1], ALU.mult, ALU.add,
                [rawk, "vecP"], [acck])
        for tap in range(width - 1):
            self.STT(av, rv[:, :, tap:tap + RL], wc(tap), av, ALU.mult, ALU.add, [rawk, "vecP", acck], [acck])
        self.CP(newhist, rv[:, :, RL:RL + H], [rawk], [nkey], en="dve")
        return av

    def mixer(self, T, l):
        nc = self.nc
        TT, segs, nrun, RL = T["TT"], T["segs"], T["nrun"], T["RL"]
        nseg = len(segs)
        prompt = T["kind"] == "p"
        A, Bn = self.A, self.Bn
        ysnT = A[:, 0:8192].rearrange("p (c t) -> p c t", c=16)
        ygnT = A[:, 8192:16384].rearrange("p (c t) -> p c t", c=16)
        mg = A[:, 16384:24576].rearrange("p (c t) -> p c t", c=16)
        pools = [[Bn, 0, 12288], [A, 16384, 24576]]

        def carve(n):
            for pl in pools:
                if pl[1] + n <= pl[2]:
                    a = pl[0][:, pl[1]:pl[1] + n]
                    pl[1] += n
                    return a
            raise AssertionError("arena overflow")
        xsT = carve(4 * TT).rearrange("p (c t) -> p c t", c=4)
        bcT = carve(4 * TT).rearrange("p (c t) -> p c t", c=4)
        sz = carve(nseg * 512).rearrange("p (s n) -> p s n", s=nseg)
        raw = [carve(1040).bitcast(F32) for _ in range(2)]
        cacc = [carve(2 * TT).bitcast(F32) for _ in range(2)]
        xs_tok = carve(512)
        B_tok = carve(128)
        rhsA = carve(2048).bitcast(F32).rearrange("p (r i) -> p r i", r=8)
        Eg = rhsA
        Wt = carve(1024).rearrange("p (r i) -> p r i", r=8)
        cbm = carve(256).bitcast(F32)
        t1 = carve(1024).bitcast(F32)
        t2 = carve(1024).bitcast(F32)
        yb = carve(1024).bitcast(F32)
        xw = carve(512)
        Sg = [carve(1024).bitcast(F32) for _ in range(1 if prompt else NSAMP)]
        Sgb1 = carve(512)
        ytok = carve(512)
        vR, vP = self.vecR, self.vecP
        sv = self.ssdv

        if not prompt:
            self.dma("sp", [(self.hs[:, :, 0:72], self.s_sconv_d[l].rearrange("s p n -> p s n"))], [], ["hs"], "hs")

        wsm = self.wsm
        ps, pk = self.ps()
        for kc in range(16):
            self.MM(ps[0:16, 0:TT], wsm[:, kc * 48 + 32: kc * 48 + 48], self.h[:, kc, 0:TT], kc == 0, kc == 15, ["wsm", "h"], [pk])
        self.CP(self.alrT[:, 0:TT], ps[0:16, 0:TT], [pk], ["alrT"], en="act")
        for si, (c0, L) in enumerate(segs):
            ps, pk = self.ps()
            for kc in range(16):
                self.MM(ps[0:L, 0:32], self.h[:, kc, c0:c0 + L], wsm[:, kc * 48: kc * 48 + 32], kc == 0, kc == 15, ["wsm", "h"], [pk])
            dt, dA, acs, ea, te, cd = [sv[:, si, q, :] for q in range(6)]
            k = "sv%d" % si
            self.TT(dt[0:L], ps[0:L, 0:32], vR[0:L, R_DTB:R_DTB + 32], ALU.add, [pk, "vecR"], [k])
            self.ACT(dt[0:L], dt[0:L], AF.Exp, [k], [k])
            self.ACT(dt[0:L], dt[0:L], AF.Ln, [k], [k], bias=1.0)
            self.TT(dA[0:L], dt[0:L], self.arow[0:L], ALU.mult, [k, "arow"], [k])
            ps2, pk2 = self.ps()
            self.MM(ps2[0:L, 0:32], self.U_f[0:L, 0:L], dA[0:L], True, True, [k, "cst"], [pk2])
            self.MM(ps2[:, 32:64], self.ones_f[0:L, :], dA[0:L], True, True, [k, "cst"], [pk2])
            self.CP(acs[0:L], ps2[0:L, 0:32], [pk2], [k], en="act")
            self.ACT(ea[0:L], ps2[0:L, 0:32], AF.Exp, [pk2], [k])
            self.ACT(cd, ps2[:, 32:64], AF.Exp, [pk2], [k])
            self.TT(te[0:L], ps2[0:L, 32:64], acs[0:L], ALU.subtract, [pk2, k], [k])
            self.ACT(te[0:L], te[0:L], AF.Exp, [k], [k])
            self.TT(te[0:L], te[0:L], dt[0:L], ALU.mult, [k], [k])

        xs_ch, bc_ch = xbc_slab_chunks()
        sidx = [0]

        def nslab():
            w, wk = self.slab("L%d_%d" % (l, sidx[0]))
            sidx[0] += 1
            return w, wk
        hfn = lambda kc: self.h[:, kc, 0:TT]
        hist_s = (lambda oc: self.cv_s[:, l, oc * 3:oc * 3 + 3].unsqueeze(1)) if prompt else \
                 (lambda oc: self.hs[:, :, oc * 3:oc * 3 + 3])
        nh_s = (lambda oc: self.cv_s[:, l, oc * 3:oc * 3 + 3].unsqueeze(1)) if prompt else \
               (lambda oc: self.ho[:, :, oc * 3:oc * 3 + 3])
        hk, nk_ = ("cv_s", "cv_s") if prompt else ("hs", "ho")
        cnt = [0]

        def xbc_slab(chs, dst, dkey):
            w, wk = nslab()
            for m, ps, pk in self.fm(T, w, wk, hfn, ["h"]):
                oc = chs[m]
                b = cnt[0] % 2
                cnt[0] += 1
                av = self.conv(T, ps, pk, 4, P_SCW, 24, P_SCB, oc, hist_s(oc), hk, nh_s(oc), nk_,
                               raw[b], "raw%d" % b, cacc[b], "cacc%d" % b)
                self.ACT(dst[:, m, 0:TT].rearrange("p (r t) -> p r t", r=nrun), av, AF.Silu, ["cacc%d" % b], [dkey])

        for g in range(4):
            xbc_slab(xs_ch[g], xsT, "xsT")
            if g % 2 == 0:
                xbc_slab(bc_ch[g // 2], bcT, "bcT")
            BT = bcT[:, 2 * (g % 2), :]
            CT = bcT[:, 2 * (g % 2) + 1, :]
            w, wk = nslab()
            for si, c0, L, ps, pk in self.tm(T, w, wk):
                self.ACT(sz[0:L, si, :], ps[0:L, :], AF.Silu, [pk], ["sz"])
            if prompt:
                self.dma("sp", [(Sg[0], self.run_ssd[:, g * 512:(g + 1) * 512])], ["run_ssd"], ["Sg0"], "Sg0")
                self.TS(Sg[0], Sg[0], self.flg[:, 1 + T["step"]:2 + T["step"]], None, ALU.mult, None, ["Sg0", "flg"], ["Sg0"])
            else:
                for s in range(NSAMP):
                    self.dma("sp", [(Sg[s], self.s_ssd_d[l, s, :, g * 512:(g + 1) * 512])], [], ["Sg%d" % s], "Sg%d" % s)
            for si, (c0, L) in enumerate(segs):
                sq_ = 0 if prompt else si
                S, Sb, Sk = Sg[sq_], Sgb1, "Sg%d" % sq_
                dt, dA, acs, ea, te, cd = [sv[:, si, q, :] for q in range(6)]
                k = "sv%d" % si
                h0 = 8 * g
                self.CP(Sb, S, [Sk], ["Sgbb"], en="act")
                ps, pk = self.ps()
                for m in range(4):
                    self.MM(ps[0:L, m * 128:(m + 1) * 128], xsT[:, m, c0:c0 + L], self.identb[:], True, True, ["xsT", "identb"], [pk])
                self.CP(xs_tok[0:L, :], ps[0:L, :], [pk], ["xs_tok"], en="act")
                ps, pk = self.ps()
                self.MM(ps[0:L, 0:128], BT[:, c0:c0 + L], self.identb[:], True, True, ["bcT", "identb"], [pk])
                self.CP(B_tok[0:L, :], ps[0:L, 0:128], [pk], ["B_tok"], en="act")
                ps, pk = self.ps()
                self.MM(ps[0:L, 0:L], BT[:, c0:c0 + L], CT[:, c0:c0 + L], True, True, ["bcT"], [pk])
                self.TT(cbm[0:L, 0:L], ps[0:L, 0:L], self.U_f[0:L, 0:L], ALU.mult, [pk, "cst"], ["cbm"])
                self.TT(rhsA[0:L, :, 0:L], dA[0:L, h0:h0 + 8].unsqueeze(2).to_broadcast([L, 8, L]),
                        self.U_f[0:L, 0:L].unsqueeze(1).to_broadcast([L, 8, L]), ALU.mult, [k, "cst"], ["rhsA0", "rhsA1"])
                for half in range(2):
                    ps, pk = self.ps()
                    for r in range(4):
                        self.MM(ps[0:L, r * L:(r + 1) * L], self.SL_f[0:L, 0:L], rhsA[0:L, half * 4 + r, 0:L], True, True,
                                ["rhsA%d" % half, "cst"], [pk])
                    self.ACT(Eg[0:L, half * 4:half * 4 + 4, 0:L], ps[0:L, 0:4 * L].rearrange("p (r i) -> p r i", r=4), AF.Exp,
                             [pk], ["rhsA%d" % half])
                self.TT(Eg[0:L, :, 0:L], Eg[0:L, :, 0:L], cbm[0:L, 0:L].unsqueeze(1).to_broadcast([L, 8, L]), ALU.mult,
                        ["rhsA0", "rhsA1", "cbm"], ["rhsA0", "rhsA1"])
                self.TT(Wt[0:L, :, 0:L], Eg[0:L, :, 0:L], dt[0:L, h0:h0 + 8].unsqueeze(2).to_broadcast([L, 8, L]), ALU.mult,
                        ["rhsA0", "rhsA1", k], ["Wt"])
                psA, pkA = self.ps()
                for r in range(8):
                    self.MM(psA[0:L, r * 64:(r + 1) * 64], Wt[0:L, r, 0:L], xs_tok[0:L, r * 64:(r + 1) * 64], True, True,
                            ["Wt", "xs_tok"], [pkA])
                psB, pkB = self.ps()
                self.MM(psB[0:L, :], CT[:, c0:c0 + L], Sb, True, True, ["bcT", "Sgbb"], [pkB])
                self.TT(t1[0:L, :].rearrange("p (r q) -> p r q", r=8), psB[0:L, :].rearrange("p (r q) -> p r q", r=8),
                        ea[0:L, h0:h0 + 8].unsqueeze(2).to_broadcast([L, 8, 64]), ALU.mult, [pkB, k], ["t1"])
                self.TT(t2[0:L, :].rearrange("p (r q) -> p r q", r=8), xs_tok[0:L, :].rearrange("p (r q) -> p r q", r=8),
                        vR[0:L, R_DSK + h0:R_DSK + h0 + 8].unsqueeze(2).to_broadcast([L, 8, 64]), ALU.mult,
                        ["xs_tok", "vecR"], ["t2"])
                self.TT(yb[0:L, :], psA[0:L, :], t1[0:L, :], ALU.add, [pkA, "t1"], ["yb"])
                self.TT(yb[0:L, :], yb[0:L, :], t2[0:L, :], ALU.add, ["yb", "t2"], ["yb"])
                self.TT(xw[0:L, :].rearrange("p (r q) -> p r q", r=8), xs_tok[0:L, :].rearrange("p (r q) -> p r q", r=8),
                        te[0:L, h0:h0 + 8].unsqueeze(2).to_broadcast([L, 8, 64]), ALU.mult, ["xs_tok", k], ["xw"])
                psC, pkC = self.ps()
                self.MM(psC[:, :], B_tok[0:L, :], xw[0:L, :], True, True, ["B_tok", "xw"], [pkC])
                self.TT(S.rearrange("p (r q) -> p r q", r=8), S.rearrange("p (r q) -> p r q", r=8),
                        cd[:, h0:h0 + 8].unsqueeze(2).to_broadcast([128, 8, 64]), ALU.mult, [Sk, k], [Sk])
                self.TT(S, S, psC[:, :], ALU.add, [Sk, pkC], [Sk])
                self.TT(yb[0:L, :], yb[0:L, :], sz[0:L, si, :], ALU.mult, ["yb", "sz"], ["yb"])
                ss = self.small[:, 0:1]
                self.ACT(t1[0:L, :], yb[0:L, :], AF.Square, ["yb", "t1"], ["t1", "ss"], accum=ss[0:L])
                self.TS(ss[0:L], ss[0:L], 1.0 / 512, EPS, ALU.mult, ALU.add, ["ss"], ["ss"])
                self.RSQ(ss[0:L], "ss")
                self.STT(ytok[0:L, :], yb[0:L, :], ss[0:L], vR[0:L, R_SNORM + g * 512:R_SNORM + (g + 1) * 512], ALU.mult, ALU.mult,
                         ["yb", "ss", "vecR"], ["ytok"])
                ps, pk = self.ps()
                for m in range(4):
                    self.MM(ps[:, m * L:(m + 1) * L], ytok[0:L, m * 128:(m + 1) * 128], self.identb[0:L, 0:L], True, True,
                            ["ytok", "identb"], [pk])
                self.CP(ysnT[:, 4 * g:4 * g + 4, c0:c0 + L], ps[:, 0:4 * L].rearrange("p (m i) -> p m i", m=4), [pk], ["ysnT"], en="act")
            if prompt:
                prs = [(self.run_ssd[:, g * 512:(g + 1) * 512], Sg[0])]
                if T["step"] >= 5:
                    prs.append((self.o_ssd[T["step"] - 5, :, g * 512:(g + 1) * 512], Sg[0]))
                self.dma("sp", prs, ["Sg0"], ["run_ssd"], "run_ssd")
            else:
                for s in range(NSAMP):
                    self.dma("sp", [(self.o_ssd[2 + NSAMP * T["step"] + s, :, g * 512:(g + 1) * 512], Sg[s])], ["Sg%d" % s], [], "o_ssds")
        if prompt:
            if T["step"] >= 5:
                self.dma("sp", [(self.o_sconv[T["step"] - 5], self.cv_s[:, l, :])], ["cv_s"], [], "o_cv")
        else:
            self.dma("sp", [(self.o_sconv[2 + NSAMP * T["step"]:2 + NSAMP * T["step"] + NSAMP].rearrange("s p n -> p s n"), self.ho[:, :, 0:72])], ["ho"], [], "o_cv")
        self.barrier()
        self.gla(T, l, nslab, ygnT)
        self.barrier()
        self.merge(T, l, nslab, ysnT, ygnT, mg)

    def gla(self, T, l, nslab, ygnT):
        TT, segs = T["TT"], T["segs"]
        prompt = T["kind"] == "p"
        nseg = len(segs)
        Bn = self.Bn
        A = self.A
        pools = [[Bn, 0, 12288], [A, 16384, 24576]]

        def carve(n):
            for pl in pools:
                if pl[1] + n <= pl[2]:
                    a = pl[0][:, pl[1]:pl[1] + n]
                    pl[1] += n
                    return a
            raise AssertionError("arena overflow")
        cum = carve(4 * TT).bitcast(F32).rearrange("p (c t) -> p c t", c=2)
        ex = [carve(2 * TT).bitcast(F32) for _ in range(2)]
        qd = carve(2 * TT).rearrange("p (c t) -> p c t", c=2)
        ki = carve(2 * TT).rearrange("p (c t) -> p c t", c=2)
        ke = carve(2 * TT).rearrange("p (c t) -> p c t", c=2)
        vt = carve(nseg * 512).rearrange("p (s n) -> p s n", s=nseg)
        sg = carve(nseg * 512).rearrange("p (s n) -> p s n", s=nseg)
        ketok = carve(256)
        att = carve(128)
        ob = carve(1024).bitcast(F32)
        osq = carve(1024).bitcast(F32)
        otok = carve(512)
        ns = 1 if prompt else NSAMP
        S = [carve(2048).bitcast(F32).rearrange("p (c v) -> p c v", c=2) for _ in range(ns)]
        Sb1 = carve(1024).rearrange("p (c v) -> p c v", c=2)
        dec = carve(64).bitcast(F32)
        zeros = carve(256).bitcast(F32)
        vR = self.vecR
        self.MS(zeros, 0.0, ["zeros"])
        for hd in range(4):
            for cc in range(2):
                ch = 2 * hd + cc
                ps, pk = self.ps()
                self.MM(ps[:, 0:TT], self.wa2[:, ch * 128:(ch + 1) * 128], self.alrT[:, 0:TT], True, True, ["wa2", "alrT"], [pk])
                e = ex[cc][:, 0:TT]
                self.ACT(e, ps[:, 0:TT], AF.Exp, [pk, "nba"], ["ex%d" % cc], bias=self.nba[:, ch:ch + 1], scale=-1.0)
                self.ACT(e, e, AF.Ln, ["ex%d" % cc], ["ex%d" % cc], bias=1.0)
                for si, (c0, L) in enumerate(segs):
                    self._pre("dve", ["ex%d" % cc, "zeros"], ["cum"])
                    i = self.nc.vector.tensor_tensor_scan(out=cum[:, cc, c0:c0 + L], data0=e[:, c0:c0 + L], data1=zeros[:, 0:L],
                                                          initial=0.0, op0=ALU.add, op1=ALU.add)
                    self.op("dve", i, ["ex%d" % cc, "zeros"], ["cum"])
                    self.ACT(dec[:, cc * nseg + si: cc * nseg + si + 1], cum[:, cc, c0 + L - 1:c0 + L], AF.Exp, ["cum"], ["dec"],
                             scale=-1.0 / 16)
            w, wk = nslab()
            for m, ps, pk in self.fm(T, w, wk, lambda kc: self.h[:, kc, 0:TT], ["h"]):
                cc = m % 2
                e = ex[m % 2][:, 0:TT]
                if m < 2:
                    self.ACT(e, cum[:, cc, 0:TT], AF.Exp, ["cum"], ["ex%d" % (m % 2)], scale=-1.0 / 16)
                    self.STT(qd[:, cc, 0:TT], ps[:, 0:TT], 1.0 / 16, e, ALU.mult, ALU.mult, [pk, "ex%d" % (m % 2)], ["qd"])
                else:
                    self.ACT(e, cum[:, cc, 0:TT], AF.Exp, ["cum"], ["ex%d" % (m % 2)], scale=1.0 / 16)
                    self.TT(ki[:, cc, 0:TT], ps[:, 0:TT], e, ALU.mult, [pk, "ex%d" % (m % 2)], ["ki"])
                    for si, (c0, L) in enumerate(segs):
                        self.TS(ke[:, cc, c0:c0 + L], ki[:, cc, c0:c0 + L], dec[:, cc * nseg + si: cc * nseg + si + 1], None,
                                ALU.mult, None, ["ki", "dec"], ["ke"])
            w, wk = nslab()
            for si, c0, L, ps, pk in self.tm(T, w, wk):
                self.CP(vt[0:L, si, :], ps[0:L, :], [pk], ["vt"], en="act")
            w, wk = nslab()
            for si, c0, L, ps, pk in self.tm(T, w, wk):
                self.ACT(sg[0:L, si, :], ps[0:L, :], AF.Silu, [pk], ["sg"])
            gsrc = lambda ap: ap.rearrange("(c p) v -> p c v", p=128)[:, 2 * hd:2 * hd + 2, :]
            if prompt:
                self.dma("sp", [(S[0], gsrc(self.run_gla))], ["run_gla"], ["S0"], "S0")
                self.TS(S[0], S[0], self.flg[:, 1 + T["step"]:2 + T["step"]], None, ALU.mult, None, ["S0", "flg"], ["S0"])
            else:
                for s in range(NSAMP):
                    self.dma("sp", [(S[s], gsrc(self.s_gla_d[l, s]))], [], ["S%d" % s], "S%d" % s)
            for si, (c0, L) in enumerate(segs):
                sq_ = 0 if prompt else si
                St, Sbt, Sk = S[sq_], Sb1, "S%d" % sq_
                self.CP(Sbt, St, [Sk], ["Sbb"], en="act")
                ps, pk = self.ps()
                for cc in range(2):
                    self.MM(ps[0:L, cc * 128:(cc + 1) * 128], ke[:, cc, c0:c0 + L], self.identb[:], True, True, ["ke", "identb"], [pk])
                self.CP(ketok[0:L, :], ps[0:L, 0:256], [pk], ["ketok"], en="act")
                ps, pk = self.ps()
                for cc in range(2):
                    self.MM(ps[0:L, 0:L], ki[:, cc, c0:c0 + L], qd[:, cc, c0:c0 + L], cc == 0, cc == 1, ["ki", "qd"], [pk])
                self.TT(att[0:L, 0:L], ps[0:L, 0:L], self.U_f[0:L, 0:L], ALU.mult, [pk, "cst"], ["att"])
                psO, pkO = self.ps()
                self.MM(psO[0:L, :], att[0:L, 0:L], vt[0:L, si, :], True, False, ["att", "vt"], [pkO])
                for cc in range(2):
                    self.MM(psO[0:L, :], qd[:, cc, c0:c0 + L], Sbt[:, cc, :], False, cc == 1, ["qd", "Sbb"], [pkO])
                for cc in range(2):
                    psS, pkS = self.ps()
                    self.MM(psS[:, :], ketok[0:L, cc * 128:(cc + 1) * 128], vt[0:L, si, :], True, True, ["ketok", "vt"], [pkS])
                    self.STT(St[:, cc, :], St[:, cc, :], dec[:, cc * nseg + si: cc * nseg + si + 1], psS[:, :], ALU.mult, ALU.add,
                             [Sk, "dec", pkS], [Sk])
                ss = self.small[:, 1:2]
                self.ACT(osq[0:L, :], psO[0:L, :], AF.Square, [pkO], ["osq", "ss2"], accum=ss[0:L])
                self.TS(ss[0:L], ss[0:L], 1.0 / 512, EPS, ALU.mult, ALU.add, ["ss2"], ["ss2"])
                self.RSQ(ss[0:L], "ss2")
                self.STT(ob[0:L, :], psO[0:L, :], ss[0:L], vR[0:L, R_GNORM:R_GNORM + 512], ALU.mult, ALU.mult,
                         [pkO, "ss2", "vecR"], ["ob"])
                self.TT(otok[0:L, :], ob[0:L, :], sg[0:L, si, :], ALU.mult, ["ob", "sg"], ["otok"])
                ps, pk = self.ps()
                for m in range(4):
                    self.MM(ps[:, m * L:(m + 1) * L], otok[0:L, m * 128:(m + 1) * 128], self.identb[0:L, 0:L], True, True,
                            ["otok", "identb"], [pk])
                self.CP(ygnT[:, 4 * hd:4 * hd + 4, c0:c0 + L], ps[:, 0:4 * L].rearrange("p (m i) -> p m i", m=4), [pk], ["ygnT"], en="act")
            gdst = lambda ap: ap.rearrange("(c p) v -> p c v", p=128)[:, 2 * hd:2 * hd + 2, :]
            if prompt:
                prs = [(gdst(self.run_gla), S[0])]
                if T["step"] >= 5:
                    prs.append((gdst(self.o_gla[T["step"] - 5]), S[0]))
                self.dma("sp", prs, ["S0"], ["run_gla"], "run_gla")
            else:
                for s in range(NSAMP):
                    self.dma("sp", [(gdst(self.o_gla[2 + NSAMP * T["step"] + s]), S[s])], ["S%d" % s], [], "o_glas")

    def merge(self, T, l, nslab, ysnT, ygnT, mg):
        TT = T["TT"]
        Bn = self.Bn
        sgt = Bn[:, 0:2048].rearrange("p (c t) -> p c t", c=4)
        acc = Bn[:, 2048:2048 + 4096].bitcast(F32).rearrange("p (c t) -> p c t", c=4)
        hfn = lambda kc: self.h[:, kc, 0:TT]
        for j in range(4):
            for n, src, sk in ((0, ysnT, "ysnT"), (1, ygnT, "ygnT")):
                w, wk = nslab()
                for m, ps, pk in self.fm(T, w, wk, hfn, ["h"]):
                    self.ACT(sgt[:, m, 0:TT], ps[:, 0:TT], AF.Sigmoid, [pk], ["sgt%d" % m])
                w, wk = nslab()
                for m, ps, pk in self.fm(T, w, wk, lambda kc: src[:, kc, 0:TT], [sk]):
                    if n == 0:
                        self.TT(acc[:, m, 0:TT], ps[:, 0:TT], sgt[:, m, 0:TT], ALU.mult, [pk, "sgt%d" % m], ["acc%d" % m])
                    else:
                        self.TT(self.rs[:, 0:TT], ps[:, 0:TT], sgt[:, m, 0:TT], ALU.mult, [pk, "sgt%d" % m], ["rs"])
                        self.TT(mg[:, 4 * j + m, 0:TT], self.rs[:, 0:TT], acc[:, m, 0:TT], ALU.add, ["rs", "acc%d" % m], ["mg"])
        for j in range(4):
            w, wk = nslab()
            for m, ps, pk in self.fm(T, w, wk, lambda kc: mg[:, kc, 0:TT], ["mg"]):
                c = 4 * j + m
                self.TT(self.x[:, c, 0:TT], self.x[:, c, 0:TT], ps[:, 0:TT], ALU.add, [pk, "x%d" % c], ["x%d" % c])
        self.sidx_after_mixer = None

    def attn(self, T, l):
        TT, segs = T["TT"], T["segs"]
        prompt = T["kind"] == "p"
        A, Bn = self.A, self.Bn
        qT = A[:, 0:8192].rearrange("p (c t) -> p c t", c=16)
        oT = A[:, 8192:16384].rearrange("p (c t) -> p c t", c=16)
        ngrp = 1 if prompt else NSAMP
        kvreg = [Bn[:, 0:8192], A[:, 16384:24576]]
        kTb = [kvreg[g % 2][:, 0:4096].rearrange("p (c m) -> p c m", c=16) for g in range(ngrp)]
        vb = [kvreg[g % 2][:, 4096:8192].rearrange("p (c n) -> p c n", c=2) for g in range(ngrp)]
        base = 8192
        sc = Bn[:, base:base + 512].bitcast(F32)
        pb = Bn[:, base + 512:base + 768]
        pT = Bn[:, base + 768:base + 768 + 1024].rearrange("p (c t) -> p c t", c=2)
        sm = self.small

        def load_kv(g):
            if prompt:
                ksrc, vsrc, rk = self.o_kT[l], self.o_v[l], ["o_kT%d" % l, "o_v%d" % l]
            else:
                ksrc, vsrc, rk = self.s_kT_d[l, g], self.s_v_d[l, g], []
            self.dma("pool", [(kTb[g], ksrc.rearrange("(c p) m -> p c m", p=128)),
                              (vb[g], vsrc.rearrange("(c p) n -> p c n", p=128))], rk, ["kv%d" % (g % 2), "sq0", "sq1"], "kvl%d" % (g % 2))
        load_kv(0)
        sidx = [42]

        def nslab():
            w, wk = self.slab("L%d_%d" % (l, sidx[0]))
            sidx[0] += 1
            return w, wk
        for j in range(4):
            w, wk = nslab()
            for m, ps, pk in self.fm(T, w, wk, lambda kc: self.h[:, kc, 0:TT], ["h"]):
                self.ACT(qT[:, 4 * j + m, 0:TT], ps[:, 0:TT], AF.Copy, [pk], ["qT"], scale=float(512 ** -0.5))
        groups = [(0, segs)] if prompt else [(s, [segs[s]]) for s in range(NSAMP)]
        for g, gsegs in groups:
            if g + 1 < ngrp:
                load_kv(g + 1)
            g0 = gsegs[0][0]
            gl = sum(L for _, L in gsegs)
            for hd in range(4):
                for (c0, L) in gsegs:
                    ps, pk = self.ps()
                    for cc in range(4):
                        self.MM(ps[0:L, 0:256], qT[:, 4 * hd + cc, c0:c0 + L], kTb[g][:, 4 * hd + cc, :], cc == 0, cc == 3,
                                ["qT", "kv%d" % (g % 2)], [pk])
                    mx = sm[:, 2:3]
                    self._pre("dve", [pk], ["mx"])
                    i = self.nc.vector.reduce_max(out=mx[0:L], in_=ps[0:L, 0:256], axis=AX.X)
                    self.op("dve", i, [pk], ["mx"])
                    self.TS(mx[0:L], mx[0:L], -1.0, None, ALU.mult, None, ["mx"], ["mx"])
                    sme = sm[:, 3:4]
                    self.ACT(sc[0:L, :], ps[0:L, 0:256], AF.Exp, [pk, "mx"], ["sc", "sme"], bias=mx[0:L], accum=sme[0:L])
                    self._pre("dve", ["sme"], ["sme"])
                    i = self.nc.vector.reciprocal(out=sme[0:L], in_=sme[0:L])
                    self.op("dve", i, ["sme"], ["sme"])
                    self.TS(pb[0:L, :], sc[0:L, :], sme[0:L], None, ALU.mult, None, ["sc", "sme"], ["pb"])
                    ps2, pk2 = self.ps()
                    for mc in range(2):
                        self.MM(ps2[:, mc * L:(mc + 1) * L], pb[0:L, mc * 128:(mc + 1) * 128], self.identb[0:L, 0:L], True, True,
                                ["pb", "identb"], [pk2])
                    self.CP(pT[:, :, c0 - g0:c0 - g0 + L], ps2[:, 0:2 * L].rearrange("p (c i) -> p c i", c=2), [pk2], ["pT"], en="act")
                for cc in range(4):
                    ps, pk = self.ps()
                    for mc in range(2):
                        self.MM(ps[:, 0:gl], vb[g][:, mc, (4 * hd + cc) * 128:(4 * hd + cc + 1) * 128], pT[:, mc, 0:gl], mc == 0, mc == 1,
                                ["kv%d" % (g % 2), "pT"], [pk])
                    self.CP(oT[:, 4 * hd + cc, g0:g0 + gl], ps[:, 0:gl], [pk], ["oT"], en="act")
        for j in range(4):
            w, wk = nslab()
            for m, ps, pk in self.fm(T, w, wk, lambda kc: oT[:, kc, 0:TT], ["oT"]):
                c = 4 * j + m
                self.TT(self.x[:, c, 0:TT], self.x[:, c, 0:TT], ps[:, 0:TT], ALU.add, [pk, "x%d" % c], ["x%d" % c])

    def ffn(self, T, l):
        TT, nrun, RL = T["TT"], T["nrun"], T["RL"]
        prompt = T["kind"] == "p"
        A, Bn = self.A, self.Bn
        act = A[:, 0:24 * 512].rearrange("p (c t) -> p c t", c=24)
        raw = [Bn[:, i * 1040:(i + 1) * 1040].bitcast(F32) for i in range(2)]
        cacc = [Bn[:, 2080 + i * 1024:2080 + (i + 1) * 1024].bitcast(F32) for i in range(2)]
        cu = Bn[:, 4128:4128 + 4096].bitcast(F32).rearrange("p (c t) -> p c t", c=4)
        if not prompt:
            self.dma("sp", [(self.hs[:, :, :], self.s_fconv_d[l].rearrange("s p n -> p s n"))], [], ["hs"], "hs")
        hist = (lambda oc: self.cv_f[:, l, oc * 2:oc * 2 + 2].unsqueeze(1)) if prompt else (lambda oc: self.hs[:, :, oc * 2:oc * 2 + 2])
        nh = (lambda oc: self.cv_f[:, l, oc * 2:oc * 2 + 2].unsqueeze(1)) if prompt else (lambda oc: self.ho[:, :, oc * 2:oc * 2 + 2])
        hk, nk_ = ("cv_f", "cv_f") if prompt else ("hs", "ho")
        sidx = [50]
        cnt = 0
        hfn = lambda kc: self.h[:, kc, 0:TT]
        for hf, (j0, j1) in enumerate(((0, 6), (6, 11))):
            nkc = 4 * (j1 - j0)
            for j in range(j0, j1):
                for part in range(2):
                    w, wk = self.slab("L%d_%d" % (l, sidx[0]))
                    sidx[0] += 1
                    for m, ps, pk in self.fm(T, w, wk, hfn, ["h"]):
                        oc = part * 44 + 4 * j + m
                        b = cnt % 2
                        cnt += 1
                        av = self.conv(T, ps, pk, 3, P_FCW, 88, P_FCB, oc, hist(oc), hk, nh(oc), nk_,
                                       raw[b], "raw%d" % b, cacc[b], "cacc%d" % b)
                        cuv = cu[:, m, 0:TT].rearrange("p (r t) -> p r t", r=nrun)
                        if part == 0:
                            self.CP(cuv, av, ["cacc%d" % b], ["cu%d" % m], en="act")
                        else:
                            self.ACT(av, av, AF.Silu, ["cacc%d" % b], ["cacc%d" % b])
                            self.TT(act[:, 4 * (j - j0) + m, 0:TT].rearrange("p (r t) -> p r t", r=nrun), av, cuv, ALU.mult,
                                    ["cacc%d" % b, "cu%d" % m], ["act"])
            for j in range(8):
                w, wk = self.slab("F%d_%d_%d" % (l, hf, j))
                for m, ps, pk in self.fm(T, w, wk, lambda kc: act[:, kc, 0:TT], ["act"], nk=nkc, nch=2, nw=256):
                    c = 2 * j + m
                    self.TT(self.x[:, c, 0:TT], self.x[:, c, 0:TT], ps[:, 0:TT], ALU.add, [pk, "x%d" % c], ["x%d" % c])
                if hf == 1 and j % 4 == 3 and T["step"] < NSTEP - 1:
                    q = j // 4
                    send, gath, gk = (self.sendS, self.gathS, "gathS") if T["kind"] == "s" else (self.sendP, self.gathP, "gathP")
                    self.dma("sp", [(send[q].rearrange("p (c t) -> p c t", c=8), self.x[:, 8 * q:8 * q + 8, 0:TT])],
                             ["x%d" % c_ for c_ in range(8 * q, 8 * q + 8)], ["send%d" % q], "send%d" % q)
                    self.cc(send[q], gath[q], ["send%d" % q], [gk])
        if prompt:
            if T["step"] >= 5:
                self.dma("sp", [(self.o_fconv[T["step"] - 5], self.cv_f[:, l, :])], ["cv_f"], [], "o_cv")
        else:
            self.dma("sp", [(self.o_fconv[2 + NSAMP * T["step"]:2 + NSAMP * T["step"] + NSAMP].rearrange("s p n -> p s n"), self.ho[:, :, :])], ["ho"], [], "o_cv")


def _slab(wcols):
    n = wcols.shape[1]
    return np.ascontiguousarray(wcols.reshape(16, 128, n).transpose(1, 0, 2).reshape(128, 16 * n))


def _layer_slabs(w_in, w_branch, w_out, w_mq, w_mo, w_ffn_in):
    xs_ch, bc_ch = xbc_slab_chunks()
    xbc = w_in[:, O_XBC:O_XBC + 3072]
    sl = []
    pick = lambda chs: np.concatenate([xbc[:, c * 128:(c + 1) * 128] for c in chs], axis=1)
    for g in range(4):
        sl.append(pick(xs_ch[g]))
        if g % 2 == 0:
            sl.append(pick(bc_ch[g // 2]))
        sl.append(w_in[:, O_Z + g * 512:O_Z + (g + 1) * 512])
    for hd in range(4):
        sl.append(np.concatenate([w_in[:, O_Q + hd * 256:O_Q + (hd + 1) * 256], w_in[:, O_K + hd * 256:O_K + (hd + 1) * 256]], axis=1))
        sl.append(w_in[:, O_V + hd * 512:O_V + (hd + 1) * 512])
        sl.append(w_in[:, O_G + hd * 512:O_G + (hd + 1) * 512])
    for j in range(4):
        for n in range(2):
            sl.append(w_in[:, O_GATES + n * 2048 + j * 512:O_GATES + n * 2048 + (j + 1) * 512])
            sl.append(w_branch[n][:, j * 512:(j + 1) * 512])
    for j in range(4):
        sl.append(w_out[:, j * 512:(j + 1) * 512])
    for j in range(4):
        sl.append(w_mq[:, j * 512:(j + 1) * 512])
    for j in range(4):
        sl.append(w_mo[:, j * 512:(j + 1) * 512])
    for j in range(11):
        sl.append(w_ffn_in[:, j * 512:(j + 1) * 512])
        sl.append(w_ffn_in[:, DFF + j * 512:DFF + (j + 1) * 512])
    assert len(sl) == NSLAB
    return np.stack([_slab(s) for s in sl])


def _pvec(v):
    return v.reshape(-1, 128).T


_NC_CACHE = {}


def kernel(x_prompt, x_sample, mem_prompt, state_ssd, state_ssd_conv, state_gla, state_ffn_conv,
           cache_mem_k, cache_mem_v, norm_mix, w_in, ssd_conv_w, ssd_conv_b, ssd_dt_bias, ssd_a_log,
           ssd_d, ssd_norm, gla_wa2, gla_ba, gla_norm, w_branch, w_out, norm_mem, w_mq, w_mk, w_mv,
           w_mo, norm_ffn, w_ffn_in, ffn_conv_w, ffn_conv_b, w_ffn_out, norm_final, _steps=NSTEP, _pairs=4):
    f = lambda a: np.asarray(a, dtype=np.float32)
    (x_prompt, x_sample, mem_prompt, state_ssd, state_ssd_conv, state_gla, state_ffn_conv, cache_mem_k, cache_mem_v,
     norm_mix, w_in, ssd_conv_w, ssd_conv_b, ssd_dt_bias, ssd_a_log, ssd_d, ssd_norm, gla_wa2, gla_ba, gla_norm,
     w_branch, w_out, norm_mem, w_mq, w_mk, w_mv, w_mo, norm_ffn, w_ffn_in, ffn_conv_w, ffn_conv_b, w_ffn_out,
     norm_final) = map(f, (x_prompt, x_sample, mem_prompt, state_ssd, state_ssd_conv, state_gla, state_ffn_conv,
                           cache_mem_k, cache_mem_v, norm_mix, w_in, ssd_conv_w, ssd_conv_b, ssd_dt_bias, ssd_a_log,
                           ssd_d, ssd_norm, gla_wa2, gla_ba, gla_norm, w_branch, w_out, norm_mem, w_mq, w_mk, w_mv,
                           w_mo, norm_ffn, w_ffn_in, ffn_conv_w, ffn_conv_b, w_ffn_out, norm_final))
    rg = [[p, p + _pairs] for p in range(_pairs)]
    key = (_steps, _pairs)
    if key not in _NC_CACHE:
        _NC_CACHE[key] = Builder(_steps, rg).build()
    nc = _NC_CACHE[key]

    cst = np.zeros((128, 512), np.float32)
    idx = np.arange(128)
    cst[:, 0:128] = np.eye(128, dtype=np.float32)
    cst[:, 128:256] = (idx[:, None] <= idx[None, :])
    cst[:, 256:384] = (idx[:, None] > idx[None, :])
    cst[:, 384:512] = 1.0
    lay = []
    for l in range(DEPTH):
        sh = {"cst": cst}
        sh["W0"] = _layer_slabs(w_in[l], w_branch[l], w_out[l], w_mq[l], w_mo[l], w_ffn_in[l])
        wf = np.zeros((2, 8, 128, 24 * 256), np.float32)
        for hf, (k0, k1) in enumerate(((0, 24), (24, 44))):
            for j in range(8):
                blk = w_ffn_out[l][k0 * 128:k1 * 128, j * 256:(j + 1) * 256].reshape(k1 - k0, 128, 256).transpose(1, 0, 2)
                wf[hf, j, :, 0:(k1 - k0) * 256] = blk.reshape(128, (k1 - k0) * 256)
        sh["WF0"] = wf
        sh["WKV0"] = np.stack([_slab(w_mk[l][:, j * 512:(j + 1) * 512]) for j in range(4)] +
                              [_slab(w_mv[l][:, j * 512:(j + 1) * 512]) for j in range(4)])
        sh["wsm0"] = _slab(np.concatenate([w_in[l][:, O_DT:O_DT + 32], w_in[l][:, O_ALR:O_ALR + 16]], axis=1))
        sh["wa2_0"] = np.ascontiguousarray(gla_wa2[l])
        vp = np.zeros((128, NVP), np.float32)
        vp[:, P_NMIX:P_NMIX + 16] = _pvec(norm_mix[l])
        vp[:, P_NMEM:P_NMEM + 16] = _pvec(norm_mem[l])
        vp[:, P_NFFN:P_NFFN + 16] = _pvec(norm_ffn[l])
        vp[:, P_NFIN:P_NFIN + 16] = _pvec(norm_final)
        for tap in range(4):
            vp[:, P_SCW + tap * 24:P_SCW + (tap + 1) * 24] = _pvec(ssd_conv_w[l, tap])
        vp[:, P_SCB:P_SCB + 24] = _pvec(ssd_conv_b[l])
        for tap in range(3):
            vp[:, P_FCW + tap * 88:P_FCW + (tap + 1) * 88] = _pvec(ffn_conv_w[l, tap])
        vp[:, P_FCB:P_FCB + 88] = _pvec(ffn_conv_b[l])
        vp[:, P_BA:P_BA + 8] = _pvec(gla_ba[l])
        sh["vecP0"] = vp
        vr = np.concatenate([ssd_dt_bias[l], ssd_a_log[l], ssd_d[l], ssd_norm[l], gla_norm[l]])
        sh["vecR0"] = np.ascontiguousarray(np.broadcast_to(vr[None, :], (128, NVR)))
        lay.append(sh)

    maps = {}
    for l in range(DEPTH):
        for p in range(_pairs):
            ss = [4 * p + i for i in range(NSAMP)]
            m = dict(lay[l])
            xT = np.zeros((D, TOK), np.float32)
            if l == 0:
                for i, s_ in enumerate(ss):
                    xT[:, 16 * i:16 * (i + 1)] = x_sample[s_].T
                xT[:, 64:64 + SEQ] = x_prompt[p].T
            m["xT"] = xT
            m["memT"] = np.ascontiguousarray(mem_prompt[p].T)
            flg = np.ones((128, 8), np.float32)
            flg[:, 0] = float(l)
            if l == 1:
                flg[:, 1 + 3] = 0.0
            m["flg"] = flg
            m["s_ssd"] = np.ascontiguousarray(np.stack([state_ssd[l, s_].reshape(2048, 128).T for s_ in ss])[None])
            m["s_sconv"] = np.ascontiguousarray(np.stack([state_ssd_conv[l, s_].reshape(3, 24, 128).transpose(2, 1, 0).reshape(128, 72)
                                                          for s_ in ss])[None])
            m["s_fconv"] = np.ascontiguousarray(np.stack([state_ffn_conv[l, s_].reshape(2, 88, 128).transpose(2, 1, 0).reshape(128, 176)
                                                          for s_ in ss])[None])
            m["s_gla"] = np.ascontiguousarray(np.stack([state_gla[l, s_].reshape(1024, 512) for s_ in ss])[None])
            m["s_kT"] = np.ascontiguousarray(np.stack([cache_mem_k[l, s_].reshape(256, 2048).T for s_ in ss])[None])
            m["s_v"] = np.ascontiguousarray(np.stack([cache_mem_v[l, s_].reshape(256, 2048) for s_ in ss])[None])
            maps[(l, p)] = m
    in_maps = [maps[(0, p)] for p in range(_pairs)] + [maps[(1, p)] for p in range(_pairs)]
    res = run_bass_kernel_spmd(nc, in_maps, core_ids=list(range(2 * _pairs))).results
    RA = lambda p: res[p % _pairs]
    RB = lambda p: res[_pairs + p % _pairs]
    R = lambda l, p: RA(p) if l == 0 else RB(p)

    B, NSEQ = 4, 16
    y_prompt = np.stack([RB(p)["yT"][:, 640:640 + SEQ].T for p in range(B)])
    y_sample = np.stack([RB(s // 4)["yT"][:, 64 + 16 * (s % 4):64 + 16 * (s % 4 + 1)].T for s in range(NSEQ)])

    def un_ssd(a):
        return a.T.reshape(32, 64, 128)

    def un_conv(a, w, nch):
        return a.reshape(128, nch, w).transpose(2, 1, 0).reshape(w, nch * 128)
    fin = lambda l: l
    sl = lambda l, s: 2 + NSAMP * l + s % 4
    p_ssd = np.stack([[un_ssd(R(l, p)["o_ssd"][fin(l)]) for p in range(B)] for l in range(DEPTH)])
    s_ssd = np.stack([[un_ssd(R(l, s // 4)["o_ssd"][sl(l, s)]) for s in range(NSEQ)] for l in range(DEPTH)])
    p_sconv = np.stack([[un_conv(R(l, p)["o_sconv"][fin(l)], 3, 24) for p in range(B)] for l in range(DEPTH)])
    s_sconv = np.stack([[un_conv(R(l, s // 4)["o_sconv"][sl(l, s)], 3, 24) for s in range(NSEQ)] for l in range(DEPTH)])
    p_gla = np.stack([[R(l, p)["o_gla"][fin(l)].reshape(4, 256, 512) for p in range(B)] for l in range(DEPTH)])
    s_gla = np.stack([[R(l, s // 4)["o_gla"][sl(l, s)].reshape(4, 256, 512) for s in range(NSEQ)] for l in range(DEPTH)])
    p_fconv = np.stack([[un_conv(R(l, p)["o_fconv"][fin(l)], 2, 88) for p in range(B)] for l in range(DEPTH)])
    s_fconv = np.stack([[un_conv(R(l, s // 4)["o_fconv"][sl(l, s)], 2, 88) for s in range(NSEQ)] for l in range(DEPTH)])
    p_mem_k = np.stack([[R(l, p)["o_kT"][0].T.reshape(256, 4, 512) for p in range(B)] for l in range(DEPTH)])
    p_mem_v = np.stack([[R(l, p)["o_v"][0].reshape(256, 4, 512) for p in range(B)] for l in range(DEPTH)])
    outs = (y_prompt, y_sample, p_ssd, p_sconv, p_gla, p_fconv, p_mem_k, p_mem_v, s_ssd, s_sconv, s_gla, s_fconv)
    return tuple(np.ascontiguousarray(o, dtype=np.float32) for o in outs)
```

```python
import numpy as np
from contextlib import ExitStack
import concourse.bass as bass
import concourse.mybir as mybir
from concourse.bass_utils import run_bass_kernel_spmd

F32 = mybir.dt.float32
BF16 = mybir.dt.bfloat16
AF = mybir.ActivationFunctionType
ALU = mybir.AluOpType
AX = mybir.AxisListType

D = 2048
KC = 16
SEQ = 2048
NSAMP = 4
SLEN = 16
TOK = 64 + 5 * 512
NY = 2 * 64 + 5 * 512
NSTEP = 7
DEPTH = 2
EPS = 1e-6
DFF = 5632
IN_DIM = 15408
O_Z, O_XBC, O_DT, O_Q, O_K, O_V, O_G, O_ALR, O_GATES = 0, 2048, 5120, 5152, 6176, 7200, 9248, 11296, 11312
P_NMIX, P_NMEM, P_NFFN, P_NFIN, P_SCW, P_SCB, P_FCW, P_FCB, P_BA, NVP = 0, 16, 32, 48, 64, 160, 184, 448, 536, 544
R_DTB, R_ALOG, R_DSK, R_SNORM, R_GNORM, NVR = 0, 32, 64, 96, 2144, 2656
NSLAB = 72
NW = 512


def xbc_slab_chunks():
    xs = [[4 * g + m for m in range(4)] for g in range(4)]
    bc = [[16 + 2 * j, 20 + 2 * j, 16 + 2 * j + 1, 20 + 2 * j + 1] for j in range(2)]
    return xs, bc


class E:
    def __init__(self, h, sem, name):
        self.h, self.sem, self.cnt, self.waited, self.name = h, sem, 0, {}, name


class Builder:
    def __init__(self, n_tiles=NSTEP, rg=None):
        self.n_tiles, self.n_layers = n_tiles, 1
        self.rg = rg or [[0, 4], [1, 5], [2, 6], [3, 7]]
        self.nc = nc = bass.Bass("TRN2", target_bir_lowering=False)
        self.es = ExitStack()
        self.st = {}
        self.dsem = {}
        self.engs = {}
        for nm, h in (("pe", nc.tensor), ("act", nc.scalar), ("dve", nc.vector), ("pool", nc.gpsimd), ("sp", nc.sync)):
            sem = self.es.enter_context(nc.semaphore("e_" + nm))
            self.engs[nm] = E(h, sem, nm)
        self.psi = 0

    def _pre(self, en, R, W):
        eng = self.engs[en]
        deps = {}

        def add(ev):
            if ev is None:
                return
            sem, val = ev
            k = id(sem)
            if k not in deps or deps[k][1] < val:
                deps[k] = (sem, val)
        for k in R:
            s = self.st.get(k)
            if s:
                add(s[0])
        for k in W:
            s = self.st.get(k)
            if s:
                add(s[0])
                for ev in s[1].values():
                    add(ev)
        for sem, val in deps.values():
            if en == "pe" and sem is eng.sem:
                continue
            if eng.waited.get(id(sem), 0) < val:
                eng.h.wait_ge(sem, val)
                eng.waited[id(sem)] = val

    def _post(self, ev, R, W):
        for k in R:
            s = self.st.setdefault(k, [None, {}])
            s[1][id(ev[0])] = ev
        for k in W:
            self.st[k] = [ev, {}]

    def op(self, en, inst, R, W):
        eng = self.engs[en]
        eng.cnt += 1
        inst.then_inc(eng.sem, 1)
        self._post((eng.sem, eng.cnt), R, W)

    def dma(self, en, pairs, R, W, sk):
        eng = self.engs[en]
        self._pre(en, R, W)
        if sk not in self.dsem:
            self.dsem[sk] = [self.es.enter_context(self.nc.semaphore("d_" + sk)), 0]
        d = self.dsem[sk]
        if d[1] > 0 and eng.waited.get(id(d[0]), 0) < d[1]:
            eng.h.wait_ge(d[0], d[1])
            eng.waited[id(d[0])] = d[1]
        for o, i in pairs:
            eng.h.dma_start(out=o, in_=i).then_inc(d[0], 16)
            d[1] += 16
        self._post((d[0], d[1]), R, W)

    def barrier(self, keep=("wsl0", "wsl1", "wsl2")):
        names = ("pe", "act", "dve", "sp")
        for en in names + ("pool",):
            eng = self.engs[en]
            for on in names:
                o = self.engs[on]
                if o is eng or o.cnt == 0:
                    continue
                if eng.waited.get(id(o.sem), 0) < o.cnt:
                    eng.h.wait_ge(o.sem, o.cnt)
                    eng.waited[id(o.sem)] = o.cnt
            for sk, d in self.dsem.items():
                if sk in keep or d[1] == 0 or (sk == "cc" and en != "pool"):
                    continue
                if eng.waited.get(id(d[0]), 0) < d[1]:
                    eng.h.wait_ge(d[0], d[1])
                    eng.waited[id(d[0])] = d[1]
        self.st = {k: v for k, v in self.st.items() if k in keep}

    def ACT(self, out, in_, func, R, W, bias=None, scale=None, accum=None):
        self._pre("act", R, W)
        kw = {}
        if bias is not None:
            kw["bias"] = bias
        if scale is not None:
            kw["scale"] = scale
        if accum is not None:
            kw["accum_out"] = accum
        i = self.nc.scalar.activation(out=out, in_=in_, func=func, **kw)
        self.op("act", i, R, W)

    def TT(self, out, in0, in1, op, R, W, en="dve"):
        self._pre(en, R, W)
        i = self.engs[en].h.tensor_tensor(out=out, in0=in0, in1=in1, op=op)
        self.op(en, i, R, W)

    def TS(self, out, in0, s1, s2, op0, op1, R, W, en="dve"):
        self._pre(en, R, W)
        if s2 is None:
            i = self.engs[en].h.tensor_scalar(out=out, in0=in0, scalar1=s1, scalar2=None, op0=op0)
        else:
            i = self.engs[en].h.tensor_scalar(out=out, in0=in0, scalar1=s1, scalar2=s2, op0=op0, op1=op1)
        self.op(en, i, R, W)

    def STT(self, out, in0, scalar, in1, op0, op1, R, W, en="dve"):
        self._pre(en, R, W)
        i = self.engs[en].h.scalar_tensor_tensor(out=out, in0=in0, scalar=scalar, in1=in1, op0=op0, op1=op1)
        self.op(en, i, R, W)

    def CP(self, out, in_, R, W, en="dve"):
        self._pre(en, R, W)
        if en == "act":
            i = self.nc.scalar.activation(out=out, in_=in_, func=AF.Copy)
        else:
            i = self.engs[en].h.tensor_copy(out=out, in_=in_)
        self.op(en, i, R, W)

    def RSQ(self, ap, key):
        self.ACT(ap, ap, AF.Sqrt, [key], [key])
        self._pre("dve", [key], [key])
        i = self.nc.vector.reciprocal(out=ap, in_=ap)
        self.op("dve", i, [key], [key])

    def MS(self, ap, val, W, en="dve"):
        self._pre(en, [], W)
        i = self.engs[en].h.memset(ap, val)
        self.op(en, i, [], W)

    def MM(self, out, lhsT, rhs, start, stop, R, W):
        self._pre("pe", R, W)
        i = self.nc.tensor.matmul(out, lhsT=lhsT, rhs=rhs, start=start, stop=stop)
        self.op("pe", i, R, W)

    def ps(self):
        i = self.psi
        self.psi = (self.psi + 1) % 8
        return self.psum[i], "ps%d" % i

    def slab(self, expect):
        q = self.wq
        i = self.wi
        assert q[i][0] == expect, (q[i][0], expect)
        depth = 2 if q[i][3] == "s" else 1
        for j in range(i, min(i + depth, len(q) - 1) + 1):
            if j < self.wissued:
                continue
            if q[j][4] in [q[k][4] for k in range(i, j)]:
                break
            name, src, n, kd, b = q[j]
            self.dma("pool", [(self.wsl[b][:, 0:n], src)], [], ["wsl%d" % b], "wsl%d" % b)
            self.wissued = j + 1
        self.wi += 1
        return self.wsl[q[i][4]], "wsl%d" % q[i][4]

    def build(self):
        nc = self.nc
        es = self.es
        dt_in = lambda name, shape: nc.dram_tensor(name, shape, F32, kind="ExternalInput").ap()
        dt_out = lambda name, shape: nc.dram_tensor(name, shape, F32, kind="ExternalOutput").ap()
        self.xT = dt_in("xT", [D, TOK])
        self.memT = dt_in("memT", [D, 256])
        self.cst_d = dt_in("cst", [128, 512])
        self.flg_d = dt_in("flg", [128, 8])
        self.W = [dt_in("W0", [NSLAB, 128, 8192])]
        self.WF = [dt_in("WF0", [2, 8, 128, 24 * 256])]
        self.WKV = [dt_in("WKV0", [8, 128, 8192])]
        self.wsm_d = [dt_in("wsm0", [128, 16 * 48])]
        self.wa2_d = [dt_in("wa2_0", [16, 1024])]
        self.vecP_d = [dt_in("vecP0", [128, NVP])]
        self.vecR_d = [dt_in("vecR0", [128, NVR])]
        self.s_ssd_d = dt_in("s_ssd", [1, NSAMP, 128, 2048])
        self.s_sconv_d = dt_in("s_sconv", [1, NSAMP, 128, 72])
        self.s_gla_d = dt_in("s_gla", [1, NSAMP, 1024, 512])
        self.s_fconv_d = dt_in("s_fconv", [1, NSAMP, 128, 176])
        self.s_kT_d = dt_in("s_kT", [1, NSAMP, D, 256])
        self.s_v_d = dt_in("s_v", [1, NSAMP, 256, D])
        self.yT = dt_out("yT", [D, NY])
        NO = 2 + 2 * NSAMP
        self.o_ssd = dt_out("o_ssd", [NO, 128, 2048])
        self.o_sconv = dt_out("o_sconv", [NO, 128, 72])
        self.o_gla = dt_out("o_gla", [NO, 1024, 512])
        self.o_fconv = dt_out("o_fconv", [NO, 128, 176])
        self.o_kT = dt_out("o_kT", [1, D, 256])
        self.o_v = dt_out("o_v", [1, 256, D])
        self.run_ssd = nc.dram_tensor("run_ssd", [128, 2048], F32).ap()
        self.run_gla = nc.dram_tensor("run_gla", [1024, 512], F32).ap()
        self.sendP = [nc.dram_tensor("sendP%d" % i, [128, 4096], F32).ap() for i in range(2)]
        self.gathP = [nc.dram_tensor("gathP%d" % i, [256, 4096], F32).ap() for i in range(2)]
        self.sendS = [nc.dram_tensor("sendS%d" % i, [128, 512], F32).ap() for i in range(2)]
        self.gathS = [nc.dram_tensor("gathS%d" % i, [256, 512], F32).ap() for i in range(2)]

        sb = lambda name, shape, dt: es.enter_context(nc.sbuf_tensor(name, shape, dt))
        self.x = sb("x", [128, 16, 512], F32)
        self.h = sb("h", [128, 16, 512], BF16)
        self.wsl = [sb("wsl0", [128, 8192], BF16), sb("wsl1", [128, 8192], BF16)]
        self.xflat = self.x[:].rearrange("p c t -> p (c t)")
        self.wsl.append(self.xflat[:, 1024:5120].bitcast(BF16))
        self.cst = sb("cstf", [128, 512], F32)
        self.identb = sb("identb", [128, 128], BF16)
        self.onesb = sb("onesb", [128, 128], BF16)
        self.vecP = sb("vecP", [128, NVP], F32)
        self.vecR = sb("vecR", [128, NVR], F32)
        self.nba = sb("nba", [128, 8], F32)
        self.arow = sb("arow", [128, 32], F32)
        self.wsm = sb("wsm", [128, 16 * 48], BF16)
        self.wa2 = sb("wa2", [16, 1024], BF16)
        self.alrT = sb("alrT", [16, 512], BF16)
        self.ssdv = sb("ssdv", [128, 4, 6, 32], F32)
        self.cv_s = sb("cv_s", [128, DEPTH, 72], F32)
        self.cv_f = sb("cv_f", [128, DEPTH, 176], F32)
        self.hs = sb("hs", [128, NSAMP, 176], F32)
        self.ho = sb("ho", [128, NSAMP, 176], F32)
        self.rs = sb("rs", [128, 512], F32)
        self.small = sb("small", [128, 64], F32)
        self.flg = sb("flgs", [128, 8], F32)
        self.A = sb("arenaA", [128, 24576], BF16)
        self.Bn = sb("arenaB", [128, 12288], BF16)
        self.psum = [es.enter_context(nc.psum_tensor("ps%d" % i, [128, 512], F32)) for i in range(8)]

        self.wq = []
        for l in range(self.n_layers):
            for j in range(8):
                self.wq.append(["kv%d_%d" % (l, j), self.WKV[l][j], 8192, "k"])
        for t in range(self.n_tiles):
            kd = "s" if t < 2 else "p"
            for l in range(self.n_layers):
                for j in range(50 + 12):
                    self.wq.append(["L%d_%d" % (l, j), self.W[l][j], 8192, kd])
                for j in range(8):
                    self.wq.append(["F%d_0_%d" % (l, j), self.WF[l][0, j], 24 * 256, kd])
                for j in range(62, NSLAB):
                    self.wq.append(["L%d_%d" % (l, j), self.W[l][j], 8192, kd])
                for j in range(8):
                    self.wq.append(["F%d_1_%d" % (l, j), self.WF[l][1, j][:, 0:20 * 256], 20 * 256, kd])
        cnt_, prev = 0, None
        for e in self.wq:
            if e[3] != prev:
                cnt_, prev = 0, e[3]
            e.append(cnt_ % (3 if e[3] == "s" else 2))
            cnt_ += 1
        self.wi = 0
        self.wissued = 0

        self.dma("sp", [(self.cst[:], self.cst_d)], [], ["cst"], "cst")
        self.dma("pool", [(self.identb[:], self.cst_d[:, 0:128]), (self.onesb[:], self.cst_d[:, 384:512])], [], ["identb"], "identb")
        self.ident_f = self.cst[:, 0:128]
        self.U_f = self.cst[:, 128:256]
        self.SL_f = self.cst[:, 256:384]
        self.ones_f = self.cst[:, 384:512]
        self.MS(self.cv_s[:], 0.0, ["cv_s"])
        self.MS(self.cv_f[:], 0.0, ["cv_f"])

        self.dma("sp", [(self.flg[:], self.flg_d)], [], ["flg"], "flg")
        z = self.A[:, 0:16384].bitcast(F32)
        self.MS(z, 0.0, ["zt"])
        v3 = lambda ap: ap.rearrange("(c p) t -> p c t", p=128)
        self.dma("sp", [(self.run_ssd, z[:, 0:2048]),
                        (v3(self.run_gla), z[:, 0:4096].rearrange("p (c v) -> p c v", c=8)),
                        ] + [(self.gathP[i][0:128, :], z[:, 4096 * i:4096 * (i + 1)]) for i in range(2)]
                        + [(self.gathS[i][0:128, :], z[:, 512 * i:512 * (i + 1)]) for i in range(2)],
                 ["zt"], ["run_ssd", "run_gla", "gathP", "gathS"], "zinit")
        self.barrier()
        self.prologue_kv()
        steps = []
        for st_ in range(NSTEP):
            if st_ < 2:
                steps.append(dict(kind="s", step=st_, TT=64, xcol=0, ycol=64 * st_, segs=[(16 * i, 16) for i in range(4)],
                                  nrun=4, RL=16))
            else:
                steps.append(dict(kind="p", step=st_, TT=512, xcol=64 + 512 * (st_ - 2), ycol=128 + 512 * (st_ - 2),
                                  segs=[(128 * i, 128) for i in range(4)], nrun=1, RL=512))
        for T in steps[: self.n_tiles]:
            self.run_tile(T)
        sp = self.engs["sp"]
        for en, e in self.engs.items():
            if e is not sp and e.cnt > 0:
                sp.h.wait_ge(e.sem, e.cnt)
        for sk, d in self.dsem.items():
            if d[1] > 0:
                (self.engs["pool"] if sk == "cc" else sp).h.wait_ge(d[0], d[1])
        return nc

    def prologue_kv(self):
        nc = self.nc
        mT = self.A[:, 0:16 * 256].rearrange("p (c m) -> p c m", c=16)
        stage = self.Bn[:, 0:4096].bitcast(F32).rearrange("p (a n) -> p a n", a=4)
        self.dma("pool", [(mT, self.memT.rearrange("(c p) m -> p c m", p=128))], [], ["mT"], "mT")
        for l in range(self.n_layers):
            for j in range(4):
                w, wk = self.slab("kv%d_%d" % (l, j))
                for m in range(4):
                    ps, pk = self.ps()
                    for kc in range(16):
                        self.MM(ps[:, 0:256], w[:, kc * NW + m * 128: kc * NW + m * 128 + 128], mT[:, kc, :], kc == 0, kc == 15,
                                [wk, "mT"], [pk])
                    self.CP(stage[:, m, 0:256], ps[:, 0:256], [pk], ["kvst"], en="act")
                self.dma("sp", [(self.o_kT[l, j * 512:(j + 1) * 512, :].rearrange("(m p) n -> p m n", p=128), stage[:, :, 0:256])],
                         ["kvst"], ["o_kT%d" % l], "o_kT%d" % l)
            for j in range(4):
                w, wk = self.slab("kv%d_%d" % (l, 4 + j))
                for mc in range(2):
                    ps, pk = self.ps()
                    for kc in range(16):
                        self.MM(ps[:, :], mT[:, kc, mc * 128:(mc + 1) * 128], w[:, kc * NW:(kc + 1) * NW], kc == 0, kc == 15,
                                [wk, "mT"], [pk])
                    self.CP(stage[:, mc, :], ps[:, :], [pk], ["kvst"], en="act")
                self.dma("sp", [(self.o_v[l, :, j * 512:(j + 1) * 512].rearrange("(m p) n -> p m n", p=128), stage[:, 0:2, :])],
                         ["kvst"], ["o_v%d" % l], "o_v%d" % l)
        self.barrier()

    def xv(self, T):
        if T["kind"] == "s":
            return self.xflat[:, 0:16 * T["TT"]].rearrange("p (c t) -> p c t", c=16)
        return self.x

    def cc(self, send, gath, R, W):
        self._pre("pool", R, W)
        if "cc" not in self.dsem:
            self.dsem["cc"] = [self.es.enter_context(self.nc.semaphore("ccsem")), 0]
        d = self.dsem["cc"]
        self.nc.gpsimd.collective_compute("AllGather", ALU.bypass, replica_groups=self.rg, ins=[send], outs=[gath]).then_inc(d[0], 1)
        d[1] += 1
        self._post((d[0], d[1]), R, W)

    def run_tile(self, T):
        TT, stp = T["TT"], T["step"]
        samp = T["kind"] == "s"
        xk = ["x%d" % c for c in range(16)]
        v3 = lambda ap: ap.rearrange("(c p) t -> p c t", p=128)
        xsrc = v3(self.xT)[:, :, T["xcol"]:T["xcol"] + TT]
        self.dma("sp", [(self.xv(T)[:, 0:8, 0:TT], xsrc[:, 0:8, :]), (self.xv(T)[:, 8:16, 0:TT], xsrc[:, 8:16, :])], [], xk, "x")
        gath, gk = (self.gathS, "gathS") if samp else (self.gathP, "gathP")
        send = self.sendS if samp else self.sendP
        xb = self.A[:, 0:16384].bitcast(F32).rearrange("p (c t) -> p c t", c=16)
        ng = len(gath)
        cpg = 16 // ng
        self.dma("pool", [(xb[:, cpg * i:cpg * (i + 1), 0:TT], gath[i][0:128, :].rearrange("p (c t) -> p c t", c=cpg)) for i in range(ng)],
                 [gk], ["xb"], "xb")
        for c in range(16):
            self.STT(self.xv(T)[:, c, 0:TT], xb[:, c, 0:TT], self.flg[:, 0:1], self.xv(T)[:, c, 0:TT], ALU.mult, ALU.add,
                     ["xb", "flg", "x%d" % c], ["x%d" % c])
        if not samp:
            rf = self.flg[:, 1 + stp:2 + stp]
            self.TS(self.cv_s[:, 0, :], self.cv_s[:, 0, :], rf, None, ALU.mult, None, ["cv_s", "flg"], ["cv_s"])
            self.TS(self.cv_f[:, 0, :], self.cv_f[:, 0, :], rf, None, ALU.mult, None, ["cv_f", "flg"], ["cv_f"])
        self.barrier()
        self.layer(T, 0)
        self.rms(T, P_NFIN, out_f32=True)
        ydst = v3(self.yT)[:, :, T["ycol"]:T["ycol"] + TT]
        yv = self.A[:, 0:16384].bitcast(F32).rearrange("p (c t) -> p c t", c=16)
        self.dma("sp", [(ydst, yv[:, :, 0:TT])], ["yout"], [], "yout")
        self.barrier()

    def rms(self, T, g0, out_f32=False):
        TT = T["TT"]
        ps, pk = self.ps()
        for c in range(16):
            sq = self.Bn[:, 0:1024].rearrange("p (a t) -> p a t", a=2)[:, c % 2, 0:TT]
            self.ACT(sq, self.xv(T)[:, c, 0:TT], AF.Square, ["x%d" % c], ["sq%d" % (c % 2)])
            self.MM(ps[:, 0:TT], self.onesb[:], sq, c == 0, c == 15, ["sq%d" % (c % 2), "identb"], [pk])
        rs = self.rs[:, 0:TT]
        self.TS(rs, ps[:, 0:TT], 1.0 / D, EPS, ALU.mult, ALU.add, [pk], ["rs"])
        self.RSQ(rs, "rs")
        if out_f32:
            yv = self.A[:, 0:16384].bitcast(F32).rearrange("p (c t) -> p c t", c=16)
        for c in range(16):
            o = yv[:, c, 0:TT] if out_f32 else self.h[:, c, 0:TT]
            self.STT(o, self.xv(T)[:, c, 0:TT], self.vecP[:, g0 + c:g0 + c + 1], rs, ALU.mult, ALU.mult,
                     ["x%d" % c, "rs", "vecP"], ["yout" if out_f32 else "h"])

    def layer(self, T, l):
        self.dma("sp", [(self.vecP[:], self.vecP_d[l]), (self.vecR[:], self.vecR_d[l])], [], ["vecP", "vecR"], "vec")
        self.dma("pool", [(self.wsm[:], self.wsm_d[l]), (self.wa2[:], self.wa2_d[l])], [], ["wsm", "wa2"], "wsmall")
        self.ACT(self.arow[:], self.vecR[:, R_ALOG:R_ALOG + 32], AF.Exp, ["vecR"], ["arow"])
        self.TS(self.arow[:], self.arow[:], -1.0, None, ALU.mult, None, ["arow"], ["arow"])
        self.TS(self.nba[:], self.vecP[:, P_BA:P_BA + 8], -1.0, None, ALU.mult, None, ["vecP"], ["nba"])
        self.rms(T, P_NMIX)
        self.mixer(T, l)
        self.barrier()
        self.rms(T, P_NMEM)
        self.attn(T, l)
        self.barrier()
        self.rms(T, P_NFFN)
        self.ffn(T, l)
        self.barrier()

    def fm(self, T, w, wk, rhs, rk, nk=16, nch=4, nw=NW):
        TT = T["TT"]
        for m in range(nch):
            ps, pk = self.ps()
            for kc in range(nk):
                self.MM(ps[:, 0:TT], w[:, kc * nw + m * 128: kc * nw + m * 128 + 128], rhs(kc), kc == 0, kc == nk - 1,
                        [wk] + rk, [pk])
            yield m, ps, pk

    def tm(self, T, w, wk):
        for si, (c0, L) in enumerate(T["segs"]):
            ps, pk = self.ps()
            for kc in range(16):
                self.MM(ps[0:L, :], self.h[:, kc, c0:c0 + L], w[:, kc * NW:(kc + 1) * NW], kc == 0, kc == 15, [wk, "h"], [pk])
            yield si, c0, L, ps, pk

    def conv(self, T, ps, pk, width, wcol0, wstride, bcol, oc, hist, hkey, newhist, nkey, raw, rawk, acc, acck):
        nrun, RL, TT = T["nrun"], T["RL"], T["TT"]
        H = width - 1
        rv = raw[:, 0:nrun * (H + RL)].rearrange("p (r t) -> p r t", r=nrun)
        av = acc[:, 0:TT].rearrange("p (r t) -> p r t", r=nrun)
        self.CP(rv[:, :, H:H + RL], ps[:, 0:TT].rearrange("p (r t) -> p r t", r=nrun), [pk], [rawk], en="act")
        self.CP(rv[:, :, 0:H], hist, [hkey], [rawk], en="dve")
        wc = lambda tap: self.vecP[:, wcol0 + tap * wstride + oc: wcol0 + tap * wstride + oc + 1]
        self.TS(av, rv[:, :, H:H + RL], wc(width - 1), self.vecP[:, bcol + oc:bcol + oc + 1], ALU.mult, ALU.add,
                [rawk, "vecP"], [acck])
        for tap in range(width - 1):
            self.STT(av, rv[:, :, tap:tap + RL], wc(tap), av, ALU.mult, ALU.add, [rawk, "vecP", acck], [acck])
        self.CP(newhist, rv[:, :, RL:RL + H], [rawk], [nkey], en="dve")
        return av

    def mixer(self, T, l):
        nc = self.nc
        TT, segs, nrun, RL = T["TT"], T["segs"], T["nrun"], T["RL"]
        nseg = len(segs)
        prompt = T["kind"] == "p"
        A, Bn = self.A, self.Bn
        ysnT = A[:, 0:8192].rearrange("p (c t) -> p c t", c=16)
        ygnT = A[:, 8192:16384].rearrange("p (c t) -> p c t", c=16)
        mg = A[:, 16384:24576].rearrange("p (c t) -> p c t", c=16)
        pools = [[Bn, 0, 12288], [A, 16384, 24576], [A, 8192, 16384]]

        def carve(n):
            for pl in pools:
                if pl[1] + n <= pl[2]:
                    a = pl[0][:, pl[1]:pl[1] + n]
                    pl[1] += n
                    return a
            raise AssertionError("arena overflow")
        xsT = carve(4 * TT).rearrange("p (c t) -> p c t", c=4)
        bcT = carve(4 * TT).rearrange("p (c t) -> p c t", c=4)
        sz = carve(nseg * 512).rearrange("p (s n) -> p s n", s=nseg)
        raw = [carve(1040).bitcast(F32) for _ in range(2)]
        cacc = [carve(2 * TT).bitcast(F32) for _ in range(2)]
        sets = []
        for b_ in range(2):
            sets.append((carve(512), carve(128),
                         carve(2048).bitcast(F32).rearrange("p (r i) -> p r i", r=8),
                         carve(1024).rearrange("p (r i) -> p r i", r=8),
                         carve(256).bitcast(F32),
                         carve(1024).bitcast(F32), carve(1024).bitcast(F32), carve(512)))
        t1 = carve(1024).bitcast(F32)
        yb = carve(1024).bitcast(F32)
        Sg = [carve(1024).bitcast(F32) for _ in range(1 if prompt else NSAMP)]
        Sgb1 = carve(512)
        ytok = carve(512)
        vR, vP = self.vecR, self.vecP
        sv = self.ssdv

        if not prompt:
            self.dma("sp", [(self.hs[:, :, 0:72], self.s_sconv_d[l].rearrange("s p n -> p s n"))], [], ["hs"], "hs")

        wsm = self.wsm
        ps, pk = self.ps()
        for kc in range(16):
            self.MM(ps[0:16, 0:TT], wsm[:, kc * 48 + 32: kc * 48 + 48], self.h[:, kc, 0:TT], kc == 0, kc == 15, ["wsm", "h"], [pk])
        self.CP(self.alrT[:, 0:TT], ps[0:16, 0:TT], [pk], ["alrT"], en="act")
        for si, (c0, L) in enumerate(segs):
            ps, pk = self.ps()
            for kc in range(16):
                self.MM(ps[0:L, 0:32], self.h[:, kc, c0:c0 + L], wsm[:, kc * 48: kc * 48 + 32], kc == 0, kc == 15, ["wsm", "h"], [pk])
            dt, dA, acs, ea, te, cd = [sv[:, si, q, :] for q in range(6)]
            k = "sv%d" % si
            self.TT(dt[0:L], ps[0:L, 0:32], vR[0:L, R_DTB:R_DTB + 32], ALU.add, [pk, "vecR"], [k])
            self.ACT(dt[0:L], dt[0:L], AF.Exp, [k], [k])
            self.ACT(dt[0:L], dt[0:L], AF.Ln, [k], [k], bias=1.0)
            self.TT(dA[0:L], dt[0:L], self.arow[0:L], ALU.mult, [k, "arow"], [k])
            ps2, pk2 = self.ps()
            self.MM(ps2[0:L, 0:32], self.U_f[0:L, 0:L], dA[0:L], True, True, [k, "cst"], [pk2])
            self.MM(ps2[:, 32:64], self.ones_f[0:L, :], dA[0:L], True, True, [k, "cst"], [pk2])
            self.CP(acs[0:L], ps2[0:L, 0:32], [pk2], [k], en="act")
            self.ACT(ea[0:L], ps2[0:L, 0:32], AF.Exp, [pk2], [k])
            self.ACT(cd, ps2[:, 32:64], AF.Exp, [pk2], [k])
            self.TT(te[0:L], ps2[0:L, 32:64], acs[0:L], ALU.subtract, [pk2, k], [k])
            self.ACT(te[0:L], te[0:L], AF.Exp, [k], [k])
            self.TT(te[0:L], te[0:L], dt[0:L], ALU.mult, [k], [k])

        xs_ch, bc_ch = xbc_slab_chunks()
        sidx = [0]

        def nslab():
            w, wk = self.slab("L%d_%d" % (l, sidx[0]))
            sidx[0] += 1
            return w, wk
        hfn = lambda kc: self.h[:, kc, 0:TT]
        hist_s = (lambda oc: self.cv_s[:, l, oc * 3:oc * 3 + 3].unsqueeze(1)) if prompt else \
                 (lambda oc: self.hs[:, :, oc * 3:oc * 3 + 3])
        nh_s = (lambda oc: self.cv_s[:, l, oc * 3:oc * 3 + 3].unsqueeze(1)) if prompt else \
               (lambda oc: self.ho[:, :, oc * 3:oc * 3 + 3])
        hk, nk_ = ("cv_s", "cv_s") if prompt else ("hs", "ho")
        cnt = [0]

        def xbc_slab(chs, dst, dkey):
            w, wk = nslab()
            for m, ps, pk in self.fm(T, w, wk, hfn, ["h"]):
                oc = chs[m]
                b = cnt[0] % 2
                cnt[0] += 1
                av = self.conv(T, ps, pk, 4, P_SCW, 24, P_SCB, oc, hist_s(oc), hk, nh_s(oc), nk_,
                               raw[b], "raw%d" % b, cacc[b], "cacc%d" % b)
                self.ACT(dst[:, m, 0:TT].rearrange("p (r t) -> p r t", r=nrun), av, AF.Silu, ["cacc%d" % b], [dkey])

        for g in range(4):
            xbc_slab(xs_ch[g], xsT, "xsT")
            if g % 2 == 0:
                xbc_slab(bc_ch[g // 2], bcT, "bcT")
            BT = bcT[:, 2 * (g % 2), :]
            CT = bcT[:, 2 * (g % 2) + 1, :]
            w, wk = nslab()
            for si, c0, L, ps, pk in self.tm(T, w, wk):
                self.ACT(sz[0:L, si, :], ps[0:L, :], AF.Silu, [pk], ["sz"])
            if prompt:
                self.dma("sp", [(Sg[0], self.run_ssd[:, g * 512:(g + 1) * 512])], ["run_ssd"], ["Sg0"], "Sg0")
                self.TS(Sg[0], Sg[0], self.flg[:, 1 + T["step"]:2 + T["step"]], None, ALU.mult, None, ["Sg0", "flg"], ["Sg0"])
            else:
                for s in range(NSAMP):
                    self.dma("sp", [(Sg[s], self.s_ssd_d[l, s, :, g * 512:(g + 1) * 512])], [], ["Sg%d" % s], "Sg%d" % s)
            h0 = 8 * g

            def SI(si, c0, L, b):
                dt, dA, acs, ea, te, cd = [sv[:, si, q, :] for q in range(6)]
                k = "sv%d" % si
                sfx = "_%d" % b
                xs_tok, B_tok, rhsA, Wt, cbm, yd, t2, xw = sets[b]
                Eg = rhsA
                ps, pk = self.ps()
                for m in range(4):
                    self.MM(ps[0:L, m * 128:(m + 1) * 128], xsT[:, m, c0:c0 + L], self.identb[:], True, True, ["xsT", "identb"], [pk])
                self.CP(xs_tok[0:L, :], ps[0:L, :], [pk], ["xs_tok" + sfx], en="act")
                yield
                ps, pk = self.ps()
                self.MM(ps[0:L, 0:128], BT[:, c0:c0 + L], self.identb[:], True, True, ["bcT", "identb"], [pk])
                self.CP(B_tok[0:L, :], ps[0:L, 0:128], [pk], ["B_tok" + sfx], en="act")
                yield
                ps, pk = self.ps()
                self.MM(ps[0:L, 0:L], BT[:, c0:c0 + L], CT[:, c0:c0 + L], True, True, ["bcT"], [pk])
                self.TT(cbm[0:L, 0:L], ps[0:L, 0:L], self.U_f[0:L, 0:L], ALU.mult, [pk, "cst"], ["cbm" + sfx])
                yield
                self.TT(rhsA[0:L, :, 0:L], dA[0:L, h0:h0 + 8].unsqueeze(2).to_broadcast([L, 8, L]),
                        self.U_f[0:L, 0:L].unsqueeze(1).to_broadcast([L, 8, L]), ALU.mult, [k, "cst"], ["rhsA0" + sfx, "rhsA1" + sfx])
                yield
                for half in range(2):
                    ps, pk = self.ps()
                    for r in range(4):
                        self.MM(ps[0:L, r * L:(r + 1) * L], self.SL_f[0:L, 0:L], rhsA[0:L, half * 4 + r, 0:L], True, True,
                                ["rhsA%d" % half + sfx, "cst"], [pk])
                    self.ACT(Eg[0:L, half * 4:half * 4 + 4, 0:L], ps[0:L, 0:4 * L].rearrange("p (r i) -> p r i", r=4), AF.Exp,
                             [pk], ["rhsA%d" % half + sfx])
                    yield
                self.TT(Eg[0:L, :, 0:L], Eg[0:L, :, 0:L], cbm[0:L, 0:L].unsqueeze(1).to_broadcast([L, 8, L]), ALU.mult,
                        ["rhsA0" + sfx, "rhsA1" + sfx, "cbm" + sfx], ["rhsA0" + sfx, "rhsA1" + sfx])
                yield
                self.TT(Wt[0:L, :, 0:L], Eg[0:L, :, 0:L], dt[0:L, h0:h0 + 8].unsqueeze(2).to_broadcast([L, 8, L]), ALU.mult,
                        ["rhsA0" + sfx, "rhsA1" + sfx, k], ["Wt" + sfx])
                yield
                psA, pkA = self.ps()
                for r in range(8):
                    self.MM(psA[0:L, r * 64:(r + 1) * 64], Wt[0:L, r, 0:L], xs_tok[0:L, r * 64:(r + 1) * 64], True, True,
                            ["Wt" + sfx, "xs_tok" + sfx], [pkA])
                self.CP(yd[0:L, :], psA[0:L, :], [pkA], ["yd" + sfx], en="act")
                yield
                self.TT(t2[0:L, :].rearrange("p (r q) -> p r q", r=8), xs_tok[0:L, :].rearrange("p (r q) -> p r q", r=8),
                        vR[0:L, R_DSK + h0:R_DSK + h0 + 8].unsqueeze(2).to_broadcast([L, 8, 64]), ALU.mult,
                        ["xs_tok" + sfx, "vecR"], ["t2" + sfx])
                yield
                self.TT(xw[0:L, :].rearrange("p (r q) -> p r q", r=8), xs_tok[0:L, :].rearrange("p (r q) -> p r q", r=8),
                        te[0:L, h0:h0 + 8].unsqueeze(2).to_broadcast([L, 8, 64]), ALU.mult, ["xs_tok" + sfx, k], ["xw" + sfx])
                yield

            def SD(si, c0, L, b):
                sq_ = 0 if prompt else si
                S, Sb, Sk = Sg[sq_], Sgb1, "Sg%d" % sq_
                dt, dA, acs, ea, te, cd = [sv[:, si, q, :] for q in range(6)]
                k = "sv%d" % si
                sfx = "_%d" % b
                xs_tok, B_tok, rhsA, Wt, cbm, yd, t2, xw = sets[b]
                self.CP(Sb, S, [Sk], ["Sgbb"], en="act")
                yield
                psB, pkB = self.ps()
                self.MM(psB[0:L, :], CT[:, c0:c0 + L], Sb, True, True, ["bcT", "Sgbb"], [pkB])
                self.TT(t1[0:L, :].rearrange("p (r q) -> p r q", r=8), psB[0:L, :].rearrange("p (r q) -> p r q", r=8),
                        ea[0:L, h0:h0 + 8].unsqueeze(2).to_broadcast([L, 8, 64]), ALU.mult, [pkB, k], ["t1"])
                yield
                self.TT(yb[0:L, :], yd[0:L, :], t1[0:L, :], ALU.add, ["yd" + sfx, "t1"], ["yb"])
                yield
                self.TT(yb[0:L, :], yb[0:L, :], t2[0:L, :], ALU.add, ["yb", "t2" + sfx], ["yb"])
                yield
                psC, pkC = self.ps()
                self.MM(psC[:, :], B_tok[0:L, :], xw[0:L, :], True, True, ["B_tok" + sfx, "xw" + sfx], [pkC])
                self.TT(S.rearrange("p (r q) -> p r q", r=8), S.rearrange("p (r q) -> p r q", r=8),
                        cd[:, h0:h0 + 8].unsqueeze(2).to_broadcast([128, 8, 64]), ALU.mult, [Sk, k], [Sk])
                yield
                self.TT(S, S, psC[:, :], ALU.add, [Sk, pkC], [Sk])
                yield
                self.TT(yb[0:L, :], yb[0:L, :], sz[0:L, si, :], ALU.mult, ["yb", "sz"], ["yb"])
                yield
                ss = self.small[:, 0:1]
                self.ACT(t1[0:L, :], yb[0:L, :], AF.Square, ["yb", "t1"], ["t1", "ss"], accum=ss[0:L])
                yield
                self.TS(ss[0:L], ss[0:L], 1.0 / 512, EPS, ALU.mult, ALU.add, ["ss"], ["ss"])
                yield
                self.RSQ(ss[0:L], "ss")
                yield
                self.STT(ytok[0:L, :], yb[0:L, :], ss[0:L], vR[0:L, R_SNORM + g * 512:R_SNORM + (g + 1) * 512], ALU.mult, ALU.mult,
                         ["yb", "ss", "vecR"], ["ytok"])
                yield
                ps, pk = self.ps()
                for m in range(4):
                    self.MM(ps[:, m * L:(m + 1) * L], ytok[0:L, m * 128:(m + 1) * 128], self.identb[0:L, 0:L], True, True,
                            ["ytok", "identb"], [pk])
                self.CP(ysnT[:, 4 * g:4 * g + 4, c0:c0 + L], ps[:, 0:4 * L].rearrange("p (m i) -> p m i", m=4), [pk], ["ysnT"], en="act")
                yield

            def interleave(gs):
                gs = [g_ for g_ in gs if g_ is not None]
                while gs:
                    for g_ in list(gs):
                        try:
                            next(g_)
                        except StopIteration:
                            gs.remove(g_)
            interleave([SI(0, segs[0][0], segs[0][1], 0)])
            for si, (c0, L) in enumerate(segs):
                nxt = SI(si + 1, segs[si + 1][0], segs[si + 1][1], (si + 1) % 2) if si + 1 < nseg else None
                interleave([SD(si, c0, L, si % 2), nxt])
            if prompt:
                prs = [(self.run_ssd[:, g * 512:(g + 1) * 512], Sg[0])]
                if T["step"] >= 5:
                    prs.append((self.o_ssd[T["step"] - 5, :, g * 512:(g + 1) * 512], Sg[0]))
                self.dma("sp", prs, ["Sg0"], ["run_ssd"], "run_ssd")
            else:
                for s in range(NSAMP):
                    self.dma("sp", [(self.o_ssd[2 + NSAMP * T["step"] + s, :, g * 512:(g + 1) * 512], Sg[s])], ["Sg%d" % s], [], "o_ssds")
        if prompt:
            if T["step"] >= 5:
                self.dma("sp", [(self.o_sconv[T["step"] - 5], self.cv_s[:, l, :])], ["cv_s"], [], "o_cv")
        else:
            self.dma("sp", [(self.o_sconv[2 + NSAMP * T["step"]:2 + NSAMP * T["step"] + NSAMP].rearrange("s p n -> p s n"), self.ho[:, :, 0:72])], ["ho"], [], "o_cv")
        self.barrier()
        self.gla(T, l, nslab, ygnT)
        self.barrier()
        self.merge(T, l, nslab, ysnT, ygnT, mg)

    def gla(self, T, l, nslab, ygnT):
        TT, segs = T["TT"], T["segs"]
        prompt = T["kind"] == "p"
        nseg = len(segs)
        Bn = self.Bn
        A = self.A
        pools = [[Bn, 0, 12288], [A, 16384, 24576]]

        def carve(n):
            for pl in pools:
                if pl[1] + n <= pl[2]:
                    a = pl[0][:, pl[1]:pl[1] + n]
                    pl[1] += n
                    return a
            raise AssertionError("arena overflow")
        cum = carve(4 * TT).bitcast(F32).rearrange("p (c t) -> p c t", c=2)
        ex = [carve(2 * TT).bitcast(F32) for _ in range(2)]
        qd = carve(2 * TT).rearrange("p (c t) -> p c t", c=2)
        ki = carve(2 * TT).rearrange("p (c t) -> p c t", c=2)
        ke = carve(2 * TT).rearrange("p (c t) -> p c t", c=2)
        vt = carve(nseg * 512).rearrange("p (s n) -> p s n", s=nseg)
        sg = carve(nseg * 512).rearrange("p (s n) -> p s n", s=nseg)
        ketok = carve(256)
        att = carve(128)
        ob = carve(1024).bitcast(F32)
        osq = carve(1024).bitcast(F32)
        otok = carve(512)
        ns = 1 if prompt else NSAMP
        S = [carve(2048).bitcast(F32).rearrange("p (c v) -> p c v", c=2) for _ in range(ns)]
        Sb1 = carve(1024).rearrange("p (c v) -> p c v", c=2)
        dec = carve(64).bitcast(F32)
        zeros = carve(256).bitcast(F32)
        vR = self.vecR
        self.MS(zeros, 0.0, ["zeros"])
        for hd in range(4):
            for cc in range(2):
                ch = 2 * hd + cc
                ps, pk = self.ps()
                self.MM(ps[:, 0:TT], self.wa2[:, ch * 128:(ch + 1) * 128], self.alrT[:, 0:TT], True, True, ["wa2", "alrT"], [pk])
                e = ex[cc][:, 0:TT]
                self.ACT(e, ps[:, 0:TT], AF.Exp, [pk, "nba"], ["ex%d" % cc], bias=self.nba[:, ch:ch + 1], scale=-1.0)
                self.ACT(e, e, AF.Ln, ["ex%d" % cc], ["ex%d" % cc], bias=1.0)
                for si, (c0, L) in enumerate(segs):
                    self._pre("dve", ["ex%d" % cc, "zeros"], ["cum"])
                    i = self.nc.vector.tensor_tensor_scan(out=cum[:, cc, c0:c0 + L], data0=e[:, c0:c0 + L], data1=zeros[:, 0:L],
                                                          initial=0.0, op0=ALU.add, op1=ALU.add)
                    self.op("dve", i, ["ex%d" % cc, "zeros"], ["cum"])
                    self.ACT(dec[:, cc * nseg + si: cc * nseg + si + 1], cum[:, cc, c0 + L - 1:c0 + L], AF.Exp, ["cum"], ["dec"],
                             scale=-1.0 / 16)
            w, wk = nslab()
            for m, ps, pk in self.fm(T, w, wk, lambda kc: self.h[:, kc, 0:TT], ["h"]):
                cc = m % 2
                e = ex[m % 2][:, 0:TT]
                if m < 2:
                    self.ACT(e, cum[:, cc, 0:TT], AF.Exp, ["cum"], ["ex%d" % (m % 2)], scale=-1.0 / 16)
                    self.STT(qd[:, cc, 0:TT], ps[:, 0:TT], 1.0 / 16, e, ALU.mult, ALU.mult, [pk, "ex%d" % (m % 2)], ["qd"])
                else:
                    self.ACT(e, cum[:, cc, 0:TT], AF.Exp, ["cum"], ["ex%d" % (m % 2)], scale=1.0 / 16)
                    self.TT(ki[:, cc, 0:TT], ps[:, 0:TT], e, ALU.mult, [pk, "ex%d" % (m % 2)], ["ki"])
                    for si, (c0, L) in enumerate(segs):
                        self.TS(ke[:, cc, c0:c0 + L], ki[:, cc, c0:c0 + L], dec[:, cc * nseg + si: cc * nseg + si + 1], None,
                                ALU.mult, None, ["ki", "dec"], ["ke"])
            w, wk = nslab()
            for si, c0, L, ps, pk in self.tm(T, w, wk):
                self.CP(vt[0:L, si, :], ps[0:L, :], [pk], ["vt"], en="act")
            w, wk = nslab()
            for si, c0, L, ps, pk in self.tm(T, w, wk):
                self.ACT(sg[0:L, si, :], ps[0:L, :], AF.Silu, [pk], ["sg"])
            gsrc = lambda ap: ap.rearrange("(c p) v -> p c v", p=128)[:, 2 * hd:2 * hd + 2, :]
            if prompt:
                self.dma("sp", [(S[0], gsrc(self.run_gla))], ["run_gla"], ["S0"], "S0")
                self.TS(S[0], S[0], self.flg[:, 1 + T["step"]:2 + T["step"]], None, ALU.mult, None, ["S0", "flg"], ["S0"])
            else:
                for s in range(NSAMP):
                    self.dma("sp", [(S[s], gsrc(self.s_gla_d[l, s]))], [], ["S%d" % s], "S%d" % s)
            for si, (c0, L) in enumerate(segs):
                sq_ = 0 if prompt else si
                St, Sbt, Sk = S[sq_], Sb1, "S%d" % sq_
                self.CP(Sbt, St, [Sk], ["Sbb"], en="act")
                ps, pk = self.ps()
                for cc in range(2):
                    self.MM(ps[0:L, cc * 128:(cc + 1) * 128], ke[:, cc, c0:c0 + L], self.identb[:], True, True, ["ke", "identb"], [pk])
                self.CP(ketok[0:L, :], ps[0:L, 0:256], [pk], ["ketok"], en="act")
                ps, pk = self.ps()
                for cc in range(2):
                    self.MM(ps[0:L, 0:L], ki[:, cc, c0:c0 + L], qd[:, cc, c0:c0 + L], cc == 0, cc == 1, ["ki", "qd"], [pk])
                self.TT(att[0:L, 0:L], ps[0:L, 0:L], self.U_f[0:L, 0:L], ALU.mult, [pk, "cst"], ["att"])
                psO, pkO = self.ps()
                self.MM(psO[0:L, :], att[0:L, 0:L], vt[0:L, si, :], True, False, ["att", "vt"], [pkO])
                for cc in range(2):
                    self.MM(psO[0:L, :], qd[:, cc, c0:c0 + L], Sbt[:, cc, :], False, cc == 1, ["qd", "Sbb"], [pkO])
                for cc in range(2):
                    psS, pkS = self.ps()
                    self.MM(psS[:, :], ketok[0:L, cc * 128:(cc + 1) * 128], vt[0:L, si, :], True, True, ["ketok", "vt"], [pkS])
                    self.STT(St[:, cc, :], St[:, cc, :], dec[:, cc * nseg + si: cc * nseg + si + 1], psS[:, :], ALU.mult, ALU.add,
                             [Sk, "dec", pkS], [Sk])
                ss = self.small[:, 1:2]
                self.ACT(osq[0:L, :], psO[0:L, :], AF.Square, [pkO], ["osq", "ss2"], accum=ss[0:L])
                self.TS(ss[0:L], ss[0:L], 1.0 / 512, EPS, ALU.mult, ALU.add, ["ss2"], ["ss2"])
                self.RSQ(ss[0:L], "ss2")
                self.STT(ob[0:L, :], psO[0:L, :], ss[0:L], vR[0:L, R_GNORM:R_GNORM + 512], ALU.mult, ALU.mult,
                         [pkO, "ss2", "vecR"], ["ob"])
                self.TT(otok[0:L, :], ob[0:L, :], sg[0:L, si, :], ALU.mult, ["ob", "sg"], ["otok"])
                ps, pk = self.ps()
                for m in range(4):
                    self.MM(ps[:, m * L:(m + 1) * L], otok[0:L, m * 128:(m + 1) * 128], self.identb[0:L, 0:L], True, True,
                            ["otok", "identb"], [pk])
                self.CP(ygnT[:, 4 * hd:4 * hd + 4, c0:c0 + L], ps[:, 0:4 * L].rearrange("p (m i) -> p m i", m=4), [pk], ["ygnT"], en="act")
            gdst = lambda ap: ap.rearrange("(c p) v -> p c v", p=128)[:, 2 * hd:2 * hd + 2, :]
            if prompt:
                prs = [(gdst(self.run_gla), S[0])]
                if T["step"] >= 5:
                    prs.append((gdst(self.o_gla[T["step"] - 5]), S[0]))
                self.dma("sp", prs, ["S0"], ["run_gla"], "run_gla")
            else:
                for s in range(NSAMP):
                    self.dma("sp", [(gdst(self.o_gla[2 + NSAMP * T["step"] + s]), S[s])], ["S%d" % s], [], "o_glas")

    def merge(self, T, l, nslab, ysnT, ygnT, mg):
        TT = T["TT"]
        Bn = self.Bn
        sgt = Bn[:, 0:2048].rearrange("p (c t) -> p c t", c=4)
        acc = Bn[:, 2048:2048 + 4096].bitcast(F32).rearrange("p (c t) -> p c t", c=4)
        hfn = lambda kc: self.h[:, kc, 0:TT]
        for j in range(4):
            for n, src, sk in ((0, ysnT, "ysnT"), (1, ygnT, "ygnT")):
                w, wk = nslab()
                for m, ps, pk in self.fm(T, w, wk, hfn, ["h"]):
                    self.ACT(sgt[:, m, 0:TT], ps[:, 0:TT], AF.Sigmoid, [pk], ["sgt%d" % m])
                w, wk = nslab()
                for m, ps, pk in self.fm(T, w, wk, lambda kc: src[:, kc, 0:TT], [sk]):
                    if n == 0:
                        self.TT(acc[:, m, 0:TT], ps[:, 0:TT], sgt[:, m, 0:TT], ALU.mult, [pk, "sgt%d" % m], ["acc%d" % m])
                    else:
                        self.TT(self.rs[:, 0:TT], ps[:, 0:TT], sgt[:, m, 0:TT], ALU.mult, [pk, "sgt%d" % m], ["rs"])
                        self.TT(mg[:, 4 * j + m, 0:TT], self.rs[:, 0:TT], acc[:, m, 0:TT], ALU.add, ["rs", "acc%d" % m], ["mg"])
        for j in range(4):
            w, wk = nslab()
            for m, ps, pk in self.fm(T, w, wk, lambda kc: mg[:, kc, 0:TT], ["mg"]):
                c = 4 * j + m
                self.TT(self.xv(T)[:, c, 0:TT], self.xv(T)[:, c, 0:TT], ps[:, 0:TT], ALU.add, [pk, "x%d" % c], ["x%d" % c])
        self.sidx_after_mixer = None

    def attn(self, T, l):
        TT, segs = T["TT"], T["segs"]
        prompt = T["kind"] == "p"
        A, Bn = self.A, self.Bn
        qT = A[:, 0:8192].rearrange("p (c t) -> p c t", c=16)
        oT = A[:, 8192:16384].rearrange("p (c t) -> p c t", c=16)
        ngrp = 1 if prompt else NSAMP
        kvreg = [Bn[:, 0:8192], A[:, 16384:24576]]
        kTb = [kvreg[g % 2][:, 0:4096].rearrange("p (c m) -> p c m", c=16) for g in range(ngrp)]
        vb = [kvreg[g % 2][:, 4096:8192].rearrange("p (c n) -> p c n", c=2) for g in range(ngrp)]
        base = 8192
        sc = Bn[:, base:base + 512].bitcast(F32)
        pb = Bn[:, base + 512:base + 768]
        pT = Bn[:, base + 768:base + 768 + 1024].rearrange("p (c t) -> p c t", c=2)
        sm = self.small

        def load_kv(g):
            if prompt:
                ksrc, vsrc, rk = self.o_kT[l], self.o_v[l], ["o_kT%d" % l, "o_v%d" % l]
            else:
                ksrc, vsrc, rk = self.s_kT_d[l, g], self.s_v_d[l, g], []
            self.dma("pool", [(kTb[g], ksrc.rearrange("(c p) m -> p c m", p=128)),
                              (vb[g], vsrc.rearrange("(c p) n -> p c n", p=128))], rk, ["kv%d" % (g % 2), "sq0", "sq1"], "kvl%d" % (g % 2))
        load_kv(0)
        sidx = [42]

        def nslab():
            w, wk = self.slab("L%d_%d" % (l, sidx[0]))
            sidx[0] += 1
            return w, wk
        for j in range(4):
            w, wk = nslab()
            for m, ps, pk in self.fm(T, w, wk, lambda kc: self.h[:, kc, 0:TT], ["h"]):
                self.ACT(qT[:, 4 * j + m, 0:TT], ps[:, 0:TT], AF.Copy, [pk], ["qT"], scale=float(512 ** -0.5))
        groups = [(0, segs)] if prompt else [(s, [segs[s]]) for s in range(NSAMP)]
        for g, gsegs in groups:
            if g + 1 < ngrp:
                load_kv(g + 1)
            g0 = gsegs[0][0]
            gl = sum(L for _, L in gsegs)
            for hd in range(4):
                for (c0, L) in gsegs:
                    ps, pk = self.ps()
                    for cc in range(4):
                        self.MM(ps[0:L, 0:256], qT[:, 4 * hd + cc, c0:c0 + L], kTb[g][:, 4 * hd + cc, :], cc == 0, cc == 3,
                                ["qT", "kv%d" % (g % 2)], [pk])
                    mx = sm[:, 2:3]
                    self._pre("dve", [pk], ["mx"])
                    i = self.nc.vector.reduce_max(out=mx[0:L], in_=ps[0:L, 0:256], axis=AX.X)
                    self.op("dve", i, [pk], ["mx"])
                    self.TS(mx[0:L], mx[0:L], -1.0, None, ALU.mult, None, ["mx"], ["mx"])
                    sme = sm[:, 3:4]
                    self.ACT(sc[0:L, :], ps[0:L, 0:256], AF.Exp, [pk, "mx"], ["sc", "sme"], bias=mx[0:L], accum=sme[0:L])
                    self._pre("dve", ["sme"], ["sme"])
                    i = self.nc.vector.reciprocal(out=sme[0:L], in_=sme[0:L])
                    self.op("dve", i, ["sme"], ["sme"])
                    self.TS(pb[0:L, :], sc[0:L, :], sme[0:L], None, ALU.mult, None, ["sc", "sme"], ["pb"])
                    ps2, pk2 = self.ps()
                    for mc in range(2):
                        self.MM(ps2[:, mc * L:(mc + 1) * L], pb[0:L, mc * 128:(mc + 1) * 128], self.identb[0:L, 0:L], True, True,
                                ["pb", "identb"], [pk2])
                    self.CP(pT[:, :, c0 - g0:c0 - g0 + L], ps2[:, 0:2 * L].rearrange("p (c i) -> p c i", c=2), [pk2], ["pT"], en="act")
                for cc in range(4):
                    ps, pk = self.ps()
                    for mc in range(2):
                        self.MM(ps[:, 0:gl], vb[g][:, mc, (4 * hd + cc) * 128:(4 * hd + cc + 1) * 128], pT[:, mc, 0:gl], mc == 0, mc == 1,
                                ["kv%d" % (g % 2), "pT"], [pk])
                    self.CP(oT[:, 4 * hd + cc, g0:g0 + gl], ps[:, 0:gl], [pk], ["oT"], en="act")
        for j in range(4):
            w, wk = nslab()
            for m, ps, pk in self.fm(T, w, wk, lambda kc: oT[:, kc, 0:TT], ["oT"]):
                c = 4 * j + m
                self.TT(self.xv(T)[:, c, 0:TT], self.xv(T)[:, c, 0:TT], ps[:, 0:TT], ALU.add, [pk, "x%d" % c], ["x%d" % c])

    def ffn(self, T, l):
        TT, nrun, RL = T["TT"], T["nrun"], T["RL"]
        prompt = T["kind"] == "p"
        A, Bn = self.A, self.Bn
        act = A[:, 0:24 * 512].rearrange("p (c t) -> p c t", c=24)
        raw = [Bn[:, i * 1040:(i + 1) * 1040].bitcast(F32) for i in range(2)]
        cacc = [Bn[:, 2080 + i * 1024:2080 + (i + 1) * 1024].bitcast(F32) for i in range(2)]
        cu = Bn[:, 4128:4128 + 4096].bitcast(F32).rearrange("p (c t) -> p c t", c=4)
        if not prompt:
            self.dma("sp", [(self.hs[:, :, :], self.s_fconv_d[l].rearrange("s p n -> p s n"))], [], ["hs"], "hs")
        hist = (lambda oc: self.cv_f[:, l, oc * 2:oc * 2 + 2].unsqueeze(1)) if prompt else (lambda oc: self.hs[:, :, oc * 2:oc * 2 + 2])
        nh = (lambda oc: self.cv_f[:, l, oc * 2:oc * 2 + 2].unsqueeze(1)) if prompt else (lambda oc: self.ho[:, :, oc * 2:oc * 2 + 2])
        hk, nk_ = ("cv_f", "cv_f") if prompt else ("hs", "ho")
        sidx = [50]
        cnt = 0
        hfn = lambda kc: self.h[:, kc, 0:TT]
        for hf, (j0, j1) in enumerate(((0, 6), (6, 11))):
            nkc = 4 * (j1 - j0)
            for j in range(j0, j1):
                for part in range(2):
                    w, wk = self.slab("L%d_%d" % (l, sidx[0]))
                    sidx[0] += 1
                    for m, ps, pk in self.fm(T, w, wk, hfn, ["h"]):
                        oc = part * 44 + 4 * j + m
                        b = cnt % 2
                        cnt += 1
                        av = self.conv(T, ps, pk, 3, P_FCW, 88, P_FCB, oc, hist(oc), hk, nh(oc), nk_,
                                       raw[b], "raw%d" % b, cacc[b], "cacc%d" % b)
                        cuv = cu[:, m, 0:TT].rearrange("p (r t) -> p r t", r=nrun)
                        if part == 0:
                            self.CP(cuv, av, ["cacc%d" % b], ["cu%d" % m], en="act")
                        else:
                            self.ACT(av, av, AF.Silu, ["cacc%d" % b], ["cacc%d" % b])
                            self.TT(act[:, 4 * (j - j0) + m, 0:TT].rearrange("p (r t) -> p r t", r=nrun), av, cuv, ALU.mult,
                                    ["cacc%d" % b, "cu%d" % m], ["act"])
            for j in range(8):
                w, wk = self.slab("F%d_%d_%d" % (l, hf, j))
                for m, ps, pk in self.fm(T, w, wk, lambda kc: act[:, kc, 0:TT], ["act"], nk=nkc, nch=2, nw=256):
                    c = 2 * j + m
                    self.TT(self.xv(T)[:, c, 0:TT], self.xv(T)[:, c, 0:TT], ps[:, 0:TT], ALU.add, [pk, "x%d" % c], ["x%d" % c])
                if hf == 1 and j % 4 == 3 and T["step"] < NSTEP - 1:
                    q = j // 4
                    send, gath, gk = (self.sendS, self.gathS, "gathS") if T["kind"] == "s" else (self.sendP, self.gathP, "gathP")
                    self.dma("sp", [(send[q].rearrange("p (c t) -> p c t", c=8), self.xv(T)[:, 8 * q:8 * q + 8, 0:TT])],
                             ["x%d" % c_ for c_ in range(8 * q, 8 * q + 8)], ["send%d" % q], "send%d" % q)
                    self.cc(send[q], gath[q], ["send%d" % q], [gk])
        if prompt:
            if T["step"] >= 5:
                self.dma("sp", [(self.o_fconv[T["step"] - 5], self.cv_f[:, l, :])], ["cv_f"], [], "o_cv")
        else:
            self.dma("sp", [(self.o_fconv[2 + NSAMP * T["step"]:2 + NSAMP * T["step"] + NSAMP].rearrange("s p n -> p s n"), self.ho[:, :, :])], ["ho"], [], "o_cv")


def _slab(wcols):
    n = wcols.shape[1]
    return np.ascontiguousarray(wcols.reshape(16, 128, n).transpose(1, 0, 2).reshape(128, 16 * n))


def _layer_slabs(w_in, w_branch, w_out, w_mq, w_mo, w_ffn_in):
    xs_ch, bc_ch = xbc_slab_chunks()
    xbc = w_in[:, O_XBC:O_XBC + 3072]
    sl = []
    pick = lambda chs: np.concatenate([xbc[:, c * 128:(c + 1) * 128] for c in chs], axis=1)
    for g in range(4):
        sl.append(pick(xs_ch[g]))
        if g % 2 == 0:
            sl.append(pick(bc_ch[g // 2]))
        sl.append(w_in[:, O_Z + g * 512:O_Z + (g + 1) * 512])
    for hd in range(4):
        sl.append(np.concatenate([w_in[:, O_Q + hd * 256:O_Q + (hd + 1) * 256], w_in[:, O_K + hd * 256:O_K + (hd + 1) * 256]], axis=1))
        sl.append(w_in[:, O_V + hd * 512:O_V + (hd + 1) * 512])
        sl.append(w_in[:, O_G + hd * 512:O_G + (hd + 1) * 512])
    for j in range(4):
        for n in range(2):
            sl.append(w_in[:, O_GATES + n * 2048 + j * 512:O_GATES + n * 2048 + (j + 1) * 512])
            sl.append(w_branch[n][:, j * 512:(j + 1) * 512])
    for j in range(4):
        sl.append(w_out[:, j * 512:(j + 1) * 512])
    for j in range(4):
        sl.append(w_mq[:, j * 512:(j + 1) * 512])
    for j in range(4):
        sl.append(w_mo[:, j * 512:(j + 1) * 512])
    for j in range(11):
        sl.append(w_ffn_in[:, j * 512:(j + 1) * 512])
        sl.append(w_ffn_in[:, DFF + j * 512:DFF + (j + 1) * 512])
    assert len(sl) == NSLAB
    return np.stack([_slab(s) for s in sl])


def _pvec(v):
    return v.reshape(-1, 128).T


_NC_CACHE = {}


def kernel(x_prompt, x_sample, mem_prompt, state_ssd, state_ssd_conv, state_gla, state_ffn_conv,
           cache_mem_k, cache_mem_v, norm_mix, w_in, ssd_conv_w, ssd_conv_b, ssd_dt_bias, ssd_a_log,
           ssd_d, ssd_norm, gla_wa2, gla_ba, gla_norm, w_branch, w_out, norm_mem, w_mq, w_mk, w_mv,
           w_mo, norm_ffn, w_ffn_in, ffn_conv_w, ffn_conv_b, w_ffn_out, norm_final, _steps=NSTEP, _pairs=4):
    f = lambda a: np.asarray(a, dtype=np.float32)
    (x_prompt, x_sample, mem_prompt, state_ssd, state_ssd_conv, state_gla, state_ffn_conv, cache_mem_k, cache_mem_v,
     norm_mix, w_in, ssd_conv_w, ssd_conv_b, ssd_dt_bias, ssd_a_log, ssd_d, ssd_norm, gla_wa2, gla_ba, gla_norm,
     w_branch, w_out, norm_mem, w_mq, w_mk, w_mv, w_mo, norm_ffn, w_ffn_in, ffn_conv_w, ffn_conv_b, w_ffn_out,
     norm_final) = map(f, (x_prompt, x_sample, mem_prompt, state_ssd, state_ssd_conv, state_gla, state_ffn_conv,
                           cache_mem_k, cache_mem_v, norm_mix, w_in, ssd_conv_w, ssd_conv_b, ssd_dt_bias, ssd_a_log,
                           ssd_d, ssd_norm, gla_wa2, gla_ba, gla_norm, w_branch, w_out, norm_mem, w_mq, w_mk, w_mv,
                           w_mo, norm_ffn, w_ffn_in, ffn_conv_w, ffn_conv_b, w_ffn_out, norm_final))
    rg = [[p, p + _pairs] for p in range(_pairs)]
    key = (_steps, _pairs)
    if key not in _NC_CACHE:
        _NC_CACHE[key] = Builder(_steps, rg).build()
    nc = _NC_CACHE[key]

    cst = np.zeros((128, 512), np.float32)
    idx = np.arange(128)
    cst[:, 0:128] = np.eye(128, dtype=np.float32)
    cst[:, 128:256] = (idx[:, None] <= idx[None, :])
    cst[:, 256:384] = (idx[:, None] > idx[None, :])
    cst[:, 384:512] = 1.0
    lay = []
    for l in range(DEPTH):
        sh = {"cst": cst}
        sh["W0"] = _layer_slabs(w_in[l], w_branch[l], w_out[l], w_mq[l], w_mo[l], w_ffn_in[l])
        wf = np.zeros((2, 8, 128, 24 * 256), np.float32)
        for hf, (k0, k1) in enumerate(((0, 24), (24, 44))):
            for j in range(8):
                blk = w_ffn_out[l][k0 * 128:k1 * 128, j * 256:(j + 1) * 256].reshape(k1 - k0, 128, 256).transpose(1, 0, 2)
                wf[hf, j, :, 0:(k1 - k0) * 256] = blk.reshape(128, (k1 - k0) * 256)
        sh["WF0"] = wf
        sh["WKV0"] = np.stack([_slab(w_mk[l][:, j * 512:(j + 1) * 512]) for j in range(4)] +
                              [_slab(w_mv[l][:, j * 512:(j + 1) * 512]) for j in range(4)])
        sh["wsm0"] = _slab(np.concatenate([w_in[l][:, O_DT:O_DT + 32], w_in[l][:, O_ALR:O_ALR + 16]], axis=1))
        sh["wa2_0"] = np.ascontiguousarray(gla_wa2[l])
        vp = np.zeros((128, NVP), np.float32)
        vp[:, P_NMIX:P_NMIX + 16] = _pvec(norm_mix[l])
        vp[:, P_NMEM:P_NMEM + 16] = _pvec(norm_mem[l])
        vp[:, P_NFFN:P_NFFN + 16] = _pvec(norm_ffn[l])
        vp[:, P_NFIN:P_NFIN + 16] = _pvec(norm_final)
        for tap in range(4):
            vp[:, P_SCW + tap * 24:P_SCW + (tap + 1) * 24] = _pvec(ssd_conv_w[l, tap])
        vp[:, P_SCB:P_SCB + 24] = _pvec(ssd_conv_b[l])
        for tap in range(3):
            vp[:, P_FCW + tap * 88:P_FCW + (tap + 1) * 88] = _pvec(ffn_conv_w[l, tap])
        vp[:, P_FCB:P_FCB + 88] = _pvec(ffn_conv_b[l])
        vp[:, P_BA:P_BA + 8] = _pvec(gla_ba[l])
        sh["vecP0"] = vp
        vr = np.concatenate([ssd_dt_bias[l], ssd_a_log[l], ssd_d[l], ssd_norm[l], gla_norm[l]])
        sh["vecR0"] = np.ascontiguousarray(np.broadcast_to(vr[None, :], (128, NVR)))
        lay.append(sh)

    maps = {}
    for l in range(DEPTH):
        for p in range(_pairs):
            ss = [4 * p + i for i in range(NSAMP)]
            m = dict(lay[l])
            xT = np.zeros((D, TOK), np.float32)
            if l == 0:
                for i, s_ in enumerate(ss):
                    xT[:, 16 * i:16 * (i + 1)] = x_sample[s_].T
                xT[:, 64:64 + SEQ] = x_prompt[p].T
            m["xT"] = xT
            m["memT"] = np.ascontiguousarray(mem_prompt[p].T)
            flg = np.ones((128, 8), np.float32)
            flg[:, 0] = float(l)
            if l == 1:
                flg[:, 1 + 3] = 0.0
            m["flg"] = flg
            m["s_ssd"] = np.ascontiguousarray(np.stack([state_ssd[l, s_].reshape(2048, 128).T for s_ in ss])[None])
            m["s_sconv"] = np.ascontiguousarray(np.stack([state_ssd_conv[l, s_].reshape(3, 24, 128).transpose(2, 1, 0).reshape(128, 72)
                                                          for s_ in ss])[None])
            m["s_fconv"] = np.ascontiguousarray(np.stack([state_ffn_conv[l, s_].reshape(2, 88, 128).transpose(2, 1, 0).reshape(128, 176)
                                                          for s_ in ss])[None])
            m["s_gla"] = np.ascontiguousarray(np.stack([state_gla[l, s_].reshape(1024, 512) for s_ in ss])[None])
            m["s_kT"] = np.ascontiguousarray(np.stack([cache_mem_k[l, s_].reshape(256, 2048).T for s_ in ss])[None])
            m["s_v"] = np.ascontiguousarray(np.stack([cache_mem_v[l, s_].reshape(256, 2048) for s_ in ss])[None])
            maps[(l, p)] = m
    in_maps = [maps[(0, p)] for p in range(_pairs)] + [maps[(1, p)] for p in range(_pairs)]
    res = run_bass_kernel_spmd(nc, in_maps, core_ids=list(range(2 * _pairs))).results
    RA = lambda p: res[p % _pairs]
    RB = lambda p: res[_pairs + p % _pairs]
    R = lambda l, p: RA(p) if l == 0 else RB(p)

    B, NSEQ = 4, 16
    y_prompt = np.stack([RB(p)["yT"][:, 640:640 + SEQ].T for p in range(B)])
    y_sample = np.stack([RB(s // 4)["yT"][:, 64 + 16 * (s % 4):64 + 16 * (s % 4 + 1)].T for s in range(NSEQ)])

    def un_ssd(a):
        return a.T.reshape(32, 64, 128)

    def un_conv(a, w, nch):
        return a.reshape(128, nch, w).transpose(2, 1, 0).reshape(w, nch * 128)
    fin = lambda l: l
    sl = lambda l, s: 2 + NSAMP * l + s % 4
    p_ssd = np.stack([[un_ssd(R(l, p)["o_ssd"][fin(l)]) for p in range(B)] for l in range(DEPTH)])
    s_ssd = np.stack([[un_ssd(R(l, s // 4)["o_ssd"][sl(l, s)]) for s in range(NSEQ)] for l in range(DEPTH)])
    p_sconv = np.stack([[un_conv(R(l, p)["o_sconv"][fin(l)], 3, 24) for p in range(B)] for l in range(DEPTH)])
    s_sconv = np.stack([[un_conv(R(l, s // 4)["o_sconv"][sl(l, s)], 3, 24) for s in range(NSEQ)] for l in range(DEPTH)])
    p_gla = np.stack([[R(l, p)["o_gla"][fin(l)].reshape(4, 256, 512) for p in range(B)] for l in range(DEPTH)])
    s_gla = np.stack([[R(l, s // 4)["o_gla"][sl(l, s)].reshape(4, 256, 512) for s in range(NSEQ)] for l in range(DEPTH)])
    p_fconv = np.stack([[un_conv(R(l, p)["o_fconv"][fin(l)], 2, 88) for p in range(B)] for l in range(DEPTH)])
    s_fconv = np.stack([[un_conv(R(l, s // 4)["o_fconv"][sl(l, s)], 2, 88) for s in range(NSEQ)] for l in range(DEPTH)])
    p_mem_k = np.stack([[R(l, p)["o_kT"][0].T.reshape(256, 4, 512) for p in range(B)] for l in range(DEPTH)])
    p_mem_v = np.stack([[R(l, p)["o_v"][0].reshape(256, 4, 512) for p in range(B)] for l in range(DEPTH)])
    outs = (y_prompt, y_sample, p_ssd, p_sconv, p_gla, p_fconv, p_mem_k, p_mem_v, s_ssd, s_sconv, s_gla, s_fconv)
    return tuple(np.ascontiguousarray(o, dtype=np.float32) for o in outs)
```
